# Optimizing a Trainium2 kernel written in Bass

```python
import math
import jax, jax.numpy as jnp
from jax import lax
import numpy as np

D_MODEL = 1024
BATCH = 8
SEQ = 2048
DEPTH = 2
DEC_BATCH = 128
DEC_SEQ = 4
PAST_LEN = 16384
PAGE_SIZE = 128

N_AB_LAYERS = (DEPTH + 1) // 2
N_C_LAYERS = DEPTH // 2
S5_WIDTH = D_MODEL // 2
S5_GROUP_CH = 16
S5_GROUPS = S5_WIDTH // S5_GROUP_CH
S5_STATE = 64
RET_HEADS = 4
RET_DK = (D_MODEL // 2) // RET_HEADS
RET_DV = RET_DK
RET_QK = RET_HEADS * RET_DK
RET_WIDTH = RET_HEADS * RET_DV
RET_CHUNK = 128
HG_EXPAND = 128
HG_HEADS = D_MODEL // HG_EXPAND
HG_DK = HG_EXPAND
HG_DV = D_MODEL // HG_HEADS
HG_QF = HG_HEADS * HG_DK
HG_WIDTH = HG_HEADS * HG_DV
HG_CHUNK = 16
D_FF = 2816
CONV_W = 3
AB_IN = S5_WIDTH + 2 * RET_QK + 2 * RET_WIDTH
C_IN = 2 * HG_QF + 2 * HG_WIDTH
NORM_EPS = 1e-6
ROPE_BASE = 10000.0

kernel_name = 'hybrid_s5_retention_hgrn2_convffn_step'

F32 = jnp.float32


def _rmsnorm(x, w):
    xf = x.astype(F32)
    y = xf * lax.rsqrt(jnp.mean(xf * xf, axis=-1, keepdims=True) + NORM_EPS)
    return (y * w.astype(F32)).astype(x.dtype)


def _chunk_len(length, c):
    return c if length % c == 0 else length


def _to_chunks(t, c):
    b, l = t.shape[0], t.shape[1]
    return jnp.moveaxis(t.reshape((b, l // c, c) + t.shape[2:]), 1, 0)


def _from_chunks(t):
    nc, b, c = t.shape[0], t.shape[1], t.shape[2]
    return jnp.moveaxis(t, 0, 1).reshape((b, nc * c) + t.shape[3:])


def _rotary(t, pos):
    half = t.shape[-1] // 2
    inv = 1.0 / (ROPE_BASE ** jnp.linspace(0.0, 1.0, half, dtype=F32))
    ang = pos.astype(F32)[..., None] * inv
    cos = jnp.cos(ang)[:, :, None, :]
    sin = jnp.sin(ang)[:, :, None, :]
    t1, t2 = t[..., :half], t[..., half:]
    return jnp.concatenate([t1 * cos - t2 * sin, t1 * sin + t2 * cos], axis=-1)


def _s5(u, s_re0, s_im0, lam_re, lam_im, log_dt, b_re, b_im, c_re, c_im, d_skip, glu_w, glu_b):
    bsz, l, _ = u.shape
    ug = u.reshape(bsz, l, S5_GROUPS, S5_GROUP_CH)
    lam_re = lam_re.astype(F32)
    lam_im = lam_im.astype(F32)
    dt = jnp.exp(log_dt.astype(F32))[:, None]
    mag = jnp.exp(lam_re * dt)
    ang = lam_im * dt
    ab_re, ab_im = mag * jnp.cos(ang), mag * jnp.sin(ang)
    nr, ni = ab_re - 1.0, ab_im
    den = lam_re * lam_re + lam_im * lam_im
    f_re = (nr * lam_re + ni * lam_im) / den
    f_im = (ni * lam_re - nr * lam_im) / den
    b_re = b_re.astype(F32)
    b_im = b_im.astype(F32)
    bb_re = f_re[..., None] * b_re - f_im[..., None] * b_im
    bb_im = f_re[..., None] * b_im + f_im[..., None] * b_re
    bu_re = jnp.einsum('blgh,gph->blgp', ug, bb_re)
    bu_im = jnp.einsum('blgh,gph->blgp', ug, bb_im)
    s_re0 = s_re0.astype(F32)
    s_im0 = s_im0.astype(F32)
    bu_re = bu_re.at[:, 0].add(ab_re * s_re0 - ab_im * s_im0)
    bu_im = bu_im.at[:, 0].add(ab_re * s_im0 + ab_im * s_re0)
    a_re = jnp.broadcast_to(ab_re, bu_re.shape)
    a_im = jnp.broadcast_to(ab_im, bu_im.shape)

    def combine(e1, e2):
        a1r, a1i, b1r, b1i = e1
        a2r, a2i, b2r, b2i = e2
        return (a1r * a2r - a1i * a2i, a1r * a2i + a1i * a2r,
                a2r * b1r - a2i * b1i + b2r, a2r * b1i + a2i * b1r + b2i)

    _, _, s_re, s_im = lax.associative_scan(combine, (a_re, a_im, bu_re, bu_im), axis=1)
    y = (jnp.einsum('blgp,ghp->blgh', s_re, c_re.astype(F32))
         - jnp.einsum('blgp,ghp->blgh', s_im, c_im.astype(F32))
         + d_skip.astype(F32) * ug)
    y = jax.nn.gelu(y.reshape(bsz, l, S5_WIDTH))
    out = y * jax.nn.sigmoid(y @ glu_w.astype(F32) + glu_b.astype(F32))
    return out, s_re[:, -1], s_im[:, -1]


def _retention(q, k, v, s0):
    l = q.shape[1]
    c = _chunk_len(l, RET_CHUNK)
    lg = jnp.log(1.0 - 2.0 ** (-5.0 - jnp.arange(RET_HEADS, dtype=F32)))
    idx = jnp.arange(c)
    diff = idx[:, None] - idx[None, :]
    decay = jnp.where(diff[None] >= 0, jnp.exp(jnp.maximum(diff, 0)[None].astype(F32) * lg[:, None, None]), 0.0)
    q_dec = jnp.exp((idx + 1).astype(F32)[:, None] * lg[None])[None, :, :, None]
    k_dec = jnp.exp((c - 1 - idx).astype(F32)[:, None] * lg[None])[None, :, :, None]
    chunk_dec = jnp.exp(c * lg)[None, :, None, None]

    def step(s, inp):
        qc, kc, vc = inp
        scores = jnp.einsum('bihd,bjhd->bhij', qc, kc) * decay
        o = jnp.einsum('bhij,bjhe->bihe', scores, vc) + jnp.einsum('bihd,bhde->bihe', qc * q_dec, s)
        s = chunk_dec * s + jnp.einsum('bjhd,bjhe->bhde', kc * k_dec, vc)
        return s, o

    s, o = lax.scan(step, s0.astype(F32), (_to_chunks(q, c), _to_chunks(k, c), _to_chunks(v, c)))
    return _from_chunks(o), s


def _ab_mixer(h, pos, s_re0, s_im0, ret0, w_in, lam_re, lam_im, log_dt, b_re, b_im, c_re, c_im,
              d_skip, glu_w, glu_b, w_out):
    bsz, l, _ = h.shape
    z = (h @ w_in).astype(F32)
    cuts = [S5_WIDTH, S5_WIDTH + RET_QK, S5_WIDTH + 2 * RET_QK, S5_WIDTH + 2 * RET_QK + RET_WIDTH]
    u, q, k, v, g = jnp.split(z, cuts, axis=-1)
    y_s5, s_re, s_im = _s5(u, s_re0, s_im0, lam_re, lam_im, log_dt, b_re, b_im, c_re, c_im,
                           d_skip, glu_w, glu_b)
    q = _rotary(q.reshape(bsz, l, RET_HEADS, RET_DK), pos)
    k = _rotary(k.reshape(bsz, l, RET_HEADS, RET_DK), pos) * (RET_DK ** -0.5)
    v = v.reshape(bsz, l, RET_HEADS, RET_DV)
    o, ret_new = _retention(q, k, v, ret0)
    o = o * lax.rsqrt(jnp.mean(o * o, axis=-1, keepdims=True) + NORM_EPS)
    y_ret = jax.nn.silu(g) * o.reshape(bsz, l, RET_WIDTH)
    y = jnp.concatenate([y_s5, y_ret], axis=-1).astype(h.dtype) @ w_out
    return y, s_re, s_im, ret_new


def _hgrn_recurrence(q, k, log_f, v, s0):
    l = q.shape[1]
    c = _chunk_len(l, HG_CHUNK)
    causal = jnp.tril(jnp.ones((c, c), dtype=bool))

    def step(s, inp):
        qc, kc, lfc, vc = inp
        b = jnp.cumsum(lfc, axis=1)
        q_t = qc * jnp.exp(b)
        k_t = kc * jnp.exp(-b)
        scores = jnp.where(causal, jnp.einsum('bihd,bjhd->bhij', q_t, k_t), 0.0)
        o = jnp.einsum('bhij,bjhe->bihe', scores, vc) + jnp.einsum('bihd,bhde->bihe', q_t, s)
        b_last = b[:, -1]
        s = jnp.exp(b_last)[..., None] * s + jnp.einsum('bjhd,bjhe->bhde', kc * jnp.exp(b_last[:, None] - b), vc)
        return s, o

    s, o = lax.scan(step, s0.astype(F32), (_to_chunks(q, c), _to_chunks(k, c), _to_chunks(log_f, c), _to_chunks(v, c)))
    return _from_chunks(o), s


def _hgrn_mixer(h, s0, lb, w_in, norm_w, w_out):
    bsz, l, _ = h.shape
    z = (h @ w_in).astype(F32)
    q, fl, i, g = jnp.split(z, [HG_QF, 2 * HG_QF, 2 * HG_QF + HG_WIDTH], axis=-1)
    q = q.reshape(bsz, l, HG_HEADS, HG_DK)
    fl = fl.reshape(bsz, l, HG_HEADS, HG_DK)
    i = i.reshape(bsz, l, HG_HEADS, HG_DV)
    lbh = lb.reshape(HG_HEADS, HG_DK)
    forget = lbh + (1.0 - lbh) * jax.nn.sigmoid(fl)
    o, s = _hgrn_recurrence(q, 1.0 - forget, jnp.log(forget), i, s0)
    o = o * lax.rsqrt(jnp.mean(o * o, axis=-1, keepdims=True) + NORM_EPS) * norm_w.astype(F32)
    o = o.reshape(bsz, l, HG_WIDTH) * jax.nn.silu(g)
    return o.astype(h.dtype) @ w_out, s


def _conv_ffn(h, buf, w_gate, w_up, conv_w, conv_b, w_down):
    l = h.shape[1]
    gpre = h @ w_gate
    ext = jnp.concatenate([buf.astype(gpre.dtype), gpre], axis=1)
    conv = conv_b + conv_w[0] * ext[:, 0:l]
    for j in range(1, CONV_W):
        conv = conv + conv_w[j] * ext[:, j:j + l]
    y = (jax.nn.silu(conv) * (h @ w_up)) @ w_down
    return y, ext[:, l:]


def _trunk(x, pos, s5_re, s5_im, ret, hg, conv, p):
    new_re, new_im, new_ret, new_hg, new_conv = [], [], [], [], []
    lb_soft = jax.nn.softmax(p['hg_lb_logits'].astype(F32), axis=0)
    lb_all = jnp.cumsum(lb_soft, axis=0) - lb_soft[0]
    for layer in range(DEPTH):
        j = layer // 2
        h = _rmsnorm(x, p['norm_mix'][layer])
        if layer % 2 == 0:
            y, r, im, st = _ab_mixer(h, pos, s5_re[j], s5_im[j], ret[j], p['w_in_ab'][j],
                                     p['s5_lam_re'][j], p['s5_lam_im'][j], p['s5_log_dt'][j],
                                     p['s5_b_re'][j], p['s5_b_im'][j], p['s5_c_re'][j], p['s5_c_im'][j],
                                     p['s5_d'][j], p['s5_glu_w'][j], p['s5_glu_b'][j], p['w_out_ab'][j])
            new_re.append(r)
            new_im.append(im)
            new_ret.append(st)
        else:
            y, st = _hgrn_mixer(h, hg[j], lb_all[layer], p['w_in_c'][j], p['hg_norm_w'][j], p['w_out_c'][j])
            new_hg.append(st)
        x = x + y.astype(x.dtype)
        h = _rmsnorm(x, p['norm_ffn'][layer])
        y, buf = _conv_ffn(h, conv[layer], p['ffn_w_gate'][layer], p['ffn_w_up'][layer],
                           p['ffn_conv_w'][layer], p['ffn_conv_b'][layer], p['ffn_w_down'][layer])
        new_conv.append(buf)
        x = x + y.astype(x.dtype)
    x = _rmsnorm(x, p['norm_final'])
    return x, jnp.stack(new_re), jnp.stack(new_im), jnp.stack(new_ret), jnp.stack(new_hg), jnp.stack(new_conv)


def setup_inputs(seed: int = 0) -> dict:
    key = jax.random.key(seed)
    ks = iter(jax.random.split(key, 48))

    def nrm(shape, scale):
        return scale * jax.random.normal(next(ks), shape, F32)

    x_prompt = nrm((BATCH, SEQ, D_MODEL), 1.0)
    x_sample = nrm((DEC_BATCH, DEC_SEQ, D_MODEL), 1.0)
    state_s5_re = nrm((N_AB_LAYERS, DEC_BATCH, S5_GROUPS, S5_STATE), 0.1)
    state_s5_im = nrm((N_AB_LAYERS, DEC_BATCH, S5_GROUPS, S5_STATE), 0.1)
    state_ret = nrm((N_AB_LAYERS, DEC_BATCH, RET_HEADS, RET_DK, RET_DV), 0.5)
    state_hgrn = nrm((N_C_LAYERS, DEC_BATCH, HG_HEADS, HG_DK, HG_DV), 0.5)
    state_ffn_conv = nrm((DEPTH, DEC_BATCH, CONV_W - 1, D_FF), 1.0)
    pos_sample = jnp.full((DEC_BATCH,), PAST_LEN, dtype=jnp.int32)
    norm_mix = 1.0 + nrm((DEPTH, D_MODEL), 0.01)
    norm_ffn = 1.0 + nrm((DEPTH, D_MODEL), 0.01)
    norm_final = 1.0 + nrm((D_MODEL,), 0.01)
    w_in_ab = nrm((N_AB_LAYERS, D_MODEL, AB_IN), D_MODEL ** -0.5)
    s5_lam_re = -0.5 + nrm((N_AB_LAYERS, S5_GROUPS, S5_STATE), 0.01)
    s5_lam_im = math.pi * jnp.arange(S5_STATE, dtype=F32) + nrm((N_AB_LAYERS, S5_GROUPS, S5_STATE), 0.01)
    s5_log_dt = jax.random.uniform(next(ks), (N_AB_LAYERS, S5_GROUPS), F32, math.log(1e-3), math.log(1e-1))
    s5_b_re = nrm((N_AB_LAYERS, S5_GROUPS, S5_STATE, S5_GROUP_CH), (2 * S5_GROUP_CH) ** -0.5)
    s5_b_im = nrm((N_AB_LAYERS, S5_GROUPS, S5_STATE, S5_GROUP_CH), (2 * S5_GROUP_CH) ** -0.5)
    s5_c_re = nrm((N_AB_LAYERS, S5_GROUPS, S5_GROUP_CH, S5_STATE), S5_STATE ** -0.5)
    s5_c_im = nrm((N_AB_LAYERS, S5_GROUPS, S5_GROUP_CH, S5_STATE), S5_STATE ** -0.5)
    s5_d = nrm((N_AB_LAYERS, S5_GROUPS, S5_GROUP_CH), 1.0)
    s5_glu_w = nrm((N_AB_LAYERS, S5_WIDTH, S5_WIDTH), S5_WIDTH ** -0.5)
    s5_glu_b = nrm((N_AB_LAYERS, S5_WIDTH), 0.01)
    w_out_ab = nrm((N_AB_LAYERS, S5_WIDTH + RET_WIDTH, D_MODEL), (S5_WIDTH + RET_WIDTH) ** -0.5)
    w_in_c = nrm((N_C_LAYERS, D_MODEL, C_IN), D_MODEL ** -0.5)
    hg_lb_logits = nrm((DEPTH, HG_QF), 0.1)
    hg_norm_w = 1.0 + nrm((N_C_LAYERS, HG_DV), 0.01)
    w_out_c = nrm((N_C_LAYERS, HG_WIDTH, D_MODEL), HG_WIDTH ** -0.5)
    ffn_w_gate = nrm((DEPTH, D_MODEL, D_FF), D_MODEL ** -0.5)
    ffn_w_up = nrm((DEPTH, D_MODEL, D_FF), D_MODEL ** -0.5)
    ffn_conv_w = nrm((DEPTH, CONV_W, D_FF), CONV_W ** -0.5)
    ffn_conv_b = nrm((DEPTH, D_FF), 0.01)
    ffn_w_down = nrm((DEPTH, D_FF, D_MODEL), D_FF ** -0.5)
    return {'x_prompt': x_prompt, 'x_sample': x_sample, 'state_s5_re': state_s5_re,
            'state_s5_im': state_s5_im, 'state_ret': state_ret, 'state_hgrn': state_hgrn,
            'state_ffn_conv': state_ffn_conv, 'pos_sample': pos_sample,
            'norm_mix': norm_mix, 'norm_ffn': norm_ffn, 'norm_final': norm_final,
            'w_in_ab': w_in_ab, 's5_lam_re': s5_lam_re, 's5_lam_im': s5_lam_im, 's5_log_dt': s5_log_dt,
            's5_b_re': s5_b_re, 's5_b_im': s5_b_im, 's5_c_re': s5_c_re, 's5_c_im': s5_c_im,
            's5_d': s5_d, 's5_glu_w': s5_glu_w, 's5_glu_b': s5_glu_b, 'w_out_ab': w_out_ab,
            'w_in_c': w_in_c, 'hg_lb_logits': hg_lb_logits, 'hg_norm_w': hg_norm_w, 'w_out_c': w_out_c,
            'ffn_w_gate': ffn_w_gate, 'ffn_w_up': ffn_w_up, 'ffn_conv_w': ffn_conv_w,
            'ffn_conv_b': ffn_conv_b, 'ffn_w_down': ffn_w_down}


def reference(x_prompt, x_sample, state_s5_re, state_s5_im, state_ret, state_hgrn, state_ffn_conv,
              pos_sample, norm_mix, norm_ffn, norm_final, w_in_ab, s5_lam_re, s5_lam_im, s5_log_dt,
              s5_b_re, s5_b_im, s5_c_re, s5_c_im, s5_d, s5_glu_w, s5_glu_b, w_out_ab, w_in_c,
              hg_lb_logits, hg_norm_w, w_out_c, ffn_w_gate, ffn_w_up, ffn_conv_w, ffn_conv_b, ffn_w_down):
    p = {'norm_mix': norm_mix, 'norm_ffn': norm_ffn, 'norm_final': norm_final, 'w_in_ab': w_in_ab,
         's5_lam_re': s5_lam_re, 's5_lam_im': s5_lam_im, 's5_log_dt': s5_log_dt, 's5_b_re': s5_b_re,
         's5_b_im': s5_b_im, 's5_c_re': s5_c_re, 's5_c_im': s5_c_im, 's5_d': s5_d, 's5_glu_w': s5_glu_w,
         's5_glu_b': s5_glu_b, 'w_out_ab': w_out_ab, 'w_in_c': w_in_c, 'hg_lb_logits': hg_lb_logits,
         'hg_norm_w': hg_norm_w, 'w_out_c': w_out_c, 'ffn_w_gate': ffn_w_gate, 'ffn_w_up': ffn_w_up,
         'ffn_conv_w': ffn_conv_w, 'ffn_conv_b': ffn_conv_b, 'ffn_w_down': ffn_w_down}
    bp, lp = x_prompt.shape[0], x_prompt.shape[1]
    pos_p = jnp.broadcast_to(jnp.arange(lp, dtype=jnp.int32)[None, :], (bp, lp))
    z_re = jnp.zeros((N_AB_LAYERS, bp, S5_GROUPS, S5_STATE), F32)
    z_ret = jnp.zeros((N_AB_LAYERS, bp, RET_HEADS, RET_DK, RET_DV), F32)
    z_hg = jnp.zeros((N_C_LAYERS, bp, HG_HEADS, HG_DK, HG_DV), F32)
    z_conv = jnp.zeros((DEPTH, bp, CONV_W - 1, D_FF), x_prompt.dtype)
    y_prompt, re_p, im_p, ret_p, hg_p, conv_p = _trunk(x_prompt, pos_p, z_re, z_re, z_ret, z_hg, z_conv, p)
    ls = x_sample.shape[1]
    pos_s = pos_sample[:, None] + jnp.arange(ls, dtype=jnp.int32)[None, :]
    y_sample, re_s, im_s, ret_s, hg_s, conv_s = _trunk(x_sample, pos_s, state_s5_re, state_s5_im,
                                                       state_ret, state_hgrn, state_ffn_conv, p)
    return (y_prompt, y_sample, re_p, im_p, ret_p, hg_p, conv_p, re_s, im_s, ret_s, hg_s, conv_s)
```

```python
import numpy as np
import concourse.bass as bass
import concourse.mybir as mybir
from concourse.bass_utils import run_bass_kernel_spmd

F32 = mybir.dt.float32
BF16 = mybir.dt.bfloat16
I32 = mybir.dt.int32
AF = mybir.ActivationFunctionType
ALU = mybir.AluOpType

NCORES = 8
D = 1024
KT = 8
SEQ = 2048
NSEG = 2
SEGT = SEQ // NSEG
NSMP = 64
NSS = 16
W0 = SEGT + NSMP
DFF = 2816
FT = DFF // 128
EPS = 1e-6

ENGS = ["pe", "act", "dve", "pool", "sp"]
NDSEM = 24


class Tile:
    __slots__ = ("name", "ap", "w", "r", "persist")

    def __init__(self, name, ap=None):
        self.name = name
        self.ap = ap
        self.w = None
        self.r = {}
        self.persist = False


class Sched:
    def __init__(self, nc):
        self.nc = nc
        self.ops = {e: [] for e in ENGS}
        self.serial = {e: 0 for e in ENGS}
        self.nvc = len(ENGS) + NDSEM
        self.vc = {e: [0] * self.nvc for e in ENGS}
        self.snap = {e: [None] for e in ENGS}
        self.pe_inc = set()
        self.dsem_val = [0] * NDSEM
        self.dsem_next = 0
        self.eidx = {e: i for i, e in enumerate(ENGS)}
        self.out_dma = {}
        self.sp_barrier = None

    def _need(self, eng, dep, waits, raw=True):
        vc = self.vc[eng]
        if dep[0] == "e":
            _, e2, s2 = dep
            if e2 == eng and not raw and eng == "pe":
                return
            i2 = self.eidx[e2]
            if vc[i2] >= s2:
                return
            waits.append(dep)
            if e2 == "pe":
                self.pe_inc.add(s2)
            sn = self.snap[e2][s2]
            for i in range(self.nvc):
                if sn[i] > vc[i]:
                    vc[i] = sn[i]
            if vc[i2] < s2:
                vc[i2] = s2
        else:
            _, k, v = dep
            i2 = len(ENGS) + k
            if vc[i2] >= v:
                return
            waits.append(dep)
            vc[i2] = v

    def op(self, eng, fn, reads=(), writes=()):
        waits = []
        for t in reads:
            if t.w is not None:
                self._need(eng, t.w, waits)
        for t in writes:
            if t.w is not None:
                self._need(eng, t.w, waits, raw=False)
            for d in list(t.r.values()):
                self._need(eng, d, waits, raw=False)
        self.serial[eng] += 1
        s = self.serial[eng]
        tok = ("e", eng, s)
        for t in writes:
            t.w = tok
            t.r = {}
        for t in reads:
            t.r[("e", eng)] = tok
        sn = list(self.vc[eng])
        sn[self.eidx[eng]] = s
        self.snap[eng].append(tuple(sn))
        self.ops[eng].append((fn, waits, s, None))
        return tok

    def dma(self, fn, reads=(), writes=()):
        eng = "sp"
        waits = []
        if self.sp_barrier is not None and any(not t.persist for t in writes):
            for dep in self.sp_barrier:
                self._need(eng, dep, waits)
            self.sp_barrier = None
        for t in reads:
            if t.w is not None:
                self._need(eng, t.w, waits)
        for t in writes:
            if t.w is not None:
                self._need(eng, t.w, waits)
            for d in list(t.r.values()):
                self._need(eng, d, waits)
        k = self.dsem_next
        self.dsem_next = (k + 1) % NDSEM
        if self.dsem_val[k] > 0:
            self._need(eng, ("d", k, self.dsem_val[k]), waits)
        self.dsem_val[k] += 16
        tok = ("d", k, self.dsem_val[k])
        if reads:
            self.out_dma[k] = self.dsem_val[k]
        self.serial[eng] += 1
        s = self.serial[eng]
        for t in writes:
            t.w = tok
            t.r = {}
        for t in reads:
            t.r[("d", k)] = tok
        self.snap[eng].append(tuple(self.vc[eng]))
        self.ops[eng].append((fn, waits, s, k))
        return tok

    def barrier(self, full=False):
        cur = {e: self.serial[e] for e in ENGS if e != "sp"}
        for e in ENGS:
            if e == "pe":
                continue
            if e == "sp" and not full:
                deps = [("e", e2, s2) for e2, s2 in cur.items() if s2 > 0] + [("d", k, v) for k, v in self.out_dma.items()]
                self.sp_barrier = deps if self.sp_barrier is None else self.sp_barrier + deps
                continue
            waits = []
            for e2, s2 in cur.items():
                if s2 > 0 and e2 != e:
                    self._need(e, ("e", e2, s2), waits)
            for k, v in self.out_dma.items():
                self._need(e, ("d", k, v), waits)
            if waits:
                self.ops[e].append((None, waits, None, None))

    def finish(self):
        waits = []
        for e in ENGS:
            if e != "sp" and self.serial[e] > 0:
                self._need("sp", ("e", e, self.serial[e]), waits)
        for k in range(NDSEM):
            if self.dsem_val[k] > 0:
                self._need("sp", ("d", k, self.dsem_val[k]), waits)
        self.ops["sp"].append((None, waits, None, None))

    def emit(self, sems, dsems, block):
        nc = self.nc
        pe_sorted = sorted(self.pe_inc)
        pe_rank = {s: i + 1 for i, s in enumerate(pe_sorted)}
        handles = {"pe": nc.tensor, "act": nc.scalar, "dve": nc.vector, "pool": nc.gpsimd, "sp": nc.sync}

        def run(eng):
            h = handles[eng]
            for fn, waits, s, dk in self.ops[eng]:
                for d in waits:
                    if d[0] == "e":
                        v = pe_rank[d[2]] if d[1] == "pe" else d[2]
                        h.wait_ge(sems[d[1]], v)
                    else:
                        h.wait_ge(dsems[d[1]], d[2])
                if fn is None:
                    continue
                ins = fn()
                if dk is not None:
                    ins.then_inc(dsems[dk], 16)
                elif eng == "pe":
                    if s in pe_rank:
                        ins.then_inc(sems[eng], 1)
                elif eng != "sp":
                    ins.then_inc(sems[eng], 1)

        block.tensor(lambda e: run("pe"))
        block.scalar(lambda e: run("act"))
        block.vector(lambda e: run("dve"))
        block.gpsimd(lambda e: run("pool"))
        block.sync(lambda e: run("sp"))


def _const_layout():
    off = {}
    cur = 0

    def add(name, n):
        nonlocal cur
        off[name] = (cur, n)
        cur += n

    add("ident", 128)
    add("eps", 1)
    add("nw", 5 * 8)
    add("convw", 2 * 3 * FT)
    add("convb", 2 * FT)
    add("glu_b", 4)
    add("s5_d", 4)
    add("hg_lg", 2 * 8)
    add("inv_freq", 1)
    add("sgn", 1)
    add("one", 1)
    add("hg_nw", 1)
    add("maskT", 4 * 128)
    add("qdec", 4 * 128)
    add("kdec", 4)
    add("g128", 4)
    add("g4", 4)
    add("maskS", 4 * 64)
    add("qdecS", 4 * 64)
    add("kdecS", 4)
    add("seqmask", 16)
    add("triBD", 128)
    add("supBD", 128)
    add("triS", 64)
    add("supS", 64)
    return off, cur


CL, NCONST = _const_layout()
RET_GAMMA = [1.0 - 2.0 ** (-5.0 - h) for h in range(4)]


class Arena:
    def __init__(self, ap_words, nwords):
        self.ap = ap_words
        self.n = nwords
        self.cur = 0

    def mark(self):
        return self.cur

    def release(self, m):
        self.cur = m

    def alloc(self, name, free_shape, dtype=F32):
        n = int(np.prod(free_shape))
        words = n if dtype in (F32, I32) else (n + 1) // 2
        words = (words + 7) // 8 * 8
        assert self.cur + words <= self.n, f"arena overflow at {name}: need {words} have {self.n - self.cur}"
        a = self.ap[:, self.cur:self.cur + words]
        self.cur += words
        if dtype == BF16:
            a = a.bitcast(BF16)[:, 0:n]
        elif dtype == I32:
            a = a.bitcast(I32)[:, 0:n]
        else:
            a = a[:, 0:n]
        if len(free_shape) == 2:
            a = a.rearrange("p (a b) -> p a b", a=free_shape[0])
        elif len(free_shape) == 3:
            a = a.rearrange("p (a b c) -> p a b c", a=free_shape[0], b=free_shape[1])
        return Tile(name, a)


class Rot:
    def __init__(self, tiles):
        self.tiles = tiles
        self.i = 0

    def next(self):
        t = self.tiles[self.i]
        self.i = (self.i + 1) % len(self.tiles)
        return t


from functools import partial
from contextlib import ExitStack

ARENA_WORDS = 39600


class Prog:
    def __init__(self, debug=None):
        self.debug = debug or {}
        self.nc = bass.Bass("TRN2", target_bir_lowering=False)
        self.sch = Sched(self.nc)
        self.h = {"pe": self.nc.tensor, "act": self.nc.scalar, "dve": self.nc.vector, "pool": self.nc.gpsimd}
        self.inputs = {}
        self.outputs = {}

    def din(self, name, shape, dt=F32):
        ap = self.nc.dram_tensor(name, list(shape), dt, kind="ExternalInput").ap()
        self.inputs[name] = ap
        return ap

    def dout(self, name, shape):
        ap = self.nc.dram_tensor(name, list(shape), F32, kind="ExternalOutput").ap()
        self.outputs[name] = ap
        return ap

    def op(self, eng, fn, reads=(), writes=()):
        return self.sch.op(eng, fn, reads, writes)

    def tt(self, eng, out, in0, in1, op, reads, writes):
        self.op(eng, partial(self.h[eng].tensor_tensor, out=out, in0=in0, in1=in1, op=op), reads, writes)

    def stt(self, out, in0, scalar, in1, op0, op1, reads, writes):
        self.op("dve", partial(self.nc.vector.scalar_tensor_tensor, out=out, in0=in0, scalar=scalar, in1=in1,
                               op0=op0, op1=op1), reads, writes)

    def ts(self, eng, out, in0, s1, s2, op0, op1, reads, writes):
        if s2 is None and eng == "pool" and op0 == ALU.mult:
            s2, op1 = 0.0, ALU.add
        if s2 is None:
            self.op(eng, partial(self.h[eng].tensor_scalar, out=out, in0=in0, scalar1=s1, scalar2=None, op0=op0),
                    reads, writes)
        else:
            self.op(eng, partial(self.h[eng].tensor_scalar, out=out, in0=in0, scalar1=s1, scalar2=s2, op0=op0,
                                 op1=op1), reads, writes)

    def cp(self, eng, out, in_, reads, writes):
        if eng == "act":
            self.op(eng, partial(self.nc.scalar.copy, out=out, in_=in_), reads, writes)
        else:
            self.op(eng, partial(self.h[eng].tensor_copy, out=out, in_=in_), reads, writes)

    def act(self, out, in_, func, reads, writes, bias=None, scale=None, accum_out=None):
        kw = dict(out=out, in_=in_, func=func)
        if bias is not None:
            kw["bias"] = bias
        if scale is not None:
            kw["scale"] = scale
        if accum_out is not None:
            kw["accum_out"] = accum_out
        self.op("act", partial(self.nc.scalar.activation, **kw), reads, writes)

    def mm(self, out, lhsT, rhs, start, stop, reads, writes):
        self.op("pe", partial(self.nc.tensor.matmul, out, lhsT=lhsT, rhs=rhs, start=start, stop=stop), reads, writes)

    def tr(self, out, in_, ident, reads, writes):
        self.op("pe", partial(self.nc.tensor.transpose, out=out, in_=in_, identity=ident), reads, writes)

    def dma(self, out, in_, reads=(), writes=(), **kw):
        self.sch.dma(partial(self.nc.sync.dma_start, out=out, in_=in_, **kw), reads, writes)

    def load_w(self, dst, src, K, n, cast_eng="act"):
        srcv = src.rearrange("(k p) n -> p k n", p=128)
        kk = max(1, 1024 // n)
        for k0 in range(0, K, kk):
            k1 = min(K, k0 + kk)
            st = self.wstage.next()
            sv = st.ap[:, 0:(k1 - k0) * n].rearrange("p (k n) -> p k n", n=n)
            self.dma(sv, srcv[:, k0:k1, :], reads=[], writes=[st])
            self.cp(cast_eng, dst.ap[:, k0:k1, :], sv, reads=[st], writes=[dst])

    def build(self):
        nc = self.nc
        with ExitStack() as es:
            self.es = es
            self._declare_dram()
            self._alloc(es)
            self._body()
            self.sch.finish()
            sems = {e: es.enter_context(nc.semaphore("sem_" + e)) for e in ENGS}
            dsems = [es.enter_context(nc.semaphore("dsem%d" % k)) for k in range(NDSEM)]
            block = es.enter_context(nc.Block())
            self.sch.emit(sems, dsems, block)
        return nc

    def _declare_dram(self):
        d = self.din
        self.xp = d("xp", [SEQ, D])
        self.xs = d("xs", [NSMP, D])
        self.consts_d = d("consts", [128, NCONST])
        self.wg_d = d("wg", [2, D, DFF])
        self.wu_d = d("wu", [2, D, DFF])
        self.wd_d = d("wd", [2, DFF, D])
        self.conv0_d = d("conv0", [2, 32, DFF])
        self.w_in_ab = d("w_in_ab", [D, 2560])
        self.glu_w = d("glu_w", [512, 512])
        self.w_out_ab = d("w_out_ab", [D, D])
        self.s5_sp = d("s5_sp", [128, 48])
        self.s5_BT = d("s5_BT", [2, 128, 2048])
        self.s5_CT = d("s5_CT", [2, 128, 2048])
        self.s5re0 = d("s5re0", [NSS, 2048])
        self.s5im0 = d("s5im0", [NSS, 2048])
        self.ret0 = d("ret0", [NSS, 4, 128, 128])
        self.pos_d = d("pos", [128, NSS], I32)
        self.cpos_d = d("cpos", [128, SEGT])
        self.w_in_c = d("w_in_c", [D, 4096])
        self.w_out_c = d("w_out_c", [D, D])
        self.hg0 = d("hg0", [NSS, 8, 128, 128])
        o = self.dout
        self.y_p = o("y_p", [SEQ, D])
        self.y_s = o("y_s", [NSMP, D])
        self.conv_p = o("conv_p", [2, 2, DFF])
        self.conv_s = o("conv_s", [2, 32, DFF])
        self.s5re_p = o("s5re_p", [16, 128])
        self.s5im_p = o("s5im_p", [16, 128])
        self.ret_p = o("ret_p", [4, 128, 128])
        self.s5re_s = o("s5re_s", [NSS, 2048])
        self.s5im_s = o("s5im_s", [NSS, 2048])
        self.ret_s = o("ret_s", [NSS, 4, 128, 128])
        self.hg_p = o("hg_p", [8, 128, 128])
        self.hg_s = o("hg_s", [NSS, 8, 128, 128])

    def _alloc(self, es):
        nc = self.nc
        sb = lambda name, shape, dt=F32: es.enter_context(nc.sbuf_tensor(name, shape, dt))
        self.consts = Tile("consts", sb("consts_sb", [128, NCONST])[:])
        self.xT = [Tile("xT%d" % c, None) for c in range(3)]
        xT_full = sb("xT", [128, KT, W0])
        self.xT_ap = xT_full
        self.identb = Tile("identb", sb("identb", [128, 128], BF16)[:])
        self.onesb = Tile("onesb", sb("onesb", [128, 128], BF16)[:])
        self.tails = Tile("tails", sb("tails", [128, 2, 2, FT])[:])
        self.Sret = Tile("Sret", sb("Sret", [128, 4, 128])[:])
        self.Sretb = Tile("Sretb", sb("Sretb", [128, 4, 128], BF16)[:])
        self.s5carry = Tile("s5carry", sb("s5carry", [128, 2, 16])[:])
        self.Shg = Tile("Shg", sb("Shg", [128, 8, 128])[:])
        self.Shgb = Tile("Shgb", sb("Shgb", [128, 8, 128], BF16)[:])
        arena_t = sb("arena", [128, ARENA_WORDS])
        self.arena = Arena(arena_t, ARENA_WORDS)
        self.ps = [Tile("ps%d" % i, es.enter_context(nc.psum_tensor("ps%d" % i, [128, 512], F32))[:]) for i in range(8)]

    def _sq_halves(self, s):
        key = id(s)
        if not hasattr(self, "_sqh"):
            self._sqh = {}
        if key not in self._sqh:
            self._sqh[key] = (Tile(s.name + "_A", s.ap), Tile(s.name + "_B", s.ap))
        return self._sqh[key]

    def alloc_hT(self):
        return [self.arena.alloc("hT%d" % c, [KT, 512 if c < 2 else NSMP], BF16) for c in range(3)]

    def C(self, name, rows=128):
        o, n = CL[name]
        return self.consts.ap[0:rows, o:o + n]

    def chunks(self, seg):
        ch = [(0, 0, 512), (1, 512, 512)]
        if seg == 0:
            ch.append((2, 1024, NSMP))
        return ch

    def _body(self):
        nc = self.nc
        ar = self.arena
        self.dma(self.consts.ap, self.consts_d, writes=[self.consts])
        self.cp("dve", self.identb.ap, self.C("ident"), [self.consts], [self.identb])
        self.op("dve", partial(nc.vector.memset, self.onesb.ap, 1.0), [], [self.onesb])
        self.op("dve", partial(nc.vector.memset, self.tails.ap, 0.0), [], [self.tails])
        self.op("dve", partial(nc.vector.memset, self.Sret.ap, 0.0), [], [self.Sret])
        self.op("dve", partial(nc.vector.memset, self.Sretb.ap, 0.0), [], [self.Sretb])
        self.op("dve", partial(nc.vector.memset, self.s5carry.ap, 0.0), [], [self.s5carry])
        self.op("dve", partial(nc.vector.memset, self.Shg.ap, 0.0), [], [self.Shg])
        self.op("dve", partial(nc.vector.memset, self.Shgb.ap, 0.0), [], [self.Shgb])
        self._pb = 0
        for seg in range(NSEG):
            self.seg = seg
            m0 = ar.mark()
            self.wstage = Rot([ar.alloc("wst%d" % i, [1024]) for i in range(4)])
            for t_ in self.wstage.tiles:
                t_.persist = True
            self.sq = Rot([ar.alloc("sq%d" % i, [KT, 512], BF16) for i in range(1)])
            self.rs = Rot([ar.alloc("rs%d" % i, [512]) for i in range(1)])
            self.load_x(seg)
            for layer in ([0] if self.debug.get("only_ab") else [1] if self.debug.get("only_c") else [0, 1]):
                if not self.debug.get("skip_mixer"):
                    if layer == 0:
                        self.mixer_ab(seg)
                    else:
                        self.mixer_c(seg)
                if not self.debug.get("skip_ffn"):
                    self.ffn(seg, layer)
            self.final(seg)
            self.sch.barrier(full=True)
            ar.release(m0)

    def xcols(self, c, k0=0, k1=KT):
        c0, n = [(0, 512), (512, 512), (1024, NSMP)][c]
        return self.xT_ap[:, k0:k1, c0:c0 + n]

    def load_x(self, seg):
        ar = self.arena
        m = ar.mark()
        stg = Rot([ar.alloc("xst%d" % i, [D]) for i in range(2)])
        ident = self.C("ident")
        tiles = [(self.xp[seg * SEGT + t * 128: seg * SEGT + (t + 1) * 128, :], 128, t // 4, (t % 4) * 128) for t in range(8)]
        if seg == 0:
            tiles.append((self.xs[:, :], NSMP, 2, 0))
        banks = Rot([self.ps[6], self.ps[7]])
        for (src, rows, c, off) in tiles:
            st = stg.next()
            self.dma(st.ap[0:rows, :], src, writes=[st])
            for half in range(2):
                b = banks.next()
                for q in range(4):
                    k = half * 4 + q
                    self.tr(b.ap[:, q * 128:q * 128 + rows], st.ap[0:rows, k * 128:(k + 1) * 128], ident[0:rows, 0:rows],
                            [st, self.consts], [b])
                c0 = [0, 512, 1024][c] + off
                dst = self.xT_ap[:, half * 4:half * 4 + 4, c0:c0 + rows]
                srcv = b.ap.rearrange("p (q t) -> p q t", q=4)[:, :, 0:rows]
                self.cp("act", dst, srcv, [b], [self.xT[c]])
        self.sch.barrier()
        ar.release(m)

    def rmsnorm(self, seg, idx, out_tiles):
        sq, rs = self.sq, self.rs
        nwo = CL["nw"][0]
        eps_ap = self.C("eps")
        for (c, c0, n) in self.chunks(seg):
            s = sq.next()
            if not hasattr(s, "_halves"):
                pass
            sA, sB = self._sq_halves(s)
            for k in range(KT):
                xk = self.xcols(c, k, k + 1)[:, 0, :]
                if k < 5:
                    self.act(s.ap[:, k, 0:n], xk, AF.Square, [self.xT[c]], [sA])
                else:
                    self.tt("pool", s.ap[:, k, 0:n], xk, xk, ALU.mult, [self.xT[c]], [sB])
            b = self.ps[5]
            for k in range(KT):
                self.mm(b.ap[:, 0:n], self.onesb.ap, s.ap[:, k, 0:n], k == 0, k == KT - 1,
                        [self.onesb, sA if k < 5 else sB], [b])
            r = rs.next()
            self.act(r.ap[:, 0:n], b.ap[:, 0:n], AF.Ln, [b], [r], bias=eps_ap, scale=1.0 / D)
            self.act(r.ap[:, 0:n], r.ap[:, 0:n], AF.Exp, [r], [r], scale=-0.5)
            for k in range(KT):
                self.stt(out_tiles[c].ap[:, k, 0:n], self.xcols(c, k, k + 1)[:, 0, :],
                         self.consts.ap[:, nwo + idx * 8 + k: nwo + idx * 8 + k + 1], r.ap[:, 0:n],
                         ALU.mult, ALU.mult, [self.xT[c], self.consts, r], [out_tiles[c]])

    def ffn(self, seg, layer):
        nc = self.nc
        ar = self.arena
        m0 = ar.mark()
        self.hT = self.alloc_hT()
        groups = [list(range(0, 6)), list(range(6, 12)), list(range(12, 17)), list(range(17, 22))]
        wgt = Rot([ar.alloc("wg%d" % i, [KT, 128], BF16) for i in range(4)])
        wut = Rot([ar.alloc("wu%d" % i, [KT, 128], BF16) for i in range(4)])
        wdt = Rot([ar.alloc("wd%d" % i, [6, D], BF16) for i in range(2)])
        actTs = [[ar.alloc("act%d_%d" % (i, c), [6, 512 if c < 2 else NSMP], BF16) for c in range(3)] for i in range(2)]
        exth = Rot([ar.alloc("exth%d" % i, [2]) for i in range(24)])
        cbuf = Rot([ar.alloc("cb%d" % i, [512]) for i in range(3)])
        sbuf = Rot([ar.alloc("sb%d" % i, [512]) for i in range(2)])
        class _G:
            def next(_s):
                return self.pbank()
        psA = _G()
        psB = _G()
        ident = self.C("ident")
        cwo = CL["convw"][0]
        cbo = CL["convb"][0]
        cw = lambda j, f: self.consts.ap[:, cwo + (layer * 3 + j) * FT + f: cwo + (layer * 3 + j) * FT + f + 1]
        cbias = lambda f: self.consts.ap[:, cbo + layer * FT + f: cbo + layer * FT + f + 1]
        chunks = self.chunks(seg)
        if seg == 0:
            cs = ar.alloc("cs", [DFF])
            bufT = ar.alloc("bufT", [FT, 32])
            ext_sh = ar.alloc("ext_sh", [FT, 16, 2])
            ext_sb = ar.alloc("ext_sb", [FT, 16, 4])
            ext_s2 = ar.alloc("ext_s2", [FT, 32])
            self.dma(cs.ap[0:32, :], self.conv0_d[layer], writes=[cs])
            for half in range(2):
                b = self.ps[6 + half]
                f0, f1 = (0, 16) if half == 0 else (16, FT)
                for f in range(f0, f1):
                    q = f - f0
                    self.tr(b.ap[:, q * 32:(q + 1) * 32], cs.ap[0:32, f * 128:(f + 1) * 128], ident[0:32, 0:32],
                            [cs, self.consts], [b])
                self.cp("act", bufT.ap[:, f0:f1, :], b.ap[:, 0:(f1 - f0) * 32].rearrange("p (f t) -> p f t", t=32),
                        [b], [bufT])
            self.cp("pool", ext_sh.ap, bufT.ap.rearrange("p f (s j) -> p f s j", j=2), [bufT], [ext_sh])

        def load_gu(f):
            g, u = wgt.next(), wut.next()
            self.load_w(g, self.wg_d[layer][:, f * 128:(f + 1) * 128], KT, 128)
            self.load_w(u, self.wu_d[layer][:, f * 128:(f + 1) * 128], KT, 128)
            return g, u

        def load_d(grp):
            w = wdt.next()
            self.load_w(w, self.wd_d[layer][grp[0] * 128:(grp[-1] + 1) * 128, :], len(grp), D)
            return w

        allf = [f for g in groups for f in g]
        pend = {allf[0]: load_gu(allf[0]), allf[1]: load_gu(allf[1])}
        self.rmsnorm(seg, 1 + 2 * layer, self.hT)

        def phase_b(grp, wd, actT):
            for (c, c0, n) in chunks:
                for mo in range(KT):
                    b = psB.next()
                    for fl in range(len(grp)):
                        self.mm(b.ap[:, 0:n], wd.ap[:, fl, mo * 128:(mo + 1) * 128], actT[c].ap[:, fl, 0:n],
                                fl == 0, fl == len(grp) - 1, [wd, actT[c]], [b])
                    xv = self.xcols(c, mo, mo + 1)[:, 0, :]
                    self.tt("dve", xv, xv, b.ap[:, 0:n], ALU.add, [self.xT[c], b], [self.xT[c]])

        pend_b = None
        pend_tail = None
        for gi, grp in enumerate(groups):
            wd = load_d(grp)
            actT = actTs[gi % 2]
            prev_ext = None
            for fl, f in enumerate(grp):
                g, u = pend.pop(f)
                nxt = allf.index(f) + 2
                if nxt < len(allf):
                    pend[allf[nxt]] = load_gu(allf[nxt])
                for (c, c0, n) in chunks:
                    gb, ub = psA.next(), psA.next()
                    for k in range(KT):
                        self.mm(gb.ap[:, 0:n], g.ap[:, k, :], self.hT[c].ap[:, k, 0:n], k == 0, k == KT - 1,
                                [g, self.hT[c]], [gb])
                    for k in range(KT):
                        self.mm(ub.ap[:, 0:n], u.ap[:, k, :], self.hT[c].ap[:, k, 0:n], k == 0, k == KT - 1,
                                [u, self.hT[c]], [ub])
                    cb, sb_ = cbuf.next(), sbuf.next()
                    if c < 2:
                        eh, eb = exth.next(), exth.next()
                        if c == 0:
                            self.cp("pool", eh.ap, self.tails.ap[:, layer, :, f], [self.tails], [eh])
                        else:
                            self.cp("pool", eh.ap, prev_ext.ap, [prev_ext], [eh])
                        self.cp("act", eb.ap, gb.ap[:, n - 2:n], [gb], [eb])
                        self.act(cb.ap[:, 0:n], gb.ap[:, 0:n], AF.Identity, [gb, self.consts], [cb],
                                 bias=cbias(f), scale=cw(2, f))
                        self.stt(cb.ap[:, 1:n], gb.ap[:, 0:n - 1], cw(1, f), cb.ap[:, 1:n], ALU.mult, ALU.add,
                                 [gb, cb, self.consts], [cb])
                        self.stt(cb.ap[:, 2:n], gb.ap[:, 0:n - 2], cw(0, f), cb.ap[:, 2:n], ALU.mult, ALU.add,
                                 [gb, cb, self.consts], [cb])
                        hc, hc2 = exth.next(), exth.next()
                        self.ts("pool", hc.ap, eh.ap, cw(0, f), None, ALU.mult, None, [eh, self.consts], [hc])
                        self.ts("pool", hc2.ap[:, 0:1], eh.ap[:, 1:2], cw(1, f), None, ALU.mult, None, [eh, self.consts], [hc2])
                        self.tt("pool", hc.ap[:, 0:1], hc.ap[:, 0:1], hc2.ap[:, 0:1], ALU.add, [hc, hc2], [hc])
                        self.tt("dve", cb.ap[:, 0:2], cb.ap[:, 0:2], hc.ap, ALU.add, [cb, hc], [cb])
                        if c == 1:
                            self.cp("pool", self.tails.ap[:, layer, :, f], eb.ap, [eb], [self.tails])
                        prev_ext = eb
                    else:
                        gv = gb.ap[:, 0:NSMP].rearrange("p (s t) -> p s t", t=4)
                        cv = cb.ap[:, 0:NSMP].rearrange("p (s t) -> p s t", t=4)
                        esb = ext_sb.ap[:, f]
                        esh = ext_sh.ap[:, f]
                        self.cp("act", esb, gv, [gb], [ext_sb])
                        self.cp("pool", ext_s2.ap[:, f].rearrange("p (s j) -> p s j", j=2), esb[:, :, 2:4], [ext_sb], [ext_s2])
                        self.act(cb.ap[:, 0:NSMP], gb.ap[:, 0:NSMP], AF.Identity, [gb, self.consts], [cb],
                                 bias=cbias(f), scale=cw(2, f))
                        self.stt(cv[:, :, 1:4], esb[:, :, 0:3], cw(1, f), cv[:, :, 1:4], ALU.mult, ALU.add,
                                 [ext_sb, cb, self.consts], [cb])
                        self.stt(cv[:, :, 2:4], esb[:, :, 0:2], cw(0, f), cv[:, :, 2:4], ALU.mult, ALU.add,
                                 [ext_sb, cb, self.consts], [cb])
                        self.stt(cv[:, :, 0:1], esh[:, :, 1:2], cw(1, f), cv[:, :, 0:1], ALU.mult, ALU.add,
                                 [ext_sh, cb, self.consts], [cb])
                        self.stt(cv[:, :, 0:2], esh[:, :, 0:2], cw(0, f), cv[:, :, 0:2], ALU.mult, ALU.add,
                                 [ext_sh, cb, self.consts], [cb])
                    def tail(cb=cb, sb_=sb_, ub=ub, actTc=actT[c], fl=fl, n=n):
                        self.act(sb_.ap[:, 0:n], cb.ap[:, 0:n], AF.Silu, [cb], [sb_])
                        self.tt("dve", actTc.ap[:, fl, 0:n], sb_.ap[:, 0:n], ub.ap[:, 0:n], ALU.mult, [sb_, ub], [actTc])
                    if pend_tail is not None:
                        pend_tail()
                    pend_tail = tail
            if pend_tail is not None:
                pend_tail()
                pend_tail = None
            if pend_b is not None:
                phase_b(*pend_b)
            pend_b = (grp, wd, actT)
        phase_b(*pend_b)
        if seg == 0:
            cso = cs
            for q0 in range(0, FT, 4):
                b = psB.next()
                fs = list(range(q0, min(FT, q0 + 4)))
                for qi, f in enumerate(fs):
                    self.mm(b.ap[0:32, qi * 128:(qi + 1) * 128], ext_s2.ap[:, f, :], ident, True, True,
                            [ext_s2, self.consts], [b])
                self.cp("act", cso.ap[0:32, q0 * 128:(q0 + len(fs)) * 128], b.ap[0:32, 0:len(fs) * 128], [b], [cso])
            self.dma(self.conv_s[layer], cso.ap[0:32, :], reads=[cso])
        if seg == NSEG - 1:
            b = psB.next()
            tl = ar.alloc("tl", [128])
            self.mm(b.ap[0:2 * FT, 0:128], self.tails.ap[:, layer].rearrange("p j f -> p (j f)"), ident, True, True,
                    [self.tails, self.consts], [b])
            self.cp("act", tl.ap[0:2 * FT, :], b.ap[0:2 * FT, 0:128], [b], [tl])
            for j in range(2):
                self.dma(self.conv_p[layer, j].rearrange("(f p) -> f p", p=128), tl.ap[j * FT:(j + 1) * FT, :], reads=[tl])
        self.sch.barrier()
        ar.release(m0)

    def final(self, seg):
        ar = self.arena
        m0 = ar.mark()
        hF = [ar.alloc("hF%d" % c, [KT, 512 if c < 2 else NSMP]) for c in range(3)]
        self.rmsnorm(seg, 4, hF)
        osts = []
        for i in range(2):
            t_ = ar.alloc("ost%d" % i, [D])
            osts.append((t_, Tile("ostb%d" % i, t_.ap)))
        ost = Rot(osts)
        ident = self.C("ident")
        banks = Rot([self.ps[0], self.ps[1], self.ps[2], self.ps[3]])
        tiles = [(t // 4, (t % 4) * 128, 128, self.y_p[seg * SEGT + t * 128: seg * SEGT + (t + 1) * 128, :]) for t in range(8)]
        if seg == 0:
            tiles.append((2, 0, NSMP, self.y_s[:, :]))
        for (c, off, rows, dst) in tiles:
            oa, ob = ost.next()
            for half in range(2):
                b = banks.next()
                for q in range(4):
                    k = half * 4 + q
                    self.tr(b.ap[0:rows, q * 128:(q + 1) * 128], hF[c].ap[:, k, off:off + rows], ident, [hF[c], self.consts], [b])
                self.cp("act" if half == 0 else "dve", oa.ap[0:rows, half * 512:(half + 1) * 512], b.ap[0:rows, :], [b],
                        [oa] if half == 0 else [ob])
            self.dma(dst, oa.ap[0:rows, :], reads=[oa, ob])
        self.sch.barrier()
        ar.release(m0)


def _fm(v):
    v = np.asarray(v, np.float32)
    return np.ascontiguousarray(v.reshape(-1, 128).T)


def _build_consts(inp):
    c = np.zeros((128, NCONST), np.float32)

    def put(name, arr, rows=128):
        o, n = CL[name]
        arr = np.asarray(arr, np.float32).reshape(rows, n)
        c[0:rows, o:o + n] = arr

    put("ident", np.eye(128, dtype=np.float32))
    put("eps", np.full((128, 1), EPS, np.float32))
    nw = np.stack([_fm(inp["norm_mix"][0]), _fm(inp["norm_ffn"][0]), _fm(inp["norm_mix"][1]), _fm(inp["norm_ffn"][1]),
                   _fm(inp["norm_final"])], axis=1)
    put("nw", nw)
    cw = np.asarray(inp["ffn_conv_w"], np.float32).reshape(2, 3, FT, 128).transpose(3, 0, 1, 2)
    put("convw", cw)
    cb = np.asarray(inp["ffn_conv_b"], np.float32).reshape(2, FT, 128).transpose(2, 0, 1)
    put("convb", cb)
    put("glu_b", _fm(inp["s5_glu_b"][0]))
    put("s5_d", _fm(np.asarray(inp["s5_d"][0]).reshape(-1)))
    lg = np.stack([_fm(inp["hg_lb_logits"][0]), _fm(inp["hg_lb_logits"][1])], axis=1)
    put("hg_lg", lg)
    half = 64
    inv = (1.0 / (10000.0 ** np.linspace(0.0, 1.0, half, dtype=np.float32))).astype(np.float32)
    p = np.arange(128)
    put("inv_freq", inv[p % 64].reshape(128, 1))
    put("sgn", np.where(p < 64, -1.0, 1.0).reshape(128, 1))
    put("one", np.ones((128, 1), np.float32))
    put("hg_nw", np.asarray(inp["hg_norm_w"][0], np.float32).reshape(128, 1))
    g = np.array(RET_GAMMA, np.float64)
    scale = 128.0 ** -0.5
    j = np.arange(128)[:, None]
    i = np.arange(128)[None, :]
    mt = np.zeros((128, 4, 128))
    for h in range(4):
        mt[:, h, :] = np.where(i >= j, g[h] ** np.maximum(i - j, 0), 0.0) * scale
    put("maskT", mt)
    qd = np.zeros((128, 4, 128))
    kd = np.zeros((128, 4))
    for h in range(4):
        qd[:, h, :] = (g[h] ** (np.arange(128) + 1))[None, :]
        kd[:, h] = g[h] ** (127 - np.arange(128)) * scale
    put("qdec", qd)
    put("kdec", kd)
    put("g128", np.broadcast_to((g ** 128)[None, :], (128, 4)))
    put("g4", np.broadcast_to((g ** 4)[None, :], (128, 4)))
    tok = np.arange(64)
    sq_, tau = tok // 4, tok % 4
    ms = np.zeros((128, 4, 64))
    for h in range(4):
        same = (sq_[:, None] == sq_[None, :]) & (tau[None, :] >= tau[:, None])
        ms[0:64, h, :] = np.where(same, g[h] ** np.maximum(tau[None, :] - tau[:, None], 0), 0.0) * scale
    put("maskS", ms)
    qds = np.zeros((128, 4, 64))
    kds = np.zeros((128, 4))
    for h in range(4):
        qds[:, h, :] = (g[h] ** (tau + 1))[None, :]
        kds[0:64, h] = g[h] ** (3 - tau) * scale
    put("qdecS", qds)
    put("kdecS", kds)
    sm = np.zeros((128, 16))
    sm[tok, sq_] = 1.0
    put("seqmask", sm)
    jj = np.arange(128)[:, None]
    ii = np.arange(128)[None, :]
    samec = (jj // 64) == (ii // 64)
    put("triBD", (samec & (jj <= ii)).astype(np.float32))
    put("supBD", (samec & (jj > ii)).astype(np.float32))
    ts_ = np.zeros((128, 64))
    us_ = np.zeros((128, 64))
    sames = sq_[:, None] == sq_[None, :]
    ts_[0:64] = (sames & (tok[:, None] <= tok[None, :]))
    us_[0:64] = (sames & (tok[:, None] > tok[None, :]))
    put("triS", ts_)
    put("supS", us_)
    return c


def _s5_layouts(inp):
    lam_re = np.asarray(inp["s5_lam_re"][0], np.float32)
    lam_im = np.asarray(inp["s5_lam_im"][0], np.float32)
    ldt = np.asarray(inp["s5_log_dt"][0], np.float32)

    def sp(a):
        return np.ascontiguousarray(a.reshape(16, 2, 64).transpose(1, 2, 0).reshape(128, 16))

    s5_sp = np.concatenate([sp(lam_re), sp(lam_im), sp(np.repeat(ldt[:, None], 64, axis=1))], axis=1)
    BT = np.zeros((2, 128, 16, 128), np.float32)
    CT = np.zeros((2, 128, 16, 128), np.float32)
    for ri, (bsrc, csrc) in enumerate(((inp["s5_b_re"][0], inp["s5_c_re"][0]), (inp["s5_b_im"][0], inp["s5_c_im"][0]))):
        bsrc = np.asarray(bsrc, np.float32)
        csrc = np.asarray(csrc, np.float32)
        for gidx in range(32):
            jx, g2, gl = gidx // 2, gidx % 2, gidx % 8
            BT[ri, gl * 16:(gl + 1) * 16, jx, g2 * 64:(g2 + 1) * 64] = bsrc[gidx].T
            CT[ri, g2 * 64:(g2 + 1) * 64, jx, gl * 16:(gl + 1) * 16] = csrc[gidx].T
    return s5_sp, BT.reshape(2, 128, 2048), CT.reshape(2, 128, 2048)


_PROG_CACHE = {}


def _get_prog(debug=None):
    key = tuple(sorted((debug or {}).items()))
    if key not in _PROG_CACHE:
        p = Prog(debug)
        p.build()
        _PROG_CACHE[key] = p
    return _PROG_CACHE[key]


def kernel(_debug=None, _cores=NCORES, **inp):
    inp = {k: np.asarray(v) for k, v in inp.items()}
    prog = _get_prog(_debug)
    consts = _build_consts(inp)
    s5_sp, s5_BT, s5_CT = _s5_layouts(inp)
    cpos = np.ascontiguousarray(np.broadcast_to(np.arange(SEGT, dtype=np.float32)[None, :], (128, SEGT)))
    in_maps = []
    for b in range(_cores):
        m = {
            "xp": np.ascontiguousarray(inp["x_prompt"][b]),
            "xs": np.ascontiguousarray(inp["x_sample"][NSS * b:NSS * (b + 1)].reshape(NSMP, D)),
            "consts": consts,
            "wg": inp["ffn_w_gate"], "wu": inp["ffn_w_up"], "wd": inp["ffn_w_down"],
            "conv0": np.ascontiguousarray(inp["state_ffn_conv"][:, NSS * b:NSS * (b + 1)].reshape(2, 32, DFF)),
            "w_in_ab": inp["w_in_ab"][0], "glu_w": inp["s5_glu_w"][0], "w_out_ab": inp["w_out_ab"][0],
            "s5_sp": s5_sp, "s5_BT": s5_BT, "s5_CT": s5_CT,
            "s5re0": np.ascontiguousarray(inp["state_s5_re"][0, NSS * b:NSS * (b + 1)].reshape(NSS, 2048)),
            "s5im0": np.ascontiguousarray(inp["state_s5_im"][0, NSS * b:NSS * (b + 1)].reshape(NSS, 2048)),
            "ret0": np.ascontiguousarray(inp["state_ret"][0, NSS * b:NSS * (b + 1)]),
            "pos": np.ascontiguousarray(np.broadcast_to(inp["pos_sample"][NSS * b:NSS * (b + 1)].astype(np.int32)[None, :], (128, NSS))),
            "cpos": cpos,
            "w_in_c": inp["w_in_c"][0], "w_out_c": inp["w_out_c"][0],
            "hg0": np.ascontiguousarray(inp["state_hgrn"][0, NSS * b:NSS * (b + 1)]),
        }
        in_maps.append({k: v for k, v in m.items() if k in prog.inputs})
    res = run_bass_kernel_spmd(prog.nc, in_maps, core_ids=list(range(_cores)))
    R = res.results
    B = _cores
    y_p = np.stack([R[b]["y_p"] for b in range(B)])
    y_s = np.concatenate([R[b]["y_s"].reshape(NSS, 4, D) for b in range(B)])
    conv_p = np.stack([R[b]["conv_p"] for b in range(B)], axis=1)
    conv_s = np.concatenate([R[b]["conv_s"].reshape(2, NSS, 2, DFF) for b in range(B)], axis=1)
    out = {"y_p": y_p, "y_s": y_s, "conv_p": conv_p, "conv_s": conv_s}
    if "s5re_p" in R[0]:
        out["s5re_p"] = np.stack([R[b]["s5re_p"].reshape(32, 64) for b in range(B)])[None]
        out["s5im_p"] = np.stack([R[b]["s5im_p"].reshape(32, 64) for b in range(B)])[None]
        out["ret_p"] = np.stack([R[b]["ret_p"] for b in range(B)])[None]
        out["s5re_s"] = np.concatenate([R[b]["s5re_s"].reshape(NSS, 32, 64) for b in range(B)])[None]
        out["s5im_s"] = np.concatenate([R[b]["s5im_s"].reshape(NSS, 32, 64) for b in range(B)])[None]
        out["ret_s"] = np.concatenate([R[b]["ret_s"] for b in range(B)])[None]
        out["hg_p"] = np.stack([R[b]["hg_p"] for b in range(B)])[None]
        out["hg_s"] = np.concatenate([R[b]["hg_s"] for b in range(B)])[None]
    if _debug:
        return out
    f = lambda a: np.ascontiguousarray(a, dtype=np.float32)
    return (f(out["y_p"]), f(out["y_s"]), f(out["s5re_p"]), f(out["s5im_p"]), f(out["ret_p"]), f(out["hg_p"]), f(out["conv_p"]),
            f(out["s5re_s"]), f(out["s5im_s"]), f(out["ret_s"]), f(out["hg_s"]), f(out["conv_s"]))


MAGIC = 12582912.0
TWO_PI_INV = float(1.0 / (2.0 * np.pi))
C1 = 6.28125
C2 = 0.0019353071795864769
PI = float(np.pi)


def _pbank(self):
    b = self.ps[self._pb]
    self._pb = (self._pb + 1) % 8
    return b


def _range_reduce(self, eng, out, ang, tmp, reads, tiles_w):
    o, t = out, tmp
    self.ts(eng, t.ap, ang, TWO_PI_INV, MAGIC, ALU.mult, ALU.add, reads, [t])
    self.ts(eng, t.ap, t.ap, -MAGIC, None, ALU.add, None, [t], [t])
    self.stt(o.ap, t.ap, -C1, ang, ALU.mult, ALU.add, [t] + list(reads), [o])
    self.stt(o.ap, t.ap, -C2, o.ap, ALU.mult, ALU.add, [t, o], [o])
    self.ts("dve", o.ap, o.ap, -PI, PI, ALU.max, ALU.min, [o], [o])


def _sincos(self, r, sin_out, cos_out, tmp, sin_scale=None, extra_reads=()):
    if sin_scale is None:
        self.act(sin_out.ap, r.ap, AF.Sin, [r], [sin_out])
    else:
        self.act(sin_out.ap, r.ap, AF.Sin, [r] + list(extra_reads), [sin_out], scale=sin_scale)
    self.ts("dve", tmp.ap, r.ap, -1.0, None, ALU.mult, None, [r], [tmp])
    self.tt("dve", tmp.ap, tmp.ap, r.ap, ALU.max, [tmp, r], [tmp])
    self.ts("dve", tmp.ap, tmp.ap, -1.0, PI / 2, ALU.mult, ALU.add, [tmp], [tmp])
    self.act(cos_out.ap, tmp.ap, AF.Sin, [tmp], [cos_out])


Prog.pbank = _pbank
Prog.range_reduce = _range_reduce
Prog.sincos = _sincos


def _mixer_ab(self, seg):
    ar = self.arena
    m0 = ar.mark()
    chunks = self.chunks(seg)
    ws = [512, 512, NSMP]
    yTbuf = [ar.alloc("yT%d" % c, [KT, ws[c]], BF16) for c in range(3)]
    yTs = [Tile("yTs%d" % c, yTbuf[c].ap[:, 0:4, :]) for c in range(3)]
    yTr = [Tile("yTr%d" % c, yTbuf[c].ap[:, 4:8, :]) for c in range(3)]
    uT = [ar.alloc("uT%d" % c, [4, ws[c]], BF16) for c in range(3)]
    m1 = ar.mark()
    self.hT = self.alloc_hT()
    self.rmsnorm(seg, 0, self.hT)
    wt = Rot([ar.alloc("wu5_%d" % i, [KT, 128], BF16) for i in range(2)])
    for ft in range(4):
        w = wt.next()
        self.load_w(w, self.w_in_ab[:, ft * 128:(ft + 1) * 128], KT, 128)
        for (c, c0, n) in chunks:
            b = self.pbank()
            for k in range(KT):
                self.mm(b.ap[:, 0:n], w.ap[:, k, :], self.hT[c].ap[:, k, 0:n], k == 0, k == KT - 1, [w, self.hT[c]], [b])
            self.cp("act", uT[c].ap[:, ft, 0:n], b.ap[:, 0:n], [b], [uT[c]])
    if self.debug.get("no_ret"):
        for (c, c0, n) in chunks:
            self.op("dve", partial(self.nc.vector.memset, yTr[c].ap, 0.0), [], [yTr[c]])
    else:
        self.ret_part(seg, yTr)
    self.sch.barrier()
    ar.release(m1)
    if self.debug.get("no_s5"):
        for (c, c0, n) in chunks:
            self.op("dve", partial(self.nc.vector.memset, yTs[c].ap, 0.0), [], [yTs[c]])
    else:
        self.s5_part(seg, yTs, uT)
    self.sch.barrier()
    ar.release(m1)
    wout = ar.alloc("wout", [KT, D], BF16)
    self.load_w(wout, self.w_out_ab, KT, D)
    for (c, c0, n) in chunks:
        for mo in range(KT):
            b = self.pbank()
            for k in range(KT):
                src = yTs[c] if k < 4 else yTr[c]
                self.mm(b.ap[:, 0:n], wout.ap[:, k, mo * 128:(mo + 1) * 128], yTbuf[c].ap[:, k, 0:n], k == 0, k == KT - 1,
                        [wout, src], [b])
            xv = self.xcols(c, mo, mo + 1)[:, 0, :]
            self.tt("dve", xv, xv, b.ap[:, 0:n], ALU.add, [self.xT[c], b], [self.xT[c]])
    self.sch.barrier()
    ar.release(m0)


Prog.mixer_ab = _mixer_ab


def _s5_part(self, seg, yTs, uT):
    nc = self.nc
    ar = self.arena
    ident = self.C("ident")
    V = "dve"

    def T16(name):
        return ar.alloc(name, [16])

    sp = ar.alloc("s5sp", [48])
    self.dma(sp.ap, self.s5_sp, writes=[sp])
    lre, lim, ldt = sp.ap[:, 0:16], sp.ap[:, 16:32], sp.ap[:, 32:48]
    dt, mag, ang, r, tmp, sinA, cosA = [T16(n) for n in ["dt", "mag", "ang", "r", "tmp", "sinA", "cosA"]]
    self.act(dt.ap, ldt, AF.Exp, [sp], [dt])
    self.tt(V, tmp.ap, lre, dt.ap, ALU.mult, [sp, dt], [tmp])
    self.act(mag.ap, tmp.ap, AF.Exp, [tmp], [mag])
    self.tt(V, ang.ap, lim, dt.ap, ALU.mult, [sp, dt], [ang])
    tmp2 = T16("tmp2")
    self.range_reduce(V, r, ang.ap, tmp2, [ang], None)
    tmp3 = T16("tmp3")
    self.sincos(r, sinA, cosA, tmp3)
    names = ["ab_re", "ab_im", "nr", "t1", "t2", "den", "rden", "f_re", "f_im", "if_re", "if_im"]
    ab_re, ab_im, nr, t1, t2, den, rden, f_re, f_im, if_re, if_im = [T16(n) for n in names]
    self.tt(V, ab_re.ap, mag.ap, cosA.ap, ALU.mult, [mag, cosA], [ab_re])
    self.tt(V, ab_im.ap, mag.ap, sinA.ap, ALU.mult, [mag, sinA], [ab_im])
    self.ts(V, nr.ap, ab_re.ap, -1.0, None, ALU.add, None, [ab_re], [nr])
    self.tt(V, t1.ap, lre, lre, ALU.mult, [sp], [t1])
    self.tt(V, t2.ap, lim, lim, ALU.mult, [sp], [t2])
    self.tt(V, den.ap, t1.ap, t2.ap, ALU.add, [t1, t2], [den])
    self.op(V, partial(nc.vector.reciprocal, out=rden.ap, in_=den.ap), [den], [rden])
    self.tt(V, t1.ap, nr.ap, lre, ALU.mult, [nr, sp], [t1])
    self.tt(V, t2.ap, ab_im.ap, lim, ALU.mult, [ab_im, sp], [t2])
    self.tt(V, t1.ap, t1.ap, t2.ap, ALU.add, [t1, t2], [t1])
    self.tt(V, f_re.ap, t1.ap, rden.ap, ALU.mult, [t1, rden], [f_re])
    self.tt(V, t1.ap, ab_im.ap, lre, ALU.mult, [ab_im, sp], [t1])
    self.tt(V, t2.ap, nr.ap, lim, ALU.mult, [nr, sp], [t2])
    self.tt(V, t1.ap, t1.ap, t2.ap, ALU.subtract, [t1, t2], [t1])
    self.tt(V, f_im.ap, t1.ap, rden.ap, ALU.mult, [t1, rden], [f_im])
    self.tt(V, t1.ap, f_re.ap, f_re.ap, ALU.mult, [f_re], [t1])
    self.tt(V, t2.ap, f_im.ap, f_im.ap, ALU.mult, [f_im], [t2])
    self.tt(V, den.ap, t1.ap, t2.ap, ALU.add, [t1, t2], [den])
    self.op(V, partial(nc.vector.reciprocal, out=rden.ap, in_=den.ap), [den], [rden])
    self.tt(V, if_re.ap, f_re.ap, rden.ap, ALU.mult, [f_re, rden], [if_re])
    self.tt(V, t1.ap, f_im.ap, rden.ap, ALU.mult, [f_im, rden], [t1])
    self.ts(V, if_im.ap, t1.ap, -1.0, None, ALU.mult, None, [t1], [if_im])

    R = ar.alloc("R", [4096])
    u = [Tile("u%d" % i, R.ap[:, i * 1024:(i + 1) * 1024].rearrange("p (a b) -> p a b", a=16)) for i in range(4)]
    u2 = [Tile("w%d" % i, R.ap[:, i * 2048:(i + 1) * 2048].rearrange("p (a b) -> p a b", a=16)) for i in range(2)]
    ysb_t = [Tile("ysb%d" % i, R.ap[:, i * 2048:(i + 1) * 2048].rearrange("p (a b) -> p a b", a=4)) for i in range(2)]

    cE = ar.alloc("cE", [16, 128])
    sE = ar.alloc("sE", [16, 128])
    self.op(V, partial(nc.vector.memset, cE.ap[:, :, 0:1], 1.0), [], [cE])
    self.op(V, partial(nc.vector.memset, sE.ap[:, :, 0:1], 0.0), [], [sE])
    self.cp(V, cE.ap[:, :, 1:2], cosA.ap.unsqueeze(2), [cosA], [cE])
    self.cp(V, sE.ap[:, :, 1:2], sinA.ap.unsqueeze(2), [sinA], [sE])
    p_re, p_im = cosA, sinA
    pw = [(T16("pwr%d" % i), T16("pwi%d" % i)) for i in range(7)]
    n = 2
    for i in range(7):
        q_re, q_im = pw[i]
        self.tt(V, t1.ap, p_re.ap, p_re.ap, ALU.mult, [p_re], [t1])
        self.tt(V, t2.ap, p_im.ap, p_im.ap, ALU.mult, [p_im], [t2])
        self.tt(V, q_re.ap, t1.ap, t2.ap, ALU.subtract, [t1, t2], [q_re])
        self.tt(V, t1.ap, p_re.ap, p_im.ap, ALU.mult, [p_re, p_im], [t1])
        self.ts(V, q_im.ap, t1.ap, 2.0, None, ALU.mult, None, [t1], [q_im])
        if n <= 64:
            qr = q_re.ap.unsqueeze(2).broadcast_to([128, 16, n])
            qi = q_im.ap.unsqueeze(2).broadcast_to([128, 16, n])
            sc_, ss_ = cE.ap[:, :, 0:n], sE.ap[:, :, 0:n]
            self.tt(V, u[0].ap[:, :, 0:n], sc_, qr, ALU.mult, [cE, q_re], [u[0]])
            self.tt(V, u[1].ap[:, :, 0:n], ss_, qi, ALU.mult, [sE, q_im], [u[1]])
            self.tt(V, u[2].ap[:, :, 0:n], sc_, qi, ALU.mult, [cE, q_im], [u[2]])
            self.tt(V, u[3].ap[:, :, 0:n], ss_, qr, ALU.mult, [sE, q_re], [u[3]])
            self.tt(V, cE.ap[:, :, n:2 * n], u[0].ap[:, :, 0:n], u[1].ap[:, :, 0:n], ALU.subtract, [u[0], u[1]], [cE])
            self.tt(V, sE.ap[:, :, n:2 * n], u[2].ap[:, :, 0:n], u[3].ap[:, :, 0:n], ALU.add, [u[2], u[3]], [sE])
        p_re, p_im = q_re, q_im
        n *= 2
    c128, s128 = T16("c128r"), T16("s128r")
    self.tt(V, c128.ap, p_re.ap, mag.ap, ALU.mult, [p_re, mag], [c128])
    self.tt(V, s128.ap, p_im.ap, mag.ap, ALU.mult, [p_im, mag], [s128])
    irho = T16("irho")
    self.act(irho.ap, tmp.ap, AF.Exp, [tmp], [irho], scale=-1.0)
    rho_t = ar.alloc("rho_t", [16, 128])
    self.cp(V, rho_t.ap, mag.ap.unsqueeze(2).broadcast_to([128, 16, 128]), [mag], [rho_t])
    self.op(V, partial(nc.vector.memset, rho_t.ap[:, :, 0:1], 0.0), [], [rho_t])

    BT = [ar.alloc("BT%d" % i, [16, 128], BF16) for i in range(2)]
    CT = [ar.alloc("CT%d" % i, [16, 128], BF16) for i in range(2)]
    for i in range(2):
        for hf in range(2):
            st = self.wstage.next()
            self.dma(st.ap, self.s5_BT[i][:, hf * 1024:(hf + 1) * 1024], writes=[st])
            self.cp("act", BT[i].ap.rearrange("p a b -> p (a b)")[:, hf * 1024:(hf + 1) * 1024], st.ap, [st], [BT[i]])
    cst = [self.wstage.next() for _ in range(4)]
    for i in range(2):
        for hf in range(2):
            self.dma(cst[i * 2 + hf].ap, self.s5_CT[i][:, hf * 1024:(hf + 1) * 1024], writes=[cst[i * 2 + hf]])
    for hf in range(2):
        js = slice(hf * 8, (hf + 1) * 8)
        cre = cst[hf].ap.rearrange("p (a b) -> p a b", a=8)
        cim = cst[2 + hf].ap.rearrange("p (a b) -> p a b", a=8)
        fr = f_re.ap[:, js].unsqueeze(2).broadcast_to([128, 8, 128])
        fi = f_im.ap[:, js].unsqueeze(2).broadcast_to([128, 8, 128])
        w0, w1 = u2[0].ap[:, 0:8, :], u2[1].ap[:, 0:8, :]
        self.tt(V, w0, cre, fr, ALU.mult, [cst[hf], f_re] + u, [u2[0]])
        self.tt(V, w1, cim, fi, ALU.mult, [cst[2 + hf], f_im] + u, [u2[1]])
        self.tt(V, CT[0].ap[:, js, :], w0, w1, ALU.subtract, [u2[0], u2[1]], [CT[0]])
        self.tt(V, w0, cre, fi, ALU.mult, [cst[hf], f_im], [u2[0]])
        self.tt(V, w1, cim, fr, ALU.mult, [cst[2 + hf], f_re], [u2[1]])
        self.tt(V, CT[1].ap[:, js, :], w0, w1, ALU.add, [u2[0], u2[1]], [CT[1]])
    self.ts(V, CT[1].ap, CT[1].ap, -1.0, None, ALU.mult, None, [CT[1]], [CT[1]])
    CTn0 = ar.alloc("CTn0", [16, 128], BF16)
    self.ts(V, CTn0.ap, CT[0].ap, -1.0, None, ALU.mult, None, [CT[0]], [CTn0])
    gluw = ar.alloc("gluw", [4, 512], BF16)
    self.load_w(gluw, self.glu_w, 4, 512)

    tA = Rot([ar.alloc("tA%d" % i, [512]) for i in range(2)])
    tB = Rot([ar.alloc("tB%d" % i, [512]) for i in range(1)])
    ygbs = Rot([ar.alloc("ygb%d" % i, [4, 512], BF16) for i in range(1)])
    glub = CL["glu_b"][0]
    s5d = CL["s5_d"][0]

    def glu(y_t, c, n, uTc=None):
        ygb = ygbs.next()
        for ft in range(4):
            a, bq = tA.next(), tB.next()
            yv = y_t.ap[:, ft, 0:n]
            if uTc is not None:
                self.stt(yv, uTc.ap[:, ft, 0:n], self.consts.ap[:, s5d + ft: s5d + ft + 1], yv, ALU.mult, ALU.add,
                         [uTc, self.consts, y_t], [y_t])
            self.act(a.ap[:, 0:n], yv, AF.Square, [y_t], [a])
            self.ts("pool", bq.ap[:, 0:n], a.ap[:, 0:n], 0.044715, 1.0, ALU.mult, ALU.add, [a], [bq])
            self.tt("pool", bq.ap[:, 0:n], bq.ap[:, 0:n], yv, ALU.mult, [bq, y_t], [bq])
            self.act(a.ap[:, 0:n], bq.ap[:, 0:n], AF.Sigmoid, [bq], [a], scale=1.5957691216057308)
            self.tt("dve", ygb.ap[:, ft, 0:n], a.ap[:, 0:n], yv, ALU.mult, [a, y_t], [ygb])
        for mo in range(4):
            b = self.pbank()
            for k in range(4):
                self.mm(b.ap[:, 0:n], gluw.ap[:, k, mo * 128:(mo + 1) * 128], ygb.ap[:, k, 0:n], k == 0, k == 3,
                        [gluw, ygb], [b])
            a = tA.next()
            self.act(a.ap[:, 0:n], b.ap[:, 0:n], AF.Sigmoid, [b, self.consts], [a],
                     bias=self.consts.ap[:, glub + mo: glub + mo + 1])
            self.tt("pool", yTs[c].ap[:, mo, 0:n], ygb.ap[:, mo, 0:n], a.ap[:, 0:n], ALU.mult, [ygb, a], [yTs[c]])

    ysb = Rot(ysb_t)
    mloop = ar.mark()
    tq = [ar.alloc("tq%d" % i, [512]) for i in range(2)]
    tq = tq + tq
    bt = [ar.alloc("bt%d" % i, [512]) for i in range(2)]
    sts = Rot([(ar.alloc("str%d" % i, [512]), ar.alloc("sti%d" % i, [512])) for i in range(2)])
    sbs = Rot([tuple(ar.alloc("spr%d_%d" % (i, q), [512], BF16) for q in range(4)) for i in range(2)])
    pc = [ar.alloc("pc%d" % i, [4]) for i in range(4)]
    pcv = [ar.alloc("pcv%d" % i, [4]) for i in range(2)]

    carry = [Tile("s5c%d" % ft, self.s5carry.ap[:, :, ft * 4:(ft + 1) * 4]) for ft in range(4)]
    flat = lambda t, ft: t.ap[:, ft * 4:(ft + 1) * 4, :].rearrange("p a b -> p (a b)")
    y_t = None
    pending = None
    for cc in range(SEGT // 128):
        c, off = cc // 4, (cc % 4) * 128
        if cc % 4 == 0:
            y_t = ysb.next()
        for ft in range(4):
            bre, bim = self.pbank(), self.pbank()
            for jl in range(4):
                j = ft * 4 + jl
                self.mm(bre.ap[:, jl * 128:(jl + 1) * 128], BT[0].ap[:, j, :], uT[c].ap[:, ft, off:off + 128], True, True,
                        [BT[0], uT[c]], [bre])
                self.mm(bim.ap[:, jl * 128:(jl + 1) * 128], BT[1].ap[:, j, :], uT[c].ap[:, ft, off:off + 128], True, True,
                        [BT[1], uT[c]], [bim])
            ce, se = flat(cE, ft), flat(sE, ft)
            self.tt(V, bt[0].ap, bre.ap, ce, ALU.mult, [bre, cE], [bt[0]])
            self.tt(V, bt[1].ap, bim.ap, ce, ALU.mult, [bim, cE], [bt[1]])
            self.tt(V, tq[0].ap, bim.ap, se, ALU.mult, [bim, sE], [tq[0]])
            self.tt(V, tq[1].ap, bre.ap, se, ALU.mult, [bre, sE], [tq[1]])
            self.tt(V, bt[0].ap, bt[0].ap, tq[0].ap, ALU.add, [bt[0], tq[0]], [bt[0]])
            self.tt("pool", bt[1].ap, bt[1].ap, tq[1].ap, ALU.subtract, [bt[1], tq[1]], [bt[1]])
            st_r, st_i = sts.next()
            for ri in range(2):
                b0 = bt[ri].ap.rearrange("p (a b) -> p a b", a=4)[:, :, 0]
                self.tt(V, b0, b0, carry[ft].ap[:, ri, :], ALU.add, [bt[ri], carry[ft]], [bt[ri]])
            for ri, stt_ in ((0, st_r), (1, st_i)):
                self.op(V, partial(nc.vector.tensor_tensor_scan, out=stt_.ap, data0=flat(rho_t, ft), data1=bt[ri].ap,
                                   initial=0.0, op0=ALU.mult, op1=ALU.add), [rho_t, bt[ri]], [stt_])
            sr = st_r.ap.rearrange("p (a b) -> p a b", a=4)[:, :, 127]
            si = st_i.ap.rearrange("p (a b) -> p a b", a=4)[:, :, 127]
            cc_, ss_ = c128.ap[:, ft * 4:(ft + 1) * 4], s128.ap[:, ft * 4:(ft + 1) * 4]
            P = "pool"
            self.tt(P, pc[0].ap, sr, cc_, ALU.mult, [st_r, c128], [pc[0]])
            self.tt(P, pc[1].ap, si, ss_, ALU.mult, [st_i, s128], [pc[1]])
            self.tt(P, pc[2].ap, sr, ss_, ALU.mult, [st_r, s128], [pc[2]])
            self.tt(P, pc[3].ap, si, cc_, ALU.mult, [st_i, c128], [pc[3]])
            self.tt(P, carry[ft].ap[:, 0, :], pc[0].ap, pc[1].ap, ALU.subtract, [pc[0], pc[1]], [carry[ft]])
            self.tt(P, carry[ft].ap[:, 1, :], pc[2].ap, pc[3].ap, ALU.add, [pc[2], pc[3]], [carry[ft]])
            p1, p2, p3, p4 = sbs.next()
            self.tt(V, p1.ap, st_r.ap, ce, ALU.mult, [st_r, cE], [p1])
            self.tt(V, p2.ap, st_i.ap, se, ALU.mult, [st_i, sE], [p2])
            self.tt(P, p3.ap, st_r.ap, se, ALU.mult, [st_r, sE], [p3])
            self.tt(P, p4.ap, st_i.ap, ce, ALU.mult, [st_i, cE], [p4])
            def fin(ft=ft, ps_=(p1, p2, p3, p4), y_t=y_t, off=off, c=c, last=(cc % 4 == 3 and ft == 3)):
                yb = self.pbank()
                ws_ = (CT[0], CTn0, CT[1], CT[1])
                for jl in range(4):
                    j = ft * 4 + jl
                    for q in range(4):
                        self.mm(yb.ap[:, 0:128], ws_[q].ap[:, j, :], ps_[q].ap[:, jl * 128:(jl + 1) * 128],
                                jl == 0 and q == 0, jl == 3 and q == 3, [ws_[q], ps_[q]], [yb])
                self.cp("act", y_t.ap[:, ft, off:off + 128], yb.ap[:, 0:128], [yb] + u + u2, [y_t])
                if last:
                    glu(y_t, c, 512, uT[c])
            if pending is not None:
                pending()
            pending = fin
    if pending is not None:
        pending()

    if seg == NSEG - 1:
        fin = [T16("fin_re"), T16("fin_im")]
        g_re, g_im = T16("g_re"), T16("g_im")
        crt, cit = T16("crt"), T16("cit")
        self.tt(V, crt.ap, self.s5carry.ap[:, 0, :], irho.ap, ALU.mult, carry + [irho], [crt])
        self.tt(V, cit.ap, self.s5carry.ap[:, 1, :], irho.ap, ALU.mult, carry + [irho], [cit])
        cr = crt.ap
        ci = cit.ap
        carry = carry + [crt, cit]
        self.tt(V, t1.ap, cr, cosA.ap, ALU.mult, carry + [cosA], [t1])
        self.tt(V, t2.ap, ci, sinA.ap, ALU.mult, carry + [sinA], [t2])
        self.tt(V, g_re.ap, t1.ap, t2.ap, ALU.add, [t1, t2], [g_re])
        self.tt(V, t1.ap, ci, cosA.ap, ALU.mult, carry + [cosA], [t1])
        self.tt(V, t2.ap, cr, sinA.ap, ALU.mult, carry + [sinA], [t2])
        self.tt(V, g_im.ap, t1.ap, t2.ap, ALU.subtract, [t1, t2], [g_im])
        self.tt(V, t1.ap, g_re.ap, f_re.ap, ALU.mult, [g_re, f_re], [t1])
        self.tt(V, t2.ap, g_im.ap, f_im.ap, ALU.mult, [g_im, f_im], [t2])
        self.tt(V, fin[0].ap, t1.ap, t2.ap, ALU.subtract, [t1, t2], [fin[0]])
        self.tt(V, t1.ap, g_re.ap, f_im.ap, ALU.mult, [g_re, f_im], [t1])
        self.tt(V, t2.ap, g_im.ap, f_re.ap, ALU.mult, [g_im, f_re], [t2])
        self.tt(V, fin[1].ap, t1.ap, t2.ap, ALU.add, [t1, t2], [fin[1]])
        fo = ar.alloc("fo", [2, 128])
        for ri, dst in ((0, self.s5re_p), (1, self.s5im_p)):
            b = self.pbank()
            self.tr(b.ap[0:16, 0:128], fin[ri].ap, ident, [fin[ri], self.consts], [b])
            self.cp("act", fo.ap[0:16, ri, :], b.ap[0:16, 0:128], [b], [fo])
            self.dma(dst, fo.ap[0:16, ri, :], reads=[fo])

    if seg == 0 and not self.debug.get("no_sample_mix"):
        self.sch.barrier()
        ar.release(mloop)
        self.s5_sample(uT[2], BT, CT, f_re, f_im, if_re, if_im, ab_re, ab_im, glu, t1, t2)


Prog.s5_part = _s5_part


def _s5_sample(self, uTs, BT, CT, f_re, f_im, if_re, if_im, ab_re, ab_im, glu, t1, t2):
    nc = self.nc
    ar = self.arena
    ident = self.C("ident")
    V = "dve"
    s5d = CL["s5_d"][0]
    st = [ar.alloc("sst%d" % ri, [16, 16]) for ri in range(2)]
    xs = [ar.alloc("sxs%d" % ri, [16, 16]) for ri in range(2)]
    sin_rot = Rot([ar.alloc("s5in%d" % i, [512]) for i in range(1)])
    for ri, src in ((0, self.s5re0), (1, self.s5im0)):
        b = self.pbank()
        for q in range(4):
            t = sin_rot.next()
            self.dma(t.ap[0:NSS, :], src[:, q * 512:(q + 1) * 512], writes=[t])
            for jl in range(4):
                j = q * 4 + jl
                self.tr(b.ap[:, j * 16:(j + 1) * 16], t.ap[0:NSS, jl * 128:(jl + 1) * 128], ident[0:NSS, 0:NSS],
                        [t, self.consts], [b])
        self.cp("act", st[ri].ap, b.ap[:, 0:256].rearrange("p (a b) -> p a b", a=16), [b], [st[ri]])
    v = [ar.alloc("sv%d" % i, [16, 16]) for i in range(4)]
    bc = lambda t: t.ap.unsqueeze(2).broadcast_to([128, 16, 16])

    def cmul(o_re, o_im, a_re, a_im, b_re, b_im, rd):
        self.tt(V, v[0].ap, a_re.ap, b_re, ALU.mult, [a_re] + rd, [v[0]])
        self.tt(V, v[1].ap, a_im.ap, b_im, ALU.mult, [a_im] + rd, [v[1]])
        self.tt(V, v[2].ap, a_re.ap, b_im, ALU.mult, [a_re] + rd, [v[2]])
        self.tt(V, v[3].ap, a_im.ap, b_re, ALU.mult, [a_im] + rd, [v[3]])
        self.tt(V, o_re.ap, v[0].ap, v[1].ap, ALU.subtract, [v[0], v[1]], [o_re])
        self.tt(V, o_im.ap, v[2].ap, v[3].ap, ALU.add, [v[2], v[3]], [o_im])

    cmul(xs[0], xs[1], st[0], st[1], bc(if_re), bc(if_im), [if_re, if_im])
    braw = [ar.alloc("braw%d" % ri, [16, NSMP]) for ri in range(2)]
    for ft in range(4):
        bre, bim = self.pbank(), self.pbank()
        for jl in range(4):
            j = ft * 4 + jl
            self.mm(bre.ap[:, jl * 64:(jl + 1) * 64], BT[0].ap[:, j, :], uTs.ap[:, ft, 0:NSMP], True, True, [BT[0], uTs], [bre])
            self.mm(bim.ap[:, jl * 64:(jl + 1) * 64], BT[1].ap[:, j, :], uTs.ap[:, ft, 0:NSMP], True, True, [BT[1], uTs], [bim])
        self.cp("act", braw[0].ap[:, ft * 4:(ft + 1) * 4, :], bre.ap[:, 0:256].rearrange("p (a b) -> p a b", a=4), [bre], [braw[0]])
        self.cp("act", braw[1].ap[:, ft * 4:(ft + 1) * 4, :], bim.ap[:, 0:256].rearrange("p (a b) -> p a b", a=4), [bim], [braw[1]])
    ssb = [ar.alloc("ssb%d" % ri, [16, NSMP], BF16) for ri in range(2)]
    nx = [ar.alloc("snx%d" % ri, [16, 16]) for ri in range(2)]
    for tau in range(4):
        cmul(nx[0], nx[1], xs[0], xs[1], bc(ab_re), bc(ab_im), [ab_re, ab_im])
        for ri in range(2):
            bv = braw[ri].ap.rearrange("p a (s t) -> p a s t", t=4)[:, :, :, tau]
            self.tt(V, xs[ri].ap, nx[ri].ap, bv, ALU.add, [nx[ri], braw[ri]], [xs[ri]])
            self.cp("act", ssb[ri].ap.rearrange("p a (s t) -> p a s t", t=4)[:, :, :, tau], xs[ri].ap, [xs[ri]], [ssb[ri]])
    y_s = ar.alloc("y_s5s", [4, NSMP])
    for ft in range(4):
        yb = self.pbank()
        for jl in range(4):
            j = ft * 4 + jl
            self.mm(yb.ap[:, 0:NSMP], CT[0].ap[:, j, :], ssb[0].ap[:, j, :], jl == 0, False, [CT[0], ssb[0]], [yb])
            self.mm(yb.ap[:, 0:NSMP], CT[1].ap[:, j, :], ssb[1].ap[:, j, :], False, jl == 3, [CT[1], ssb[1]], [yb])
        self.stt(y_s.ap[:, ft, :], uTs.ap[:, ft, 0:NSMP], self.consts.ap[:, s5d + ft: s5d + ft + 1], yb.ap[:, 0:NSMP],
                 ALU.mult, ALU.add, [uTs, self.consts, yb], [y_s])
    glu(y_s, 2, NSMP)
    cmul(st[0], st[1], xs[0], xs[1], bc(f_re), bc(f_im), [f_re, f_im])
    so = sin_rot
    for ri, dst in ((0, self.s5re_s), (1, self.s5im_s)):
        for q in range(4):
            b = self.pbank()
            for jl in range(4):
                j = q * 4 + jl
                self.tr(b.ap[0:NSS, jl * 128:(jl + 1) * 128], st[ri].ap[:, j, :], ident, [st[ri], self.consts], [b])
            o = so.next()
            self.cp("act", o.ap[0:NSS, :], b.ap[0:NSS, :], [b], [o])
            self.dma(dst[:, q * 512:(q + 1) * 512], o.ap[0:NSS, :], reads=[o])


Prog.s5_sample = _s5_sample


def _ret_part(self, seg, yTr):
    nc = self.nc
    ar = self.arena
    ident = self.C("ident")
    chunks = self.chunks(seg)
    W = W0 if seg == 0 else SEGT
    V = "dve"
    sinT = ar.alloc("sinT", [W0])
    cosT = ar.alloc("cosT", [W0])
    mt = ar.mark()
    posf = ar.alloc("posf", [W0])
    rr = ar.alloc("rr", [W0])
    tmp = ar.alloc("rtmp", [W0])
    self.dma(posf.ap[:, 0:SEGT], self.cpos_d, writes=[posf])
    if seg > 0:
        self.ts(V, posf.ap[:, 0:SEGT], posf.ap[:, 0:SEGT], float(seg * SEGT), None, ALU.add, None, [posf], [posf])
    if seg == 0:
        posi = ar.alloc("posi", [NSS], I32)
        posff = ar.alloc("posff", [NSS])
        self.dma(posi.ap, self.pos_d, writes=[posi])
        self.cp(V, posff.ap, posi.ap, [posi], [posff])
        pv = posf.ap[:, SEGT:W0].rearrange("p (s t) -> p s t", t=4)
        for tau in range(4):
            self.ts(V, pv[:, :, tau], posff.ap, float(tau), None, ALU.add, None, [posff, posf], [posf])
    invf = self.C("inv_freq")
    self.ts(V, posf.ap[:, 0:W], posf.ap[:, 0:W], invf, None, ALU.mult, None, [posf, self.consts], [posf])
    rr_v = Tile("rr_v", rr.ap[:, 0:W])
    tmp_v = Tile("tmp_v", tmp.ap[:, 0:W])
    self.range_reduce(V, rr_v, posf.ap[:, 0:W], tmp_v, [posf], None)
    sin_v = Tile("sin_v", sinT.ap[:, 0:W])
    cos_v = Tile("cos_v", cosT.ap[:, 0:W])
    self.sincos(rr_v, sin_v, cos_v, tmp_v, sin_scale=self.C("sgn"), extra_reads=[self.consts])
    sinT, cosT = sin_v, cos_v
    self.sch.barrier()
    ar.release(mt)

    Wv = ar.alloc("Wv", [KT, 512], BF16)
    Wg = ar.alloc("Wg", [KT, 512], BF16)
    self.load_w(Wv, self.w_in_ab[:, 1536:2048], KT, 512)
    self.load_w(Wg, self.w_in_ab[:, 2048:2560], KT, 512)
    wqk = Rot([ar.alloc("wqk%d" % i, [KT, 128], BF16) for i in range(4)])
    comp = {}

    def load_sw(dst, c0):
        srcv = self.w_in_ab.rearrange("(k p) n -> p k n", p=128)
        st = self.wstage.next()
        if st.name not in comp:
            comp[st.name] = Tile(st.name + "_c")
        st2 = comp[st.name]
        sv = st.ap[:, 0:KT * 128].rearrange("p (k n) -> p k n", n=128)
        self.dma(sv[:, :, 0:64], srcv[:, :, c0 + 64:c0 + 128], writes=[st])
        self.dma(sv[:, :, 64:128], srcv[:, :, c0:c0 + 64], writes=[st2])
        self.cp("act", dst.ap, sv, [st, st2], [dst])

    ws = [512, 512, NSMP]
    qT = ar.alloc("qT", [4, 512], BF16)
    kT = ar.alloc("kT", [4, 512], BF16)
    qdT = ar.alloc("qdT", [4, 512], BF16)
    qs32 = ar.alloc("qs32", [4, NSMP])
    qd32 = ar.alloc("qd32", [4, NSMP])
    rt = Rot([ar.alloc("rt%d" % i, [512]) for i in range(2)])
    v_toks = Rot([ar.alloc("vtok%d" % i, [512], BF16) for i in range(2)])
    sgs = Rot([ar.alloc("sg%d" % i, [512]) for i in range(2)])
    scms = Rot([ar.alloc("scm%d" % i, [4, 128], BF16) for i in range(2)])
    kds = Rot([ar.alloc("kd%d" % i, [4, 128], BF16) for i in range(2)])
    ytoks = Rot([ar.alloc("ytok%d" % i, [512], BF16) for i in range(2)])
    junk = ar.alloc("junk", [512])
    sss = Rot([ar.alloc("ss%d" % i, [4]) for i in range(2)])
    maskT = self.C("maskT").rearrange("p (h i) -> p h i", h=4)
    maskS = self.C("maskS", 64).rearrange("p (h i) -> p h i", h=4)
    qdec = self.C("qdec").rearrange("p (h i) -> p h i", h=4)
    qdecS = self.C("qdecS").rearrange("p (h i) -> p h i", h=4)
    kdec = self.C("kdec")
    kdecS = self.C("kdecS", 64)
    g128 = self.C("g128")
    pSs = Rot([ar.alloc("rpS%d" % i, [4, 128]) for i in range(1)])
    identb = self.identb
    eps_ap = self.C("eps")

    def epilogue(ob, sg, rows, c, off):
        ss = sss.next()
        for h in range(4):
            self.act(junk.ap[0:rows, h * 128:(h + 1) * 128], ob.ap[0:rows, h * 128:(h + 1) * 128], AF.Square, [ob], [junk, ss],
                     accum_out=ss.ap[0:rows, h:h + 1])
        self.act(ss.ap[0:rows, :], ss.ap[0:rows, :], AF.Ln, [junk, self.consts], [ss], bias=eps_ap[0:rows, :], scale=1.0 / 128)
        self.act(ss.ap[0:rows, :], ss.ap[0:rows, :], AF.Exp, [ss], [ss], scale=-0.5)
        y_tok = ytoks.next()
        for h in range(4):
            self.stt(y_tok.ap[0:rows, h * 128:(h + 1) * 128], ob.ap[0:rows, h * 128:(h + 1) * 128], ss.ap[0:rows, h:h + 1],
                     sg.ap[0:rows, h * 128:(h + 1) * 128], ALU.mult, ALU.mult, [ob, ss, sg], [y_tok])
        yb = self.pbank()
        ybf = yb.ap.bitcast(BF16)
        for h in range(4):
            self.tr(ybf[:, h * 128:h * 128 + rows], y_tok.ap[0:rows, h * 128:(h + 1) * 128], identb.ap[0:rows, 0:rows],
                    [y_tok, identb], [yb])
        self.cp("act", yTr[c].ap[:, :, off:off + rows], ybf[:, 0:512].rearrange("p (h t) -> p h t", h=4)[:, :, 0:rows],
                [yb], [yTr[c]])

    for (c, c0, n) in chunks:
        for which, base, dst in (("q", 512, qT), ("k", 1024, kT)):
            for h in range(4):
                wn, wsw = wqk.next(), wqk.next()
                st = self.wstage.next()
                sv = st.ap[:, 0:KT * 128].rearrange("p (k n) -> p k n", n=128)
                c0w = base + h * 128
                self.dma(sv, self.w_in_ab.rearrange("(k p) n -> p k n", p=128)[:, :, c0w:c0w + 128], writes=[st])
                self.cp("act", wn.ap, sv, [st], [wn])
                self.cp("act", wsw.ap[:, :, 0:64], sv[:, :, 64:128], [st], [wsw])
                self.cp("act", wsw.ap[:, :, 64:128], sv[:, :, 0:64], [st], [wsw])
                pn, psw = self.pbank(), self.pbank()
                for k in range(KT):
                    self.mm(pn.ap[:, 0:n], wn.ap[:, k, :], self.hT[c].ap[:, k, 0:n], k == 0, k == KT - 1, [wn, self.hT[c]], [pn])
                for k in range(KT):
                    self.mm(psw.ap[:, 0:n], wsw.ap[:, k, :], self.hT[c].ap[:, k, 0:n], k == 0, k == KT - 1, [wsw, self.hT[c]], [psw])
                a, b2 = rt.next(), rt.next()
                self.tt(V, a.ap[:, 0:n], pn.ap[:, 0:n], cosT.ap[:, c0:c0 + n], ALU.mult, [pn, cosT], [a])
                self.tt(V, b2.ap[:, 0:n], psw.ap[:, 0:n], sinT.ap[:, c0:c0 + n], ALU.mult, [psw, sinT], [b2])
                if c == 2 and which == "q":
                    self.tt(V, qs32.ap[:, h, :], a.ap[:, 0:n], b2.ap[:, 0:n], ALU.add, [a, b2], [qs32])
                    self.cp("act", dst.ap[:, h, 0:n], qs32.ap[:, h, :], [qs32], [dst])
                else:
                    self.tt(V, dst.ap[:, h, 0:n], a.ap[:, 0:n], b2.ap[:, 0:n], ALU.add, [a, b2], [dst])
        if c < 2:
            for h in range(4):
                self.tt("pool", qdT.ap[:, h, :].rearrange("p (t i) -> p t i", t=4),
                        qT.ap[:, h, :].rearrange("p (t i) -> p t i", t=4),
                        qdec[:, h, :].unsqueeze(1).broadcast_to([128, 4, 128]), ALU.mult, [qT, self.consts], [qdT])
        else:
            self.tt("pool", qd32.ap, qs32.ap, qdecS, ALU.mult, [qs32, self.consts], [qd32])
        ntile = n // 128 if c < 2 else 1

        def pre(t, c=c):
            off = t * 128
            rows = 128 if c < 2 else NSMP
            vb, gb = self.pbank(), self.pbank()
            for k in range(KT):
                self.mm(vb.ap[0:rows, :], self.hT[c].ap[:, k, off:off + rows], Wv.ap[:, k, :], k == 0, k == KT - 1,
                        [self.hT[c], Wv], [vb])
            for k in range(KT):
                self.mm(gb.ap[0:rows, :], self.hT[c].ap[:, k, off:off + rows], Wg.ap[:, k, :], k == 0, k == KT - 1,
                        [self.hT[c], Wg], [gb])
            v_tok, sg = v_toks.next(), sgs.next()
            self.cp("act", v_tok.ap[0:rows, :], vb.ap[0:rows, :], [vb], [v_tok])
            self.act(sg.ap[0:rows, :], gb.ap[0:rows, :], AF.Silu, [gb], [sg])
            sb_ = self.pbank()
            for h in range(4):
                self.mm(sb_.ap[0:rows, h * 128:h * 128 + rows], kT.ap[:, h, off:off + rows], qT.ap[:, h, off:off + rows], True, True,
                        [kT, qT], [sb_])
            scm = scms.next()
            mk = maskT if c < 2 else maskS
            self.tt(V, scm.ap[0:rows, :, 0:rows], sb_.ap[0:rows, :].rearrange("p (h i) -> p h i", h=4)[:, :, 0:rows], mk,
                    ALU.mult, [sb_, self.consts], [scm])
            kb = self.pbank()
            kbf = kb.ap.bitcast(BF16)
            for h in range(4):
                self.tr(kbf[0:rows, h * 128:(h + 1) * 128], kT.ap[:, h, off:off + rows], identb.ap, [kT, identb], [kb])
            kd = kds.next()
            kdc = (kdec if c < 2 else kdecS).unsqueeze(2).broadcast_to([rows, 4, 128])
            self.tt(V, kd.ap[0:rows], kbf[0:rows, 0:512].rearrange("p (h d) -> p h d", h=4), kdc, ALU.mult,
                    [kb, self.consts], [kd])
            return v_tok, sg, scm, kd

        cur = pre(0)
        for t in range(ntile):
            off = t * 128
            rows = 128 if c < 2 else NSMP
            v_tok, sg, scm, kd = cur
            if c < 2:
                pS = pSs.next()
                self.tt("pool", pS.ap, self.Sret.ap, g128.unsqueeze(2).broadcast_to([128, 4, 128]), ALU.mult,
                        [self.Sret, self.consts], [pS])
                ob = self.pbank()
                for h in range(4):
                    self.mm(ob.ap[:, h * 128:(h + 1) * 128], scm.ap[:, h, :], v_tok.ap[:, h * 128:(h + 1) * 128], True, False,
                            [scm, v_tok], [ob])
                    self.mm(ob.ap[:, h * 128:(h + 1) * 128], qdT.ap[:, h, off:off + 128], self.Sretb.ap[:, h, :], False, True,
                            [qdT, self.Sretb], [ob])
                kvb = self.pbank()
                for h in range(4):
                    self.mm(kvb.ap[:, h * 128:(h + 1) * 128], kd.ap[:, h, :], v_tok.ap[:, h * 128:(h + 1) * 128], True, True,
                            [kd, v_tok], [kvb])
                self.tt(V, self.Sretb.ap.rearrange("p a b -> p (a b)"), pS.ap.rearrange("p a b -> p (a b)"), kvb.ap, ALU.add,
                        [pS, kvb], [self.Sretb])
                self.tt(V, self.Sret.ap.rearrange("p a b -> p (a b)"), pS.ap.rearrange("p a b -> p (a b)"), kvb.ap, ALU.add,
                        [pS, kvb], [self.Sret])
                if t + 1 < ntile:
                    cur = pre(t + 1)
                epilogue(ob, sg, 128, c, off)
            else:
                otb = [self.ps[h] for h in range(4)]
                locb = Rot([self.ps[4], self.ps[5], self.ps[6], self.ps[7]])
                for h in range(4):
                    self.mm(otb[h].ap[:, 0:NSMP], v_tok.ap[0:NSMP, h * 128:(h + 1) * 128], scm.ap[0:NSMP, h, 0:NSMP], True, False,
                            [v_tok, scm], [otb[h]])
                s0rot = Rot([ar.alloc("s0r%d" % i, [4, 128]) for i in range(3)])
                sorot = Rot([ar.alloc("sor%d" % i, [4, 128]) for i in range(2)])
                vms = Rot([ar.alloc("vm%d" % i, [512], BF16) for i in range(2)])
                smo = CL["seqmask"][0]
                nxtS0 = s0rot.next()
                self.dma(nxtS0.ap, self.ret0[0].rearrange("h p d -> p h d"), writes=[nxtS0])
                for s in range(NSS):
                    S0s = nxtS0
                    if s + 1 < NSS:
                        nxtS0 = s0rot.next()
                        self.dma(nxtS0.ap, self.ret0[s + 1].rearrange("h p d -> p h d"), writes=[nxtS0])
                    for h in range(4):
                        self.mm(otb[h].ap[:, 4 * s:4 * s + 4], S0s.ap[:, h, :], qd32.ap[:, h, 4 * s:4 * s + 4], False, s == NSS - 1,
                                [S0s, qd32], [otb[h]])
                    vm = vms.next()
                    self.ts(V, vm.ap[0:NSMP, :], v_tok.ap[0:NSMP, :], self.consts.ap[0:NSMP, smo + s:smo + s + 1], None,
                            ALU.mult, None, [v_tok, self.consts], [vm])
                    kvb = locb.next()
                    for h in range(4):
                        self.mm(kvb.ap[:, h * 128:(h + 1) * 128], kd.ap[0:NSMP, h, :], vm.ap[0:NSMP, h * 128:(h + 1) * 128], True, True,
                                [kd, vm], [kvb])
                    So = sorot.next()
                    self.tt("pool", So.ap, S0s.ap, self.C("g4").unsqueeze(2).broadcast_to([128, 4, 128]), ALU.mult,
                            [S0s, self.consts], [So])
                    sov = So.ap.rearrange("p a b -> p (a b)")
                    self.tt(V, sov, sov, kvb.ap, ALU.add, [So, kvb], [So])
                    self.dma(self.ret_s[s].rearrange("h p d -> p h d"), So.ap, reads=[So])
                oT32 = ar.alloc("oT32", [4, NSMP])
                for h in range(4):
                    self.cp("act", oT32.ap[:, h, :], otb[h].ap[:, 0:NSMP], [otb[h]], [oT32])
                ob = locb.next()
                self._pb = 0
                for h in range(4):
                    self.tr(ob.ap[0:NSMP, h * 128:(h + 1) * 128], oT32.ap[:, h, :], ident, [oT32, self.consts], [ob])
                epilogue(ob, sg, NSMP, c, 0)
    if seg == NSEG - 1:
        self.dma(self.ret_p.rearrange("h p d -> p h d"), self.Sret.ap, reads=[self.Sret])


Prog.ret_part = _ret_part


def _mixer_c(self, seg):
    nc = self.nc
    ar = self.arena
    m0 = ar.mark()
    chunks = self.chunks(seg)
    identb = self.identb
    V = "dve"
    ws = [512, 512, NSMP]
    yT = [ar.alloc("yTc%d" % c, [KT, ws[c]], BF16) for c in range(3)]
    m1 = ar.mark()
    self.hT = self.alloc_hT()
    self.rmsnorm(seg, 2, self.hT)
    self.sch.barrier()
    sqt = self.sq.tiles[0]
    rst_ = self.rs.tiles[0]
    xtra = sqt.ap.bitcast(F32).rearrange("p a b -> p (a b)")
    lgo = CL["hg_lg"][0]
    dlg = ar.alloc("dlg", [8])
    oml = ar.alloc("oml", [8])
    self.tt(V, dlg.ap, self.consts.ap[:, lgo:lgo + 8], self.consts.ap[:, lgo + 8:lgo + 16], ALU.subtract, [self.consts], [dlg])
    self.act(oml.ap, dlg.ap, AF.Sigmoid, [dlg], [oml])
    lnoml = ar.alloc("lnoml", [8])
    self.act(lnoml.ap, oml.ap, AF.Ln, [oml], [lnoml])
    one_ap = self.C("one")
    eps_ap = self.C("eps")
    nw_ap = self.C("hg_nw")
    rst = ar.alloc("rst", [512])
    rstS = ar.alloc("rstS", [NSMP])
    self.op(V, partial(nc.vector.memset, rst.ap, 1.0), [], [rst])
    self.op(V, partial(nc.vector.memset, rst.ap.rearrange("p (a b) -> p a b", b=64)[:, :, 0:1], 0.0), [], [rst])
    self.op(V, partial(nc.vector.memset, rstS.ap, 1.0), [], [rstS])
    self.op(V, partial(nc.vector.memset, rstS.ap.rearrange("p (a b) -> p a b", b=4)[:, :, 0:1], 0.0), [], [rstS])
    wts = Rot([ar.alloc("wc%d" % i, [KT, 128], BF16) for i in range(8)])
    qtT = ar.alloc("qtT", [8, 512], BF16)
    ktT = ar.alloc("ktT", [8, 512], BF16)
    kkT = ar.alloc("kkT", [8, 512], BF16)
    vT = ar.alloc("vT", [8, 512], BF16)
    sgT = ar.alloc("sgT", [8, 512], BF16)
    ebls = Rot([ar.alloc("ebl%d" % i, [8, 16]) for i in range(2)])
    qs32 = ar.alloc("qs32c", [8, NSMP])
    blkA = ar.alloc("htaB", [2560])
    setA = [Tile("hta%d" % i, blkA.ap[:, i * 512:(i + 1) * 512]) for i in range(5)]
    setB = [Tile("htb%d" % i, xtra[:, i * 512:(i + 1) * 512]) for i in range(4)] + [Tile("htb4", rst_.ap)]
    tsets = [setA, setB]
    vtoks = Rot([ar.alloc("hvt%d" % i, [1024], BF16) for i in range(2)])
    kktoks = Rot([ar.alloc("hkt%d" % i, [1024], BF16) for i in range(2)])
    scms = Rot([ar.alloc("hsc%d" % i, [8, 64], BF16) for i in range(2)])
    pSs = Rot([ar.alloc("hpS%d" % i, [8, 128]) for i in range(1)])
    sqb = ar.alloc("hsq", [512], BF16)
    rstd = ar.alloc("hrstd", [512])
    otmp = ar.alloc("hotmp", [512])
    tri64 = self.C("triBD")[0:64, 0:64]
    triS = self.C("triS", 64)
    ident = self.C("ident")

    def proj(w, c, n):
        b = self.pbank()
        for k in range(KT):
            self.mm(b.ap[:, 0:n], w.ap[:, k, :], self.hT[c].ap[:, k, 0:n], k == 0, k == KT - 1, [w, self.hT[c]], [b])
        return b

    def load_head(h):
        wq, wf, wv, wg = wts.next(), wts.next(), wts.next(), wts.next()
        self.load_w(wf, self.w_in_c[:, 1024 + h * 128:1024 + (h + 1) * 128], KT, 128, cast_eng="dve")
        self.load_w(wv, self.w_in_c[:, 2048 + h * 128:2048 + (h + 1) * 128], KT, 128, cast_eng="dve")
        self.load_w(wg, self.w_in_c[:, 3072 + h * 128:3072 + (h + 1) * 128], KT, 128, cast_eng="dve")
        self.load_w(wq, self.w_in_c[:, h * 128:(h + 1) * 128], KT, 128, cast_eng="dve")
        return wq, wf, wv, wg

    def epi_rest(o_ap, o_tiles, c, o64):
        sb_ = self.pbank()
        self.mm(sb_.ap, self.onesb.ap, sqb.ap, True, True, [self.onesb, sqb], [sb_])
        self.act(rstd.ap, sb_.ap, AF.Ln, [sb_, self.consts], [rstd], bias=eps_ap, scale=1.0 / 128)
        self.act(rstd.ap, rstd.ap, AF.Exp, [rstd], [rstd], scale=-0.5)
        self.tt(V, otmp.ap, o_ap, rstd.ap, ALU.mult, o_tiles + [rstd], [otmp])
        self.tt("pool", yT[c].ap[:, :, o64:o64 + 64], otmp.ap.rearrange("p (h i) -> p h i", h=8), sgT.ap[:, :, o64:o64 + 64],
                ALU.mult, [otmp, sgT], [yT[c]])

    nxt_w = load_head(0)
    work = [(c, c0, n, h) for (c, c0, n) in chunks for h in range(8)]
    for wi, (c, c0, n, h) in enumerate(work):
        sample = (c == 2)
        blk = 64 if not sample else 4
        nb = n // blk
        if h == 0:
            ebl = ebls.next()
        wq, wf, wv, wg = nxt_w
        if wi + 1 < len(work):
            nxt_w = load_head(work[wi + 1][3])
        kf, lf, bT, enb, kt = tsets[wi % 2]
        pf = proj(wf, c, n)
        self.act(kf.ap[:, 0:n], pf.ap[:, 0:n], AF.Exp, [pf], [kf])
        pv = proj(wv, c, n)
        pg = proj(wg, c, n)
        self.act(kt.ap[:, 0:n], pg.ap[:, 0:n], AF.Exp, [pg], [kt], scale=-1.0)
        self.act(lf.ap[:, 0:n], kf.ap[:, 0:n], AF.Ln, [kf, self.consts], [lf], bias=one_ap)
        self.act(kt.ap[:, 0:n], kt.ap[:, 0:n], AF.Ln, [kt, self.consts], [kt], bias=one_ap)
        self.act(kf.ap[:, 0:n], lf.ap[:, 0:n], AF.Exp, [lf, lnoml], [kf], bias=lnoml.ap[:, h:h + 1], scale=-1.0)
        self.act(kt.ap[:, 0:n], kt.ap[:, 0:n], AF.Exp, [kt], [kt], scale=-1.0)
        self.cp("act", vT.ap[:, h, 0:n], pv.ap[:, 0:n], [pv], [vT])
        self.act(lf.ap[:, 0:n], kf.ap[:, 0:n], AF.Ln, [kf, self.consts], [lf], bias=one_ap, scale=-1.0)
        self.stt(sgT.ap[:, h, 0:n], pg.ap[:, 0:n], nw_ap, kt.ap[:, 0:n], ALU.mult, ALU.mult, [pg, self.consts, kt], [sgT])
        rs_ap = rst.ap[:, 0:n] if not sample else rstS.ap[:, 0:n]
        self.op(V, partial(nc.vector.tensor_tensor_scan, out=bT.ap[:, 0:n], data0=rs_ap, data1=lf.ap[:, 0:n], initial=0.0,
                           op0=ALU.mult, op1=ALU.add), [rst, rstS, lf], [bT])
        pq = proj(wq, c, n)
        self.act(lf.ap[:, 0:n], bT.ap[:, 0:n], AF.Exp, [bT], [lf])
        self.act(enb.ap[:, 0:n], bT.ap[:, 0:n], AF.Exp, [bT], [enb], scale=-1.0)
        eb = lf
        ebv = eb.ap[:, 0:n].rearrange("p (a b) -> p a b", b=blk)[:, :, blk - 1]
        self.cp("pool", ebl.ap[:, h, 0:nb], ebv, [eb], [ebl])
        if sample:
            self.tt(V, qs32.ap[:, h, :], pq.ap[:, 0:n], eb.ap[:, 0:n], ALU.mult, [pq, eb], [qs32])
            self.cp("act", qtT.ap[:, h, 0:n], qs32.ap[:, h, :], [qs32], [qtT])
        else:
            self.tt(V, qtT.ap[:, h, 0:n], pq.ap[:, 0:n], eb.ap[:, 0:n], ALU.mult, [pq, eb], [qtT])
        self.tt(V, kt.ap[:, 0:n], kf.ap[:, 0:n], enb.ap[:, 0:n], ALU.mult, [kf, enb], [kt])
        self.cp("act", ktT.ap[:, h, 0:n], kt.ap[:, 0:n], [kt], [ktT])
        self.tt(V, kkT.ap[:, h, 0:n].rearrange("p (a b) -> p a b", b=blk), kt.ap[:, 0:n].rearrange("p (a b) -> p a b", b=blk),
                ebl.ap[:, h, 0:nb].unsqueeze(2).broadcast_to([128, nb, blk]), ALU.mult, [kt, ebl], [kkT])
        if h < 7:
            continue
        nt = n // 64 if not sample else 1

        def pre(t):
            o64 = t * 64
            vb, kb = self.pbank(), self.pbank()
            vbf, kbf = vb.ap.bitcast(BF16), kb.ap.bitcast(BF16)
            for hh in range(8):
                self.tr(vbf[0:64, hh * 128:(hh + 1) * 128], vT.ap[:, hh, o64:o64 + 64], identb.ap, [vT, identb], [vb])
            for hh in range(8):
                self.tr(kbf[0:64, hh * 128:(hh + 1) * 128], kkT.ap[:, hh, o64:o64 + 64], identb.ap, [kkT, identb], [kb])
            v_tok, kk_tok = vtoks.next(), kktoks.next()
            self.cp("act", v_tok.ap[0:64, :], vbf[0:64, :], [vb], [v_tok])
            self.cp(V, kk_tok.ap[0:64, :], kbf[0:64, :], [kb], [kk_tok])
            sb_ = self.pbank()
            for hh in range(8):
                self.mm(sb_.ap[0:64, hh * 64:(hh + 1) * 64], ktT.ap[:, hh, o64:o64 + 64], qtT.ap[:, hh, o64:o64 + 64], True, True,
                        [ktT, qtT], [sb_])
            scm = scms.next()
            mk = (triS if sample else tri64).unsqueeze(1).broadcast_to([64, 8, 64])
            self.tt(V, scm.ap[0:64], sb_.ap[0:64, :].rearrange("p (h i) -> p h i", h=8), mk, ALU.mult, [sb_, self.consts], [scm])
            return v_tok, kk_tok, scm

        if not sample:
            cur = pre(0)
            for t in range(nt):
                o64 = t * 64
                v_tok, kk_tok, scm = cur
                pS = pSs.next()
                self.tt("pool", pS.ap, self.Shg.ap, ebl.ap[:, :, t:t + 1].broadcast_to([128, 8, 128]), ALU.mult,
                        [self.Shg, ebl], [pS])
                ob = self.pbank()
                for hh in range(8):
                    self.mm(ob.ap[:, hh * 64:(hh + 1) * 64], v_tok.ap[0:64, hh * 128:(hh + 1) * 128], scm.ap[0:64, hh, :], True, False,
                            [v_tok, scm], [ob])
                    self.mm(ob.ap[:, hh * 64:(hh + 1) * 64], self.Shgb.ap[:, hh, :], qtT.ap[:, hh, o64:o64 + 64], False, True,
                            [self.Shgb, qtT], [ob])
                self.act(sqb.ap, ob.ap, AF.Square, [ob], [sqb])
                for half in range(2):
                    kvb = self.pbank()
                    for q in range(4):
                        hh = half * 4 + q
                        self.mm(kvb.ap[:, q * 128:(q + 1) * 128], kk_tok.ap[0:64, hh * 128:(hh + 1) * 128],
                                v_tok.ap[0:64, hh * 128:(hh + 1) * 128], True, True, [kk_tok, v_tok], [kvb])
                    psv = pS.ap[:, half * 4:(half + 1) * 4, :].rearrange("p a b -> p (a b)")
                    self.tt(V, self.Shgb.ap[:, half * 4:(half + 1) * 4, :].rearrange("p a b -> p (a b)"), psv, kvb.ap, ALU.add,
                            [pS, kvb], [self.Shgb])
                    self.tt(V, self.Shg.ap[:, half * 4:(half + 1) * 4, :].rearrange("p a b -> p (a b)"), psv, kvb.ap, ALU.add,
                            [pS, kvb], [self.Shg])
                if t + 1 < nt:
                    cur = pre(t + 1)
                epi_rest(ob.ap, [ob], c, o64)
        else:
            self.sch.barrier()
            big = Tile("hbig", None)
            v_tok, kk_tok, scm = pre(0)
            ob = self.ps[0]
            ib = self.ps[1]
            locb = Rot([self.ps[2], self.ps[3], self.ps[4], self.ps[5], self.ps[6], self.ps[7]])
            for hh in range(8):
                self.mm(ob.ap[:, hh * 64:(hh + 1) * 64], v_tok.ap[0:64, hh * 128:(hh + 1) * 128], scm.ap[0:64, hh, :], True, True,
                        [v_tok, scm], [ob])
            hfree = self.hT[0].ap.bitcast(F32).rearrange("p a b -> p (a b)")
            s0rot = Rot([Tile("hs0r0", blkA.ap[:, 1024:2048].rearrange("p (a b) -> p a b", a=8)),
                         Tile("hs0r1", hfree[:, 0:1024].rearrange("p (a b) -> p a b", a=8))])
            sorot = Rot([Tile("hsor0", xtra[:, 0:1024].rearrange("p (a b) -> p a b", a=8)),
                         Tile("hsor1", hfree[:, 1024:2048].rearrange("p (a b) -> p a b", a=8))])
            vms = Rot([Tile("hvm0", xtra[:, 1024:1536].bitcast(BF16))])
            smo = CL["seqmask"][0]
            nxtS0 = s0rot.next()
            self.dma(nxtS0.ap, self.hg0[0].rearrange("h p d -> p h d"), writes=[nxtS0])
            for s_ in range(NSS):
                S0s = nxtS0
                if s_ + 1 < NSS:
                    nxtS0 = s0rot.next()
                    self.dma(nxtS0.ap, self.hg0[s_ + 1].rearrange("h p d -> p h d"), writes=[nxtS0])
                for hh in range(8):
                    self.mm(ib.ap[:, hh * 64 + 4 * s_:hh * 64 + 4 * s_ + 4], S0s.ap[:, hh, :], qs32.ap[:, hh, 4 * s_:4 * s_ + 4], True, True,
                            [S0s, qs32], [ib])
                vm = vms.next()
                self.ts(V, vm.ap[0:64, :], v_tok.ap[0:64, :], self.consts.ap[0:64, smo + s_:smo + s_ + 1], None, ALU.mult, None,
                        [v_tok, self.consts], [vm])
                So = sorot.next()
                self.tt("pool", So.ap, S0s.ap, ebl.ap[:, :, s_:s_ + 1].broadcast_to([128, 8, 128]), ALU.mult, [S0s, ebl], [So])
                for half in range(2):
                    kvb = locb.next()
                    for q in range(4):
                        hh = half * 4 + q
                        self.mm(kvb.ap[:, q * 128:(q + 1) * 128], kk_tok.ap[0:64, hh * 128:(hh + 1) * 128],
                                vm.ap[0:64, hh * 128:(hh + 1) * 128], True, True, [kk_tok, vm], [kvb])
                    sov = So.ap[:, half * 4:(half + 1) * 4, :].rearrange("p a b -> p (a b)")
                    self.tt(V, sov, sov, kvb.ap, ALU.add, [So, kvb], [So])
                self.dma(self.hg_s[s_].rearrange("h p d -> p h d"), So.ap, reads=[So])
            oi = setA[0]
            self.cp("act", oi.ap, ib.ap, [ib], [oi])
            osum = setA[1]
            self.tt(V, osum.ap, ob.ap, oi.ap, ALU.add, [ob, oi], [osum])
            self._pb = 0
            self.act(sqb.ap, osum.ap, AF.Square, [osum], [sqb])
            epi_rest(osum.ap, [osum], c, 0)
    if seg == NSEG - 1:
        self.dma(self.hg_p.rearrange("h p d -> p h d"), self.Shg.ap, reads=[self.Shg])
    self.sch.barrier()
    ar.release(m1)
    wout = ar.alloc("woutc", [KT, D], BF16)
    self.load_w(wout, self.w_out_c, KT, D)
    for (c, c0, n) in chunks:
        for mo in range(KT):
            b = self.pbank()
            for k in range(KT):
                self.mm(b.ap[:, 0:n], wout.ap[:, k, mo * 128:(mo + 1) * 128], yT[c].ap[:, k, 0:n], k == 0, k == KT - 1,
                        [wout, yT[c]], [b])
            xv = self.xcols(c, mo, mo + 1)[:, 0, :]
            self.tt("dve", xv, xv, b.ap[:, 0:n], ALU.add, [self.xT[c], b], [self.xT[c]])
    self.sch.barrier()
    ar.release(m0)


Prog.mixer_c = _mixer_c
```

```python
import numpy as np
import concourse.bass as bass
import concourse.mybir as mybir
from concourse.bass_utils import run_bass_kernel_spmd

F32 = mybir.dt.float32
BF16 = mybir.dt.bfloat16
I32 = mybir.dt.int32
AF = mybir.ActivationFunctionType
ALU = mybir.AluOpType

NCORES = 8
D = 1024
KT = 8
SEQ = 2048
NSEG = 2
SEGT = SEQ // NSEG
NSMP = 64
NSS = 16
W0 = SEGT + NSMP
DFF = 2816
FT = DFF // 128
EPS = 1e-6

ENGS = ["pe", "act", "dve", "pool", "sp"]
NDSEM = 24


class Tile:
    __slots__ = ("name", "ap", "w", "r", "persist")

    def __init__(self, name, ap=None):
        self.name = name
        self.ap = ap
        self.w = None
        self.r = {}
        self.persist = False


class Sched:
    def __init__(self, nc):
        self.nc = nc
        self.ops = {e: [] for e in ENGS}
        self.serial = {e: 0 for e in ENGS}
        self.nvc = len(ENGS) + NDSEM
        self.vc = {e: [0] * self.nvc for e in ENGS}
        self.snap = {e: [None] for e in ENGS}
        self.pe_inc = set()
        self.dsem_val = [0] * NDSEM
        self.dsem_next = 0
        self.eidx = {e: i for i, e in enumerate(ENGS)}
        self.out_dma = {}
        self.sp_barrier = None

    def _need(self, eng, dep, waits, raw=True):
        vc = self.vc[eng]
        if dep[0] == "e":
            _, e2, s2 = dep
            if e2 == eng and not raw and eng == "pe":
                return
            i2 = self.eidx[e2]
            if vc[i2] >= s2:
                return
            waits.append(dep)
            if e2 == "pe":
                self.pe_inc.add(s2)
            sn = self.snap[e2][s2]
            for i in range(self.nvc):
                if sn[i] > vc[i]:
                    vc[i] = sn[i]
            if vc[i2] < s2:
                vc[i2] = s2
        else:
            _, k, v = dep
            i2 = len(ENGS) + k
            if vc[i2] >= v:
                return
            waits.append(dep)
            vc[i2] = v

    def op(self, eng, fn, reads=(), writes=()):
        waits = []
        for t in reads:
            if t.w is not None:
                self._need(eng, t.w, waits)
        for t in writes:
            if t.w is not None:
                self._need(eng, t.w, waits, raw=False)
            for d in list(t.r.values()):
                self._need(eng, d, waits, raw=False)
        self.serial[eng] += 1
        s = self.serial[eng]
        tok = ("e", eng, s)
        for t in writes:
            t.w = tok
            t.r = {}
        for t in reads:
            t.r[("e", eng)] = tok
        sn = list(self.vc[eng])
        sn[self.eidx[eng]] = s
        self.snap[eng].append(tuple(sn))
        self.ops[eng].append((fn, waits, s, None))
        return tok

    def dma(self, fn, reads=(), writes=()):
        eng = "sp"
        waits = []
        if self.sp_barrier is not None and any(not t.persist for t in writes):
            for dep in self.sp_barrier:
                self._need(eng, dep, waits)
            self.sp_barrier = None
        for t in reads:
            if t.w is not None:
                self._need(eng, t.w, waits)
        for t in writes:
            if t.w is not None:
                self._need(eng, t.w, waits)
            for d in list(t.r.values()):
                self._need(eng, d, waits)
        k = self.dsem_next
        self.dsem_next = (k + 1) % NDSEM
        if self.dsem_val[k] > 0:
            self._need(eng, ("d", k, self.dsem_val[k]), waits)
        self.dsem_val[k] += 16
        tok = ("d", k, self.dsem_val[k])
        if reads:
            self.out_dma[k] = self.dsem_val[k]
        self.serial[eng] += 1
        s = self.serial[eng]
        for t in writes:
            t.w = tok
            t.r = {}
        for t in reads:
            t.r[("d", k)] = tok
        self.snap[eng].append(tuple(self.vc[eng]))
        self.ops[eng].append((fn, waits, s, k))
        return tok

    def barrier(self, full=False):
        cur = {e: self.serial[e] for e in ENGS if e != "sp"}
        for e in ENGS:
            if e == "pe":
                continue
            if e == "sp" and not full:
                deps = [("e", e2, s2) for e2, s2 in cur.items() if s2 > 0] + [("d", k, v) for k, v in self.out_dma.items()]
                self.sp_barrier = deps if self.sp_barrier is None else self.sp_barrier + deps
                continue
            waits = []
            for e2, s2 in cur.items():
                if s2 > 0 and e2 != e:
                    self._need(e, ("e", e2, s2), waits)
            for k, v in self.out_dma.items():
                self._need(e, ("d", k, v), waits)
            if waits:
                self.ops[e].append((None, waits, None, None))

    def finish(self):
        waits = []
        for e in ENGS:
            if e != "sp" and self.serial[e] > 0:
                self._need("sp", ("e", e, self.serial[e]), waits)
        for k in range(NDSEM):
            if self.dsem_val[k] > 0:
                self._need("sp", ("d", k, self.dsem_val[k]), waits)
        self.ops["sp"].append((None, waits, None, None))

    def emit(self, sems, dsems, block):
        nc = self.nc
        pe_sorted = sorted(self.pe_inc)
        pe_rank = {s: i + 1 for i, s in enumerate(pe_sorted)}
        handles = {"pe": nc.tensor, "act": nc.scalar, "dve": nc.vector, "pool": nc.gpsimd, "sp": nc.sync}

        def run(eng):
            h = handles[eng]
            for fn, waits, s, dk in self.ops[eng]:
                for d in waits:
                    if d[0] == "e":
                        v = pe_rank[d[2]] if d[1] == "pe" else d[2]
                        h.wait_ge(sems[d[1]], v)
                    else:
                        h.wait_ge(dsems[d[1]], d[2])
                if fn is None:
                    continue
                ins = fn()
                if dk is not None:
                    ins.then_inc(dsems[dk], 16)
                elif eng == "pe":
                    if s in pe_rank:
                        ins.then_inc(sems[eng], 1)
                elif eng != "sp":
                    ins.then_inc(sems[eng], 1)

        block.tensor(lambda e: run("pe"))
        block.scalar(lambda e: run("act"))
        block.vector(lambda e: run("dve"))
        block.gpsimd(lambda e: run("pool"))
        block.sync(lambda e: run("sp"))


def _const_layout():
    off = {}
    cur = 0

    def add(name, n):
        nonlocal cur
        off[name] = (cur, n)
        cur += n

    add("ident", 128)
    add("eps", 1)
    add("nw", 5 * 8)
    add("convw", 2 * 3 * FT)
    add("convb", 2 * FT)
    add("glu_b", 4)
    add("s5_d", 4)
    add("hg_lg", 2 * 8)
    add("inv_freq", 1)
    add("sgn", 1)
    add("one", 1)
    add("hg_nw", 1)
    add("maskT", 4 * 128)
    add("qdec", 4 * 128)
    add("kdec", 4)
    add("g128", 4)
    add("g4", 4)
    add("maskS", 4 * 64)
    add("qdecS", 4 * 64)
    add("kdecS", 4)
    add("seqmask", 16)
    add("triBD", 128)
    add("supBD", 128)
    add("triS", 64)
    add("supS", 64)
    return off, cur


CL, NCONST = _const_layout()
RET_GAMMA = [1.0 - 2.0 ** (-5.0 - h) for h in range(4)]


class Arena:
    def __init__(self, ap_words, nwords):
        self.ap = ap_words
        self.n = nwords
        self.cur = 0

    def mark(self):
        return self.cur

    def release(self, m):
        self.cur = m

    def alloc(self, name, free_shape, dtype=F32):
        n = int(np.prod(free_shape))
        words = n if dtype in (F32, I32) else (n + 1) // 2
        words = (words + 7) // 8 * 8
        assert self.cur + words <= self.n, f"arena overflow at {name}: need {words} have {self.n - self.cur}"
        a = self.ap[:, self.cur:self.cur + words]
        self.cur += words
        if dtype == BF16:
            a = a.bitcast(BF16)[:, 0:n]
        elif dtype == I32:
            a = a.bitcast(I32)[:, 0:n]
        else:
            a = a[:, 0:n]
        if len(free_shape) == 2:
            a = a.rearrange("p (a b) -> p a b", a=free_shape[0])
        elif len(free_shape) == 3:
            a = a.rearrange("p (a b c) -> p a b c", a=free_shape[0], b=free_shape[1])
        return Tile(name, a)


class Rot:
    def __init__(self, tiles):
        self.tiles = tiles
        self.i = 0

    def next(self):
        t = self.tiles[self.i]
        self.i = (self.i + 1) % len(self.tiles)
        return t


from functools import partial
from contextlib import ExitStack

ARENA_WORDS = 39600


class Prog:
    def __init__(self, debug=None):
        self.debug = debug or {}
        self.nc = bass.Bass("TRN2", target_bir_lowering=False)
        self.sch = Sched(self.nc)
        self.h = {"pe": self.nc.tensor, "act": self.nc.scalar, "dve": self.nc.vector, "pool": self.nc.gpsimd}
        self.inputs = {}
        self.outputs = {}

    def din(self, name, shape, dt=F32):
        ap = self.nc.dram_tensor(name, list(shape), dt, kind="ExternalInput").ap()
        self.inputs[name] = ap
        return ap

    def dout(self, name, shape):
        ap = self.nc.dram_tensor(name, list(shape), F32, kind="ExternalOutput").ap()
        self.outputs[name] = ap
        return ap

    def op(self, eng, fn, reads=(), writes=()):
        return self.sch.op(eng, fn, reads, writes)

    def tt(self, eng, out, in0, in1, op, reads, writes):
        self.op(eng, partial(self.h[eng].tensor_tensor, out=out, in0=in0, in1=in1, op=op), reads, writes)

    def stt(self, out, in0, scalar, in1, op0, op1, reads, writes):
        self.op("dve", partial(self.nc.vector.scalar_tensor_tensor, out=out, in0=in0, scalar=scalar, in1=in1,
                               op0=op0, op1=op1), reads, writes)

    def ts(self, eng, out, in0, s1, s2, op0, op1, reads, writes):
        if s2 is None and eng == "pool" and op0 == ALU.mult:
            s2, op1 = 0.0, ALU.add
        if s2 is None:
            self.op(eng, partial(self.h[eng].tensor_scalar, out=out, in0=in0, scalar1=s1, scalar2=None, op0=op0),
                    reads, writes)
        else:
            self.op(eng, partial(self.h[eng].tensor_scalar, out=out, in0=in0, scalar1=s1, scalar2=s2, op0=op0,
                                 op1=op1), reads, writes)

    def cp(self, eng, out, in_, reads, writes):
        if eng == "act":
            self.op(eng, partial(self.nc.scalar.copy, out=out, in_=in_), reads, writes)
        else:
            self.op(eng, partial(self.h[eng].tensor_copy, out=out, in_=in_), reads, writes)

    def act(self, out, in_, func, reads, writes, bias=None, scale=None, accum_out=None):
        kw = dict(out=out, in_=in_, func=func)
        if bias is not None:
            kw["bias"] = bias
        if scale is not None:
            kw["scale"] = scale
        if accum_out is not None:
            kw["accum_out"] = accum_out
        self.op("act", partial(self.nc.scalar.activation, **kw), reads, writes)

    def mm(self, out, lhsT, rhs, start, stop, reads, writes):
        self.op("pe", partial(self.nc.tensor.matmul, out, lhsT=lhsT, rhs=rhs, start=start, stop=stop), reads, writes)

    def tr(self, out, in_, ident, reads, writes):
        self.op("pe", partial(self.nc.tensor.transpose, out=out, in_=in_, identity=ident), reads, writes)

    def dma(self, out, in_, reads=(), writes=(), **kw):
        self.sch.dma(partial(self.nc.sync.dma_start, out=out, in_=in_, **kw), reads, writes)

    def load_w(self, dst, src, K, n, cast_eng="act"):
        srcv = src.rearrange("(k p) n -> p k n", p=128)
        kk = max(1, 1024 // n)
        for k0 in range(0, K, kk):
            k1 = min(K, k0 + kk)
            st = self.wstage.next()
            sv = st.ap[:, 0:(k1 - k0) * n].rearrange("p (k n) -> p k n", n=n)
            self.dma(sv, srcv[:, k0:k1, :], reads=[], writes=[st])
            self.cp(cast_eng, dst.ap[:, k0:k1, :], sv, reads=[st], writes=[dst])

    def build(self):
        nc = self.nc
        with ExitStack() as es:
            self.es = es
            self._declare_dram()
            self._alloc(es)
            self._body()
            self.sch.finish()
            sems = {e: es.enter_context(nc.semaphore("sem_" + e)) for e in ENGS}
            dsems = [es.enter_context(nc.semaphore("dsem%d" % k)) for k in range(NDSEM)]
            block = es.enter_context(nc.Block())
            self.sch.emit(sems, dsems, block)
        return nc

    def _declare_dram(self):
        d = self.din
        self.xp = d("xp", [SEQ, D])
        self.xs = d("xs", [NSMP, D])
        self.consts_d = d("consts", [128, NCONST])
        self.wg_d = d("wg", [2, D, DFF])
        self.wu_d = d("wu", [2, D, DFF])
        self.wd_d = d("wd", [2, DFF, D])
        self.conv0_d = d("conv0", [2, 32, DFF])
        self.w_in_ab = d("w_in_ab", [D, 2560])
        self.glu_w = d("glu_w", [512, 512])
        self.w_out_ab = d("w_out_ab", [D, D])
        self.s5_sp = d("s5_sp", [128, 48])
        self.s5_BT = d("s5_BT", [2, 128, 2048])
        self.s5_CT = d("s5_CT", [2, 128, 2048])
        self.s5re0 = d("s5re0", [NSS, 2048])
        self.s5im0 = d("s5im0", [NSS, 2048])
        self.ret0 = d("ret0", [NSS, 4, 128, 128])
        self.pos_d = d("pos", [128, NSS], I32)
        self.cpos_d = d("cpos", [128, SEGT])
        self.w_in_c = d("w_in_c", [D, 4096])
        self.w_out_c = d("w_out_c", [D, D])
        self.hg0 = d("hg0", [NSS, 8, 128, 128])
        o = self.dout
        self.y_p = o("y_p", [SEQ, D])
        self.y_s = o("y_s", [NSMP, D])
        self.conv_p = o("conv_p", [2, 2, DFF])
        self.conv_s = o("conv_s", [2, 32, DFF])
        self.s5re_p = o("s5re_p", [16, 128])
        self.s5im_p = o("s5im_p", [16, 128])
        self.ret_p = o("ret_p", [4, 128, 128])
        self.s5re_s = o("s5re_s", [NSS, 2048])
        self.s5im_s = o("s5im_s", [NSS, 2048])
        self.ret_s = o("ret_s", [NSS, 4, 128, 128])
        self.hg_p = o("hg_p", [8, 128, 128])
        self.hg_s = o("hg_s", [NSS, 8, 128, 128])

    def _alloc(self, es):
        nc = self.nc
        sb = lambda name, shape, dt=F32: es.enter_context(nc.sbuf_tensor(name, shape, dt))
        self.consts = Tile("consts", sb("consts_sb", [128, NCONST])[:])
        self.xT = [Tile("xT%d" % c, None) for c in range(3)]
        xT_full = sb("xT", [128, KT, W0])
        self.xT_ap = xT_full
        self.identb = Tile("identb", sb("identb", [128, 128], BF16)[:])
        self.onesb = Tile("onesb", sb("onesb", [128, 128], BF16)[:])
        self.tails = Tile("tails", sb("tails", [128, 2, 2, FT])[:])
        self.Sret = Tile("Sret", sb("Sret", [128, 4, 128])[:])
        self.Sretb = Tile("Sretb", sb("Sretb", [128, 4, 128], BF16)[:])
        self.s5carry = Tile("s5carry", sb("s5carry", [128, 2, 16])[:])
        self.Shg = Tile("Shg", sb("Shg", [128, 8, 128])[:])
        self.Shgb = Tile("Shgb", sb("Shgb", [128, 8, 128], BF16)[:])
        arena_t = sb("arena", [128, ARENA_WORDS])
        self.arena = Arena(arena_t, ARENA_WORDS)
        self.ps = [Tile("ps%d" % i, es.enter_context(nc.psum_tensor("ps%d" % i, [128, 512], F32))[:]) for i in range(8)]

    def _sq_halves(self, s):
        key = id(s)
        if not hasattr(self, "_sqh"):
            self._sqh = {}
        if key not in self._sqh:
            self._sqh[key] = (Tile(s.name + "_A", s.ap), Tile(s.name + "_B", s.ap))
        return self._sqh[key]

    def alloc_hT(self):
        return [self.arena.alloc("hT%d" % c, [KT, 512 if c < 2 else NSMP], BF16) for c in range(3)]

    def C(self, name, rows=128):
        o, n = CL[name]
        return self.consts.ap[0:rows, o:o + n]

    def chunks(self, seg):
        ch = [(0, 0, 512), (1, 512, 512)]
        if seg == 0:
            ch.append((2, 1024, NSMP))
        return ch

    def _body(self):
        nc = self.nc
        ar = self.arena
        self.dma(self.consts.ap, self.consts_d, writes=[self.consts])
        self.cp("dve", self.identb.ap, self.C("ident"), [self.consts], [self.identb])
        self.op("dve", partial(nc.vector.memset, self.onesb.ap, 1.0), [], [self.onesb])
        self.op("dve", partial(nc.vector.memset, self.tails.ap, 0.0), [], [self.tails])
        self.op("dve", partial(nc.vector.memset, self.Sret.ap, 0.0), [], [self.Sret])
        self.op("dve", partial(nc.vector.memset, self.Sretb.ap, 0.0), [], [self.Sretb])
        self.op("dve", partial(nc.vector.memset, self.s5carry.ap, 0.0), [], [self.s5carry])
        self.op("dve", partial(nc.vector.memset, self.Shg.ap, 0.0), [], [self.Shg])
        self.op("dve", partial(nc.vector.memset, self.Shgb.ap, 0.0), [], [self.Shgb])
        self._pb = 0
        for seg in range(NSEG):
            self.seg = seg
            m0 = ar.mark()
            self.wstage = Rot([ar.alloc("wst%d" % i, [1024]) for i in range(4)])
            for t_ in self.wstage.tiles:
                t_.persist = True
            self.sq = Rot([ar.alloc("sq%d" % i, [KT, 512], BF16) for i in range(1)])
            self.rs = Rot([ar.alloc("rs%d" % i, [512]) for i in range(1)])
            self.load_x(seg)
            for layer in ([0] if self.debug.get("only_ab") else [1] if self.debug.get("only_c") else [0, 1]):
                if not self.debug.get("skip_mixer"):
                    if layer == 0:
                        self.mixer_ab(seg)
                    else:
                        self.mixer_c(seg)
                if not self.debug.get("skip_ffn"):
                    self.ffn(seg, layer)
            self.final(seg)
            self.sch.barrier(full=True)
            ar.release(m0)

    def xcols(self, c, k0=0, k1=KT):
        c0, n = [(0, 512), (512, 512), (1024, NSMP)][c]
        return self.xT_ap[:, k0:k1, c0:c0 + n]

    def load_x(self, seg):
        ar = self.arena
        m = ar.mark()
        stg = Rot([ar.alloc("xst%d" % i, [D]) for i in range(2)])
        ident = self.C("ident")
        tiles = [(self.xp[seg * SEGT + t * 128: seg * SEGT + (t + 1) * 128, :], 128, t // 4, (t % 4) * 128) for t in range(8)]
        if seg == 0:
            tiles.append((self.xs[:, :], NSMP, 2, 0))
        banks = Rot([self.ps[6], self.ps[7]])
        for (src, rows, c, off) in tiles:
            st = stg.next()
            self.dma(st.ap[0:rows, :], src, writes=[st])
            for half in range(2):
                b = banks.next()
                for q in range(4):
                    k = half * 4 + q
                    self.tr(b.ap[:, q * 128:q * 128 + rows], st.ap[0:rows, k * 128:(k + 1) * 128], ident[0:rows, 0:rows],
                            [st, self.consts], [b])
                c0 = [0, 512, 1024][c] + off
                dst = self.xT_ap[:, half * 4:half * 4 + 4, c0:c0 + rows]
                srcv = b.ap.rearrange("p (q t) -> p q t", q=4)[:, :, 0:rows]
                self.cp("act", dst, srcv, [b], [self.xT[c]])
        self.sch.barrier()
        ar.release(m)

    def rmsnorm(self, seg, idx, out_tiles):
        sq, rs = self.sq, self.rs
        nwo = CL["nw"][0]
        eps_ap = self.C("eps")
        for (c, c0, n) in self.chunks(seg):
            s = sq.next()
            if not hasattr(s, "_halves"):
                pass
            sA, sB = self._sq_halves(s)
            for k in range(KT):
                xk = self.xcols(c, k, k + 1)[:, 0, :]
                if k < 5:
                    self.act(s.ap[:, k, 0:n], xk, AF.Square, [self.xT[c]], [sA])
                else:
                    self.tt("pool", s.ap[:, k, 0:n], xk, xk, ALU.mult, [self.xT[c]], [sB])
            b = self.ps[5]
            for k in range(KT):
                self.mm(b.ap[:, 0:n], self.onesb.ap, s.ap[:, k, 0:n], k == 0, k == KT - 1,
                        [self.onesb, sA if k < 5 else sB], [b])
            r = rs.next()
            self.act(r.ap[:, 0:n], b.ap[:, 0:n], AF.Ln, [b], [r], bias=eps_ap, scale=1.0 / D)
            self.act(r.ap[:, 0:n], r.ap[:, 0:n], AF.Exp, [r], [r], scale=-0.5)
            for k in range(KT):
                self.stt(out_tiles[c].ap[:, k, 0:n], self.xcols(c, k, k + 1)[:, 0, :],
                         self.consts.ap[:, nwo + idx * 8 + k: nwo + idx * 8 + k + 1], r.ap[:, 0:n],
                         ALU.mult, ALU.mult, [self.xT[c], self.consts, r], [out_tiles[c]])

    def ffn(self, seg, layer):
        nc = self.nc
        ar = self.arena
        m0 = ar.mark()
        self.hT = self.alloc_hT()
        groups = [list(range(0, 6)), list(range(6, 12)), list(range(12, 17)), list(range(17, 22))]
        wgt = Rot([ar.alloc("wg%d" % i, [KT, 128], BF16) for i in range(4)])
        wut = Rot([ar.alloc("wu%d" % i, [KT, 128], BF16) for i in range(4)])
        wdt = Rot([ar.alloc("wd%d" % i, [6, D], BF16) for i in range(2)])
        actTs = [[ar.alloc("act%d_%d" % (i, c), [6, 512 if c < 2 else NSMP], BF16) for c in range(3)] for i in range(2)]
        exth = Rot([ar.alloc("exth%d" % i, [2]) for i in range(24)])
        cbuf = Rot([ar.alloc("cb%d" % i, [512]) for i in range(3)])
        sbuf = Rot([ar.alloc("sb%d" % i, [512]) for i in range(2)])
        class _G:
            def next(_s):
                return self.pbank()
        psA = _G()
        psB = _G()
        ident = self.C("ident")
        cwo = CL["convw"][0]
        cbo = CL["convb"][0]
        cw = lambda j, f: self.consts.ap[:, cwo + (layer * 3 + j) * FT + f: cwo + (layer * 3 + j) * FT + f + 1]
        cbias = lambda f: self.consts.ap[:, cbo + layer * FT + f: cbo + layer * FT + f + 1]
        chunks = self.chunks(seg)
        if seg == 0:
            cs = ar.alloc("cs", [DFF])
            bufT = ar.alloc("bufT", [FT, 32])
            ext_sh = ar.alloc("ext_sh", [FT, 16, 2])
            ext_sb = ar.alloc("ext_sb", [FT, 16, 4])
            ext_s2 = ar.alloc("ext_s2", [FT, 32])
            self.dma(cs.ap[0:32, :], self.conv0_d[layer], writes=[cs])
            for half in range(2):
                b = self.ps[6 + half]
                f0, f1 = (0, 16) if half == 0 else (16, FT)
                for f in range(f0, f1):
                    q = f - f0
                    self.tr(b.ap[:, q * 32:(q + 1) * 32], cs.ap[0:32, f * 128:(f + 1) * 128], ident[0:32, 0:32],
                            [cs, self.consts], [b])
                self.cp("act", bufT.ap[:, f0:f1, :], b.ap[:, 0:(f1 - f0) * 32].rearrange("p (f t) -> p f t", t=32),
                        [b], [bufT])
            self.cp("pool", ext_sh.ap, bufT.ap.rearrange("p f (s j) -> p f s j", j=2), [bufT], [ext_sh])

        def load_gu(f):
            g, u = wgt.next(), wut.next()
            self.load_w(g, self.wg_d[layer][:, f * 128:(f + 1) * 128], KT, 128)
            self.load_w(u, self.wu_d[layer][:, f * 128:(f + 1) * 128], KT, 128)
            return g, u

        def load_d(grp):
            w = wdt.next()
            self.load_w(w, self.wd_d[layer][grp[0] * 128:(grp[-1] + 1) * 128, :], len(grp), D)
            return w

        allf = [f for g in groups for f in g]
        pend = {allf[0]: load_gu(allf[0]), allf[1]: load_gu(allf[1])}
        rs_saved = self.rs
        self.rs = Rot([rs_saved.tiles[0], ar.alloc("rs_x", [512])])
        self.rmsnorm(seg, 1 + 2 * layer, self.hT)
        self.rs = rs_saved

        def phase_b(grp, wd, actT):
            for (c, c0, n) in chunks:
                for mo in range(KT):
                    b = psB.next()
                    for fl in range(len(grp)):
                        self.mm(b.ap[:, 0:n], wd.ap[:, fl, mo * 128:(mo + 1) * 128], actT[c].ap[:, fl, 0:n],
                                fl == 0, fl == len(grp) - 1, [wd, actT[c]], [b])
                    xv = self.xcols(c, mo, mo + 1)[:, 0, :]
                    self.tt("dve", xv, xv, b.ap[:, 0:n], ALU.add, [self.xT[c], b], [self.xT[c]])

        pend_b = None
        pend_tail = None
        for gi, grp in enumerate(groups):
            wd = load_d(grp)
            actT = actTs[gi % 2]
            prev_ext = None
            for fl, f in enumerate(grp):
                g, u = pend.pop(f)
                nxt = allf.index(f) + 2
                if nxt < len(allf):
                    pend[allf[nxt]] = load_gu(allf[nxt])
                for (c, c0, n) in chunks:
                    gb, ub = psA.next(), psA.next()
                    for k in range(KT):
                        self.mm(gb.ap[:, 0:n], g.ap[:, k, :], self.hT[c].ap[:, k, 0:n], k == 0, k == KT - 1,
                                [g, self.hT[c]], [gb])
                    for k in range(KT):
                        self.mm(ub.ap[:, 0:n], u.ap[:, k, :], self.hT[c].ap[:, k, 0:n], k == 0, k == KT - 1,
                                [u, self.hT[c]], [ub])
                    cb, sb_ = cbuf.next(), sbuf.next()
                    if c < 2:
                        eh, eb = exth.next(), exth.next()
                        if c == 0:
                            self.cp("pool", eh.ap, self.tails.ap[:, layer, :, f], [self.tails], [eh])
                        else:
                            self.cp("pool", eh.ap, prev_ext.ap, [prev_ext], [eh])
                        self.cp("act", eb.ap, gb.ap[:, n - 2:n], [gb], [eb])
                        self.act(cb.ap[:, 0:n], gb.ap[:, 0:n], AF.Identity, [gb, self.consts], [cb],
                                 bias=cbias(f), scale=cw(2, f))
                        self.stt(cb.ap[:, 1:n], gb.ap[:, 0:n - 1], cw(1, f), cb.ap[:, 1:n], ALU.mult, ALU.add,
                                 [gb, cb, self.consts], [cb])
                        self.stt(cb.ap[:, 2:n], gb.ap[:, 0:n - 2], cw(0, f), cb.ap[:, 2:n], ALU.mult, ALU.add,
                                 [gb, cb, self.consts], [cb])
                        hc, hc2 = exth.next(), exth.next()
                        self.ts("pool", hc.ap, eh.ap, cw(0, f), None, ALU.mult, None, [eh, self.consts], [hc])
                        self.ts("pool", hc2.ap[:, 0:1], eh.ap[:, 1:2], cw(1, f), None, ALU.mult, None, [eh, self.consts], [hc2])
                        self.tt("pool", hc.ap[:, 0:1], hc.ap[:, 0:1], hc2.ap[:, 0:1], ALU.add, [hc, hc2], [hc])
                        self.tt("dve", cb.ap[:, 0:2], cb.ap[:, 0:2], hc.ap, ALU.add, [cb, hc], [cb])
                        if c == 1:
                            self.cp("pool", self.tails.ap[:, layer, :, f], eb.ap, [eb], [self.tails])
                        prev_ext = eb
                    else:
                        gv = gb.ap[:, 0:NSMP].rearrange("p (s t) -> p s t", t=4)
                        cv = cb.ap[:, 0:NSMP].rearrange("p (s t) -> p s t", t=4)
                        esb = ext_sb.ap[:, f]
                        esh = ext_sh.ap[:, f]
                        self.cp("act", esb, gv, [gb], [ext_sb])
                        self.cp("pool", ext_s2.ap[:, f].rearrange("p (s j) -> p s j", j=2), esb[:, :, 2:4], [ext_sb], [ext_s2])
                        self.act(cb.ap[:, 0:NSMP], gb.ap[:, 0:NSMP], AF.Identity, [gb, self.consts], [cb],
                                 bias=cbias(f), scale=cw(2, f))
                        self.stt(cv[:, :, 1:4], esb[:, :, 0:3], cw(1, f), cv[:, :, 1:4], ALU.mult, ALU.add,
                                 [ext_sb, cb, self.consts], [cb])
                        self.stt(cv[:, :, 2:4], esb[:, :, 0:2], cw(0, f), cv[:, :, 2:4], ALU.mult, ALU.add,
                                 [ext_sb, cb, self.consts], [cb])
                        self.stt(cv[:, :, 0:1], esh[:, :, 1:2], cw(1, f), cv[:, :, 0:1], ALU.mult, ALU.add,
                                 [ext_sh, cb, self.consts], [cb])
                        self.stt(cv[:, :, 0:2], esh[:, :, 0:2], cw(0, f), cv[:, :, 0:2], ALU.mult, ALU.add,
                                 [ext_sh, cb, self.consts], [cb])
                    def tail(cb=cb, sb_=sb_, ub=ub, actTc=actT[c], fl=fl, n=n):
                        self.act(sb_.ap[:, 0:n], cb.ap[:, 0:n], AF.Silu, [cb], [sb_])
                        self.tt("dve", actTc.ap[:, fl, 0:n], sb_.ap[:, 0:n], ub.ap[:, 0:n], ALU.mult, [sb_, ub], [actTc])
                    if pend_tail is not None:
                        pend_tail()
                    pend_tail = tail
            if pend_tail is not None:
                pend_tail()
                pend_tail = None
            if pend_b is not None:
                phase_b(*pend_b)
            pend_b = (grp, wd, actT)
        phase_b(*pend_b)
        if seg == 0:
            cso = cs
            for q0 in range(0, FT, 4):
                b = psB.next()
                fs = list(range(q0, min(FT, q0 + 4)))
                for qi, f in enumerate(fs):
                    self.mm(b.ap[0:32, qi * 128:(qi + 1) * 128], ext_s2.ap[:, f, :], ident, True, True,
                            [ext_s2, self.consts], [b])
                self.cp("act", cso.ap[0:32, q0 * 128:(q0 + len(fs)) * 128], b.ap[0:32, 0:len(fs) * 128], [b], [cso])
            self.dma(self.conv_s[layer], cso.ap[0:32, :], reads=[cso])
        if seg == NSEG - 1:
            b = psB.next()
            tl = ar.alloc("tl", [128])
            self.mm(b.ap[0:2 * FT, 0:128], self.tails.ap[:, layer].rearrange("p j f -> p (j f)"), ident, True, True,
                    [self.tails, self.consts], [b])
            self.cp("act", tl.ap[0:2 * FT, :], b.ap[0:2 * FT, 0:128], [b], [tl])
            for j in range(2):
                self.dma(self.conv_p[layer, j].rearrange("(f p) -> f p", p=128), tl.ap[j * FT:(j + 1) * FT, :], reads=[tl])
        self.sch.barrier()
        ar.release(m0)

    def final(self, seg):
        ar = self.arena
        m0 = ar.mark()
        hF = [ar.alloc("hF%d" % c, [KT, 512 if c < 2 else NSMP]) for c in range(3)]
        rs_saved = self.rs
        self.rs = Rot([rs_saved.tiles[0], ar.alloc("rs_x", [512])])
        self.rmsnorm(seg, 4, hF)
        self.rs = rs_saved
        osts = []
        for i in range(2):
            t_ = ar.alloc("ost%d" % i, [D])
            osts.append((t_, Tile("ostb%d" % i, t_.ap)))
        ost = Rot(osts)
        ident = self.C("ident")
        banks = Rot([self.ps[0], self.ps[1], self.ps[2], self.ps[3]])
        tiles = [(t // 4, (t % 4) * 128, 128, self.y_p[seg * SEGT + t * 128: seg * SEGT + (t + 1) * 128, :]) for t in range(8)]
        if seg == 0:
            tiles.append((2, 0, NSMP, self.y_s[:, :]))
        for (c, off, rows, dst) in tiles:
            oa, ob = ost.next()
            for half in range(2):
                b = banks.next()
                for q in range(4):
                    k = half * 4 + q
                    self.tr(b.ap[0:rows, q * 128:(q + 1) * 128], hF[c].ap[:, k, off:off + rows], ident, [hF[c], self.consts], [b])
                self.cp("act" if half == 0 else "dve", oa.ap[0:rows, half * 512:(half + 1) * 512], b.ap[0:rows, :], [b],
                        [oa] if half == 0 else [ob])
            self.dma(dst, oa.ap[0:rows, :], reads=[oa, ob])
        ar.release(m0)


def _fm(v):
    v = np.asarray(v, np.float32)
    return np.ascontiguousarray(v.reshape(-1, 128).T)


def _build_consts(inp):
    c = np.zeros((128, NCONST), np.float32)

    def put(name, arr, rows=128):
        o, n = CL[name]
        arr = np.asarray(arr, np.float32).reshape(rows, n)
        c[0:rows, o:o + n] = arr

    put("ident", np.eye(128, dtype=np.float32))
    put("eps", np.full((128, 1), EPS, np.float32))
    nw = np.stack([_fm(inp["norm_mix"][0]), _fm(inp["norm_ffn"][0]), _fm(inp["norm_mix"][1]), _fm(inp["norm_ffn"][1]),
                   _fm(inp["norm_final"])], axis=1)
    put("nw", nw)
    cw = np.asarray(inp["ffn_conv_w"], np.float32).reshape(2, 3, FT, 128).transpose(3, 0, 1, 2)
    put("convw", cw)
    cb = np.asarray(inp["ffn_conv_b"], np.float32).reshape(2, FT, 128).transpose(2, 0, 1)
    put("convb", cb)
    put("glu_b", _fm(inp["s5_glu_b"][0]))
    put("s5_d", _fm(np.asarray(inp["s5_d"][0]).reshape(-1)))
    lg = np.stack([_fm(inp["hg_lb_logits"][0]), _fm(inp["hg_lb_logits"][1])], axis=1)
    put("hg_lg", lg)
    half = 64
    inv = (1.0 / (10000.0 ** np.linspace(0.0, 1.0, half, dtype=np.float32))).astype(np.float32)
    p = np.arange(128)
    put("inv_freq", inv[p % 64].reshape(128, 1))
    put("sgn", np.where(p < 64, -1.0, 1.0).reshape(128, 1))
    put("one", np.ones((128, 1), np.float32))
    put("hg_nw", np.asarray(inp["hg_norm_w"][0], np.float32).reshape(128, 1))
    g = np.array(RET_GAMMA, np.float64)
    scale = 128.0 ** -0.5
    j = np.arange(128)[:, None]
    i = np.arange(128)[None, :]
    mt = np.zeros((128, 4, 128))
    for h in range(4):
        mt[:, h, :] = np.where(i >= j, g[h] ** np.maximum(i - j, 0), 0.0) * scale
    put("maskT", mt)
    qd = np.zeros((128, 4, 128))
    kd = np.zeros((128, 4))
    for h in range(4):
        qd[:, h, :] = (g[h] ** (np.arange(128) + 1))[None, :]
        kd[:, h] = g[h] ** (127 - np.arange(128)) * scale
    put("qdec", qd)
    put("kdec", kd)
    put("g128", np.broadcast_to((g ** 128)[None, :], (128, 4)))
    put("g4", np.broadcast_to((g ** 4)[None, :], (128, 4)))
    tok = np.arange(64)
    sq_, tau = tok // 4, tok % 4
    ms = np.zeros((128, 4, 64))
    for h in range(4):
        same = (sq_[:, None] == sq_[None, :]) & (tau[None, :] >= tau[:, None])
        ms[0:64, h, :] = np.where(same, g[h] ** np.maximum(tau[None, :] - tau[:, None], 0), 0.0) * scale
    put("maskS", ms)
    qds = np.zeros((128, 4, 64))
    kds = np.zeros((128, 4))
    for h in range(4):
        qds[:, h, :] = (g[h] ** (tau + 1))[None, :]
        kds[0:64, h] = g[h] ** (3 - tau) * scale
    put("qdecS", qds)
    put("kdecS", kds)
    sm = np.zeros((128, 16))
    sm[tok, sq_] = 1.0
    put("seqmask", sm)
    jj = np.arange(128)[:, None]
    ii = np.arange(128)[None, :]
    samec = (jj // 64) == (ii // 64)
    put("triBD", (samec & (jj <= ii)).astype(np.float32))
    put("supBD", (samec & (jj > ii)).astype(np.float32))
    ts_ = np.zeros((128, 64))
    us_ = np.zeros((128, 64))
    sames = sq_[:, None] == sq_[None, :]
    ts_[0:64] = (sames & (tok[:, None] <= tok[None, :]))
    us_[0:64] = (sames & (tok[:, None] > tok[None, :]))
    put("triS", ts_)
    put("supS", us_)
    return c


def _s5_layouts(inp):
    lam_re = np.asarray(inp["s5_lam_re"][0], np.float32)
    lam_im = np.asarray(inp["s5_lam_im"][0], np.float32)
    ldt = np.asarray(inp["s5_log_dt"][0], np.float32)

    def sp(a):
        return np.ascontiguousarray(a.reshape(16, 2, 64).transpose(1, 2, 0).reshape(128, 16))

    s5_sp = np.concatenate([sp(lam_re), sp(lam_im), sp(np.repeat(ldt[:, None], 64, axis=1))], axis=1)
    BT = np.zeros((2, 128, 16, 128), np.float32)
    CT = np.zeros((2, 128, 16, 128), np.float32)
    for ri, (bsrc, csrc) in enumerate(((inp["s5_b_re"][0], inp["s5_c_re"][0]), (inp["s5_b_im"][0], inp["s5_c_im"][0]))):
        bsrc = np.asarray(bsrc, np.float32)
        csrc = np.asarray(csrc, np.float32)
        for gidx in range(32):
            jx, g2, gl = gidx // 2, gidx % 2, gidx % 8
            BT[ri, gl * 16:(gl + 1) * 16, jx, g2 * 64:(g2 + 1) * 64] = bsrc[gidx].T
            CT[ri, g2 * 64:(g2 + 1) * 64, jx, gl * 16:(gl + 1) * 16] = csrc[gidx].T
    return s5_sp, BT.reshape(2, 128, 2048), CT.reshape(2, 128, 2048)


_PROG_CACHE = {}


def _get_prog(debug=None):
    key = tuple(sorted((debug or {}).items()))
    if key not in _PROG_CACHE:
        p = Prog(debug)
        p.build()
        _PROG_CACHE[key] = p
    return _PROG_CACHE[key]


def kernel(_debug=None, _cores=NCORES, **inp):
    inp = {k: np.asarray(v) for k, v in inp.items()}
    prog = _get_prog(_debug)
    consts = _build_consts(inp)
    s5_sp, s5_BT, s5_CT = _s5_layouts(inp)
    cpos = np.ascontiguousarray(np.broadcast_to(np.arange(SEGT, dtype=np.float32)[None, :], (128, SEGT)))
    in_maps = []
    for b in range(_cores):
        m = {
            "xp": np.ascontiguousarray(inp["x_prompt"][b]),
            "xs": np.ascontiguousarray(inp["x_sample"][NSS * b:NSS * (b + 1)].reshape(NSMP, D)),
            "consts": consts,
            "wg": inp["ffn_w_gate"], "wu": inp["ffn_w_up"], "wd": inp["ffn_w_down"],
            "conv0": np.ascontiguousarray(inp["state_ffn_conv"][:, NSS * b:NSS * (b + 1)].reshape(2, 32, DFF)),
            "w_in_ab": inp["w_in_ab"][0], "glu_w": inp["s5_glu_w"][0], "w_out_ab": inp["w_out_ab"][0],
            "s5_sp": s5_sp, "s5_BT": s5_BT, "s5_CT": s5_CT,
            "s5re0": np.ascontiguousarray(inp["state_s5_re"][0, NSS * b:NSS * (b + 1)].reshape(NSS, 2048)),
            "s5im0": np.ascontiguousarray(inp["state_s5_im"][0, NSS * b:NSS * (b + 1)].reshape(NSS, 2048)),
            "ret0": np.ascontiguousarray(inp["state_ret"][0, NSS * b:NSS * (b + 1)]),
            "pos": np.ascontiguousarray(np.broadcast_to(inp["pos_sample"][NSS * b:NSS * (b + 1)].astype(np.int32)[None, :], (128, NSS))),
            "cpos": cpos,
            "w_in_c": inp["w_in_c"][0], "w_out_c": inp["w_out_c"][0],
            "hg0": np.ascontiguousarray(inp["state_hgrn"][0, NSS * b:NSS * (b + 1)]),
        }
        in_maps.append({k: v for k, v in m.items() if k in prog.inputs})
    res = run_bass_kernel_spmd(prog.nc, in_maps, core_ids=list(range(_cores)))
    R = res.results
    B = _cores
    y_p = np.stack([R[b]["y_p"] for b in range(B)])
    y_s = np.concatenate([R[b]["y_s"].reshape(NSS, 4, D) for b in range(B)])
    conv_p = np.stack([R[b]["conv_p"] for b in range(B)], axis=1)
    conv_s = np.concatenate([R[b]["conv_s"].reshape(2, NSS, 2, DFF) for b in range(B)], axis=1)
    out = {"y_p": y_p, "y_s": y_s, "conv_p": conv_p, "conv_s": conv_s}
    if "s5re_p" in R[0]:
        out["s5re_p"] = np.stack([R[b]["s5re_p"].reshape(32, 64) for b in range(B)])[None]
        out["s5im_p"] = np.stack([R[b]["s5im_p"].reshape(32, 64) for b in range(B)])[None]
        out["ret_p"] = np.stack([R[b]["ret_p"] for b in range(B)])[None]
        out["s5re_s"] = np.concatenate([R[b]["s5re_s"].reshape(NSS, 32, 64) for b in range(B)])[None]
        out["s5im_s"] = np.concatenate([R[b]["s5im_s"].reshape(NSS, 32, 64) for b in range(B)])[None]
        out["ret_s"] = np.concatenate([R[b]["ret_s"] for b in range(B)])[None]
        out["hg_p"] = np.stack([R[b]["hg_p"] for b in range(B)])[None]
        out["hg_s"] = np.concatenate([R[b]["hg_s"] for b in range(B)])[None]
    if _debug:
        return out
    f = lambda a: np.ascontiguousarray(a, dtype=np.float32)
    return (f(out["y_p"]), f(out["y_s"]), f(out["s5re_p"]), f(out["s5im_p"]), f(out["ret_p"]), f(out["hg_p"]), f(out["conv_p"]),
            f(out["s5re_s"]), f(out["s5im_s"]), f(out["ret_s"]), f(out["hg_s"]), f(out["conv_s"]))


MAGIC = 12582912.0
TWO_PI_INV = float(1.0 / (2.0 * np.pi))
C1 = 6.28125
C2 = 0.0019353071795864769
PI = float(np.pi)


def _pbank(self):
    b = self.ps[self._pb]
    self._pb = (self._pb + 1) % 8
    return b


def _range_reduce(self, eng, out, ang, tmp, reads, tiles_w):
    o, t = out, tmp
    self.ts(eng, t.ap, ang, TWO_PI_INV, MAGIC, ALU.mult, ALU.add, reads, [t])
    self.ts(eng, t.ap, t.ap, -MAGIC, None, ALU.add, None, [t], [t])
    self.stt(o.ap, t.ap, -C1, ang, ALU.mult, ALU.add, [t] + list(reads), [o])
    self.stt(o.ap, t.ap, -C2, o.ap, ALU.mult, ALU.add, [t, o], [o])
    self.ts("dve", o.ap, o.ap, -PI, PI, ALU.max, ALU.min, [o], [o])


def _sincos(self, r, sin_out, cos_out, tmp, sin_scale=None, extra_reads=()):
    if sin_scale is None:
        self.act(sin_out.ap, r.ap, AF.Sin, [r], [sin_out])
    else:
        self.act(sin_out.ap, r.ap, AF.Sin, [r] + list(extra_reads), [sin_out], scale=sin_scale)
    self.ts("dve", tmp.ap, r.ap, -1.0, None, ALU.mult, None, [r], [tmp])
    self.tt("dve", tmp.ap, tmp.ap, r.ap, ALU.max, [tmp, r], [tmp])
    self.ts("dve", tmp.ap, tmp.ap, -1.0, PI / 2, ALU.mult, ALU.add, [tmp], [tmp])
    self.act(cos_out.ap, tmp.ap, AF.Sin, [tmp], [cos_out])


Prog.pbank = _pbank
Prog.range_reduce = _range_reduce
Prog.sincos = _sincos


def _mixer_ab(self, seg):
    ar = self.arena
    m0 = ar.mark()
    chunks = self.chunks(seg)
    ws = [512, 512, NSMP]
    yTbuf = [ar.alloc("yT%d" % c, [KT, ws[c]], BF16) for c in range(3)]
    yTs = [Tile("yTs%d" % c, yTbuf[c].ap[:, 0:4, :]) for c in range(3)]
    yTr = [Tile("yTr%d" % c, yTbuf[c].ap[:, 4:8, :]) for c in range(3)]
    uT = [ar.alloc("uT%d" % c, [4, ws[c]], BF16) for c in range(3)]
    m1 = ar.mark()
    self.hT = self.alloc_hT()
    self.rmsnorm(seg, 0, self.hT)
    wt = Rot([ar.alloc("wu5_%d" % i, [KT, 128], BF16) for i in range(2)])
    for ft in range(4):
        w = wt.next()
        self.load_w(w, self.w_in_ab[:, ft * 128:(ft + 1) * 128], KT, 128)
        for (c, c0, n) in chunks:
            b = self.pbank()
            for k in range(KT):
                self.mm(b.ap[:, 0:n], w.ap[:, k, :], self.hT[c].ap[:, k, 0:n], k == 0, k == KT - 1, [w, self.hT[c]], [b])
            self.cp("act", uT[c].ap[:, ft, 0:n], b.ap[:, 0:n], [b], [uT[c]])
    if self.debug.get("no_ret"):
        for (c, c0, n) in chunks:
            self.op("dve", partial(self.nc.vector.memset, yTr[c].ap, 0.0), [], [yTr[c]])
    else:
        self.ret_part(seg, yTr)
    self.sch.barrier()
    ar.release(m1)
    if self.debug.get("no_s5"):
        for (c, c0, n) in chunks:
            self.op("dve", partial(self.nc.vector.memset, yTs[c].ap, 0.0), [], [yTs[c]])
    else:
        self.s5_part(seg, yTs, uT)
    self.sch.barrier()
    ar.release(m1)
    wout = ar.alloc("wout", [KT, D], BF16)
    self.load_w(wout, self.w_out_ab, KT, D)
    for (c, c0, n) in chunks:
        for mo in range(KT):
            b = self.pbank()
            for k in range(KT):
                src = yTs[c] if k < 4 else yTr[c]
                self.mm(b.ap[:, 0:n], wout.ap[:, k, mo * 128:(mo + 1) * 128], yTbuf[c].ap[:, k, 0:n], k == 0, k == KT - 1,
                        [wout, src], [b])
            xv = self.xcols(c, mo, mo + 1)[:, 0, :]
            self.tt("dve", xv, xv, b.ap[:, 0:n], ALU.add, [self.xT[c], b], [self.xT[c]])
    self.sch.barrier()
    ar.release(m0)


Prog.mixer_ab = _mixer_ab


def _s5_part(self, seg, yTs, uT):
    nc = self.nc
    ar = self.arena
    ident = self.C("ident")
    V = "dve"

    def T16(name):
        return ar.alloc(name, [16])

    sp = ar.alloc("s5sp", [48])
    self.dma(sp.ap, self.s5_sp, writes=[sp])
    lre, lim, ldt = sp.ap[:, 0:16], sp.ap[:, 16:32], sp.ap[:, 32:48]
    dt, mag, ang, r, tmp, sinA, cosA = [T16(n) for n in ["dt", "mag", "ang", "r", "tmp", "sinA", "cosA"]]
    self.act(dt.ap, ldt, AF.Exp, [sp], [dt])
    self.tt(V, tmp.ap, lre, dt.ap, ALU.mult, [sp, dt], [tmp])
    self.act(mag.ap, tmp.ap, AF.Exp, [tmp], [mag])
    self.tt(V, ang.ap, lim, dt.ap, ALU.mult, [sp, dt], [ang])
    tmp2 = T16("tmp2")
    self.range_reduce(V, r, ang.ap, tmp2, [ang], None)
    tmp3 = T16("tmp3")
    self.sincos(r, sinA, cosA, tmp3)
    names = ["ab_re", "ab_im", "nr", "t1", "t2", "den", "rden", "f_re", "f_im", "if_re", "if_im"]
    ab_re, ab_im, nr, t1, t2, den, rden, f_re, f_im, if_re, if_im = [T16(n) for n in names]
    self.tt(V, ab_re.ap, mag.ap, cosA.ap, ALU.mult, [mag, cosA], [ab_re])
    self.tt(V, ab_im.ap, mag.ap, sinA.ap, ALU.mult, [mag, sinA], [ab_im])
    self.ts(V, nr.ap, ab_re.ap, -1.0, None, ALU.add, None, [ab_re], [nr])
    self.tt(V, t1.ap, lre, lre, ALU.mult, [sp], [t1])
    self.tt(V, t2.ap, lim, lim, ALU.mult, [sp], [t2])
    self.tt(V, den.ap, t1.ap, t2.ap, ALU.add, [t1, t2], [den])
    self.op(V, partial(nc.vector.reciprocal, out=rden.ap, in_=den.ap), [den], [rden])
    self.tt(V, t1.ap, nr.ap, lre, ALU.mult, [nr, sp], [t1])
    self.tt(V, t2.ap, ab_im.ap, lim, ALU.mult, [ab_im, sp], [t2])
    self.tt(V, t1.ap, t1.ap, t2.ap, ALU.add, [t1, t2], [t1])
    self.tt(V, f_re.ap, t1.ap, rden.ap, ALU.mult, [t1, rden], [f_re])
    self.tt(V, t1.ap, ab_im.ap, lre, ALU.mult, [ab_im, sp], [t1])
    self.tt(V, t2.ap, nr.ap, lim, ALU.mult, [nr, sp], [t2])
    self.tt(V, t1.ap, t1.ap, t2.ap, ALU.subtract, [t1, t2], [t1])
    self.tt(V, f_im.ap, t1.ap, rden.ap, ALU.mult, [t1, rden], [f_im])
    self.tt(V, t1.ap, f_re.ap, f_re.ap, ALU.mult, [f_re], [t1])
    self.tt(V, t2.ap, f_im.ap, f_im.ap, ALU.mult, [f_im], [t2])
    self.tt(V, den.ap, t1.ap, t2.ap, ALU.add, [t1, t2], [den])
    self.op(V, partial(nc.vector.reciprocal, out=rden.ap, in_=den.ap), [den], [rden])
    self.tt(V, if_re.ap, f_re.ap, rden.ap, ALU.mult, [f_re, rden], [if_re])
    self.tt(V, t1.ap, f_im.ap, rden.ap, ALU.mult, [f_im, rden], [t1])
    self.ts(V, if_im.ap, t1.ap, -1.0, None, ALU.mult, None, [t1], [if_im])

    R = ar.alloc("R", [4096])
    u = [Tile("u%d" % i, R.ap[:, i * 1024:(i + 1) * 1024].rearrange("p (a b) -> p a b", a=16)) for i in range(4)]
    u2 = [Tile("w%d" % i, R.ap[:, i * 2048:(i + 1) * 2048].rearrange("p (a b) -> p a b", a=16)) for i in range(2)]
    ysb_t = [Tile("ysb%d" % i, R.ap[:, i * 2048:(i + 1) * 2048].rearrange("p (a b) -> p a b", a=4)) for i in range(2)]

    cE = ar.alloc("cE", [16, 128])
    sE = ar.alloc("sE", [16, 128])
    self.op(V, partial(nc.vector.memset, cE.ap[:, :, 0:1], 1.0), [], [cE])
    self.op(V, partial(nc.vector.memset, sE.ap[:, :, 0:1], 0.0), [], [sE])
    self.cp(V, cE.ap[:, :, 1:2], cosA.ap.unsqueeze(2), [cosA], [cE])
    self.cp(V, sE.ap[:, :, 1:2], sinA.ap.unsqueeze(2), [sinA], [sE])
    p_re, p_im = cosA, sinA
    pw = [(T16("pwr%d" % i), T16("pwi%d" % i)) for i in range(7)]
    n = 2
    for i in range(7):
        q_re, q_im = pw[i]
        self.tt(V, t1.ap, p_re.ap, p_re.ap, ALU.mult, [p_re], [t1])
        self.tt(V, t2.ap, p_im.ap, p_im.ap, ALU.mult, [p_im], [t2])
        self.tt(V, q_re.ap, t1.ap, t2.ap, ALU.subtract, [t1, t2], [q_re])
        self.tt(V, t1.ap, p_re.ap, p_im.ap, ALU.mult, [p_re, p_im], [t1])
        self.ts(V, q_im.ap, t1.ap, 2.0, None, ALU.mult, None, [t1], [q_im])
        if n <= 64:
            qr = q_re.ap.unsqueeze(2).broadcast_to([128, 16, n])
            qi = q_im.ap.unsqueeze(2).broadcast_to([128, 16, n])
            sc_, ss_ = cE.ap[:, :, 0:n], sE.ap[:, :, 0:n]
            self.tt(V, u[0].ap[:, :, 0:n], sc_, qr, ALU.mult, [cE, q_re], [u[0]])
            self.tt(V, u[1].ap[:, :, 0:n], ss_, qi, ALU.mult, [sE, q_im], [u[1]])
            self.tt(V, u[2].ap[:, :, 0:n], sc_, qi, ALU.mult, [cE, q_im], [u[2]])
            self.tt(V, u[3].ap[:, :, 0:n], ss_, qr, ALU.mult, [sE, q_re], [u[3]])
            self.tt(V, cE.ap[:, :, n:2 * n], u[0].ap[:, :, 0:n], u[1].ap[:, :, 0:n], ALU.subtract, [u[0], u[1]], [cE])
            self.tt(V, sE.ap[:, :, n:2 * n], u[2].ap[:, :, 0:n], u[3].ap[:, :, 0:n], ALU.add, [u[2], u[3]], [sE])
        p_re, p_im = q_re, q_im
        n *= 2
    c128, s128 = T16("c128r"), T16("s128r")
    self.tt(V, c128.ap, p_re.ap, mag.ap, ALU.mult, [p_re, mag], [c128])
    self.tt(V, s128.ap, p_im.ap, mag.ap, ALU.mult, [p_im, mag], [s128])
    irho = T16("irho")
    self.act(irho.ap, tmp.ap, AF.Exp, [tmp], [irho], scale=-1.0)
    rho_t = ar.alloc("rho_t", [16, 128])
    self.cp(V, rho_t.ap, mag.ap.unsqueeze(2).broadcast_to([128, 16, 128]), [mag], [rho_t])
    self.op(V, partial(nc.vector.memset, rho_t.ap[:, :, 0:1], 0.0), [], [rho_t])

    BT = [ar.alloc("BT%d" % i, [16, 128], BF16) for i in range(2)]
    CT = [ar.alloc("CT%d" % i, [16, 128], BF16) for i in range(2)]
    for i in range(2):
        for hf in range(2):
            st = self.wstage.next()
            self.dma(st.ap, self.s5_BT[i][:, hf * 1024:(hf + 1) * 1024], writes=[st])
            self.cp("act", BT[i].ap.rearrange("p a b -> p (a b)")[:, hf * 1024:(hf + 1) * 1024], st.ap, [st], [BT[i]])
    cst = [self.wstage.next() for _ in range(4)]
    for i in range(2):
        for hf in range(2):
            self.dma(cst[i * 2 + hf].ap, self.s5_CT[i][:, hf * 1024:(hf + 1) * 1024], writes=[cst[i * 2 + hf]])
    for hf in range(2):
        js = slice(hf * 8, (hf + 1) * 8)
        cre = cst[hf].ap.rearrange("p (a b) -> p a b", a=8)
        cim = cst[2 + hf].ap.rearrange("p (a b) -> p a b", a=8)
        fr = f_re.ap[:, js].unsqueeze(2).broadcast_to([128, 8, 128])
        fi = f_im.ap[:, js].unsqueeze(2).broadcast_to([128, 8, 128])
        w0, w1 = u2[0].ap[:, 0:8, :], u2[1].ap[:, 0:8, :]
        self.tt(V, w0, cre, fr, ALU.mult, [cst[hf], f_re] + u, [u2[0]])
        self.tt(V, w1, cim, fi, ALU.mult, [cst[2 + hf], f_im] + u, [u2[1]])
        self.tt(V, CT[0].ap[:, js, :], w0, w1, ALU.subtract, [u2[0], u2[1]], [CT[0]])
        self.tt(V, w0, cre, fi, ALU.mult, [cst[hf], f_im], [u2[0]])
        self.tt(V, w1, cim, fr, ALU.mult, [cst[2 + hf], f_re], [u2[1]])
        self.tt(V, CT[1].ap[:, js, :], w0, w1, ALU.add, [u2[0], u2[1]], [CT[1]])
    self.ts(V, CT[1].ap, CT[1].ap, -1.0, None, ALU.mult, None, [CT[1]], [CT[1]])
    CTn0 = ar.alloc("CTn0", [16, 128], BF16)
    self.ts(V, CTn0.ap, CT[0].ap, -1.0, None, ALU.mult, None, [CT[0]], [CTn0])
    gluw = ar.alloc("gluw", [4, 512], BF16)
    self.load_w(gluw, self.glu_w, 4, 512)

    tA = Rot([ar.alloc("tA%d" % i, [512]) for i in range(2)])
    tB = Rot([ar.alloc("tB%d" % i, [512]) for i in range(1)])
    ygbs = Rot([ar.alloc("ygb%d" % i, [4, 512], BF16) for i in range(1)])
    glub = CL["glu_b"][0]
    s5d = CL["s5_d"][0]

    def glu(y_t, c, n, uTc=None):
        ygb = ygbs.next()
        for ft in range(4):
            a, bq = tA.next(), tB.next()
            yv = y_t.ap[:, ft, 0:n]
            if uTc is not None:
                self.stt(yv, uTc.ap[:, ft, 0:n], self.consts.ap[:, s5d + ft: s5d + ft + 1], yv, ALU.mult, ALU.add,
                         [uTc, self.consts, y_t], [y_t])
            self.act(a.ap[:, 0:n], yv, AF.Square, [y_t], [a])
            self.ts("pool", bq.ap[:, 0:n], a.ap[:, 0:n], 0.044715, 1.0, ALU.mult, ALU.add, [a], [bq])
            self.tt("pool", bq.ap[:, 0:n], bq.ap[:, 0:n], yv, ALU.mult, [bq, y_t], [bq])
            self.act(a.ap[:, 0:n], bq.ap[:, 0:n], AF.Sigmoid, [bq], [a], scale=1.5957691216057308)
            self.tt("dve", ygb.ap[:, ft, 0:n], a.ap[:, 0:n], yv, ALU.mult, [a, y_t], [ygb])
        for mo in range(4):
            b = self.pbank()
            for k in range(4):
                self.mm(b.ap[:, 0:n], gluw.ap[:, k, mo * 128:(mo + 1) * 128], ygb.ap[:, k, 0:n], k == 0, k == 3,
                        [gluw, ygb], [b])
            a = tA.next()
            self.act(a.ap[:, 0:n], b.ap[:, 0:n], AF.Sigmoid, [b, self.consts], [a],
                     bias=self.consts.ap[:, glub + mo: glub + mo + 1])
            self.tt("pool", yTs[c].ap[:, mo, 0:n], ygb.ap[:, mo, 0:n], a.ap[:, 0:n], ALU.mult, [ygb, a], [yTs[c]])

    ysb = Rot(ysb_t)
    mloop = ar.mark()
    tq = [ar.alloc("tq%d" % i, [512]) for i in range(2)]
    tq = tq + tq
    bt = [ar.alloc("bt%d" % i, [512]) for i in range(2)]
    sts = Rot([(ar.alloc("str%d" % i, [512]), ar.alloc("sti%d" % i, [512])) for i in range(2)])
    sbs = Rot([tuple(ar.alloc("spr%d_%d" % (i, q), [512], BF16) for q in range(4)) for i in range(2)])
    pc = [ar.alloc("pc%d" % i, [4]) for i in range(4)]
    pcv = [ar.alloc("pcv%d" % i, [4]) for i in range(2)]

    carry = [Tile("s5c%d" % ft, self.s5carry.ap[:, :, ft * 4:(ft + 1) * 4]) for ft in range(4)]
    flat = lambda t, ft: t.ap[:, ft * 4:(ft + 1) * 4, :].rearrange("p a b -> p (a b)")
    y_t = None
    pending = None
    for cc in range(SEGT // 128):
        c, off = cc // 4, (cc % 4) * 128
        if cc % 4 == 0:
            y_t = ysb.next()
        for ft in range(4):
            bre, bim = self.pbank(), self.pbank()
            for jl in range(4):
                j = ft * 4 + jl
                self.mm(bre.ap[:, jl * 128:(jl + 1) * 128], BT[0].ap[:, j, :], uT[c].ap[:, ft, off:off + 128], True, True,
                        [BT[0], uT[c]], [bre])
                self.mm(bim.ap[:, jl * 128:(jl + 1) * 128], BT[1].ap[:, j, :], uT[c].ap[:, ft, off:off + 128], True, True,
                        [BT[1], uT[c]], [bim])
            ce, se = flat(cE, ft), flat(sE, ft)
            self.tt(V, bt[0].ap, bre.ap, ce, ALU.mult, [bre, cE], [bt[0]])
            self.tt(V, bt[1].ap, bim.ap, ce, ALU.mult, [bim, cE], [bt[1]])
            self.tt(V, tq[0].ap, bim.ap, se, ALU.mult, [bim, sE], [tq[0]])
            self.tt(V, tq[1].ap, bre.ap, se, ALU.mult, [bre, sE], [tq[1]])
            self.tt(V, bt[0].ap, bt[0].ap, tq[0].ap, ALU.add, [bt[0], tq[0]], [bt[0]])
            self.tt(V, bt[1].ap, bt[1].ap, tq[1].ap, ALU.subtract, [bt[1], tq[1]], [bt[1]])
            st_r, st_i = sts.next()
            for ri in range(2):
                b0 = bt[ri].ap.rearrange("p (a b) -> p a b", a=4)[:, :, 0]
                self.tt(V, b0, b0, carry[ft].ap[:, ri, :], ALU.add, [bt[ri], carry[ft]], [bt[ri]])
            for ri, stt_ in ((0, st_r), (1, st_i)):
                self.op(V, partial(nc.vector.tensor_tensor_scan, out=stt_.ap, data0=flat(rho_t, ft), data1=bt[ri].ap,
                                   initial=0.0, op0=ALU.mult, op1=ALU.add), [rho_t, bt[ri]], [stt_])
            sr = st_r.ap.rearrange("p (a b) -> p a b", a=4)[:, :, 127]
            si = st_i.ap.rearrange("p (a b) -> p a b", a=4)[:, :, 127]
            cc_, ss_ = c128.ap[:, ft * 4:(ft + 1) * 4], s128.ap[:, ft * 4:(ft + 1) * 4]
            P = "pool"
            self.tt(P, pc[0].ap, sr, cc_, ALU.mult, [st_r, c128], [pc[0]])
            self.tt(P, pc[1].ap, si, ss_, ALU.mult, [st_i, s128], [pc[1]])
            self.tt(P, pc[2].ap, sr, ss_, ALU.mult, [st_r, s128], [pc[2]])
            self.tt(P, pc[3].ap, si, cc_, ALU.mult, [st_i, c128], [pc[3]])
            self.tt(P, carry[ft].ap[:, 0, :], pc[0].ap, pc[1].ap, ALU.subtract, [pc[0], pc[1]], [carry[ft]])
            self.tt(P, carry[ft].ap[:, 1, :], pc[2].ap, pc[3].ap, ALU.add, [pc[2], pc[3]], [carry[ft]])
            p1, p2, p3, p4 = sbs.next()
            self.tt(V, p1.ap, st_r.ap, ce, ALU.mult, [st_r, cE], [p1])
            self.tt(V, p2.ap, st_i.ap, se, ALU.mult, [st_i, sE], [p2])
            self.tt(P, p3.ap, st_r.ap, se, ALU.mult, [st_r, sE], [p3])
            self.tt(P, p4.ap, st_i.ap, ce, ALU.mult, [st_i, cE], [p4])
            def fin(ft=ft, ps_=(p1, p2, p3, p4), y_t=y_t, off=off, c=c, last=(cc % 4 == 3 and ft == 3)):
                yb = self.pbank()
                ws_ = (CT[0], CTn0, CT[1], CT[1])
                for jl in range(4):
                    j = ft * 4 + jl
                    for q in range(4):
                        self.mm(yb.ap[:, 0:128], ws_[q].ap[:, j, :], ps_[q].ap[:, jl * 128:(jl + 1) * 128],
                                jl == 0 and q == 0, jl == 3 and q == 3, [ws_[q], ps_[q]], [yb])
                self.cp("act", y_t.ap[:, ft, off:off + 128], yb.ap[:, 0:128], [yb] + u + u2, [y_t])
                if last:
                    glu(y_t, c, 512, uT[c])
            if pending is not None:
                pending()
            pending = fin
    if pending is not None:
        pending()

    if seg == NSEG - 1:
        fin = [T16("fin_re"), T16("fin_im")]
        g_re, g_im = T16("g_re"), T16("g_im")
        crt, cit = T16("crt"), T16("cit")
        self.tt(V, crt.ap, self.s5carry.ap[:, 0, :], irho.ap, ALU.mult, carry + [irho], [crt])
        self.tt(V, cit.ap, self.s5carry.ap[:, 1, :], irho.ap, ALU.mult, carry + [irho], [cit])
        cr = crt.ap
        ci = cit.ap
        carry = carry + [crt, cit]
        self.tt(V, t1.ap, cr, cosA.ap, ALU.mult, carry + [cosA], [t1])
        self.tt(V, t2.ap, ci, sinA.ap, ALU.mult, carry + [sinA], [t2])
        self.tt(V, g_re.ap, t1.ap, t2.ap, ALU.add, [t1, t2], [g_re])
        self.tt(V, t1.ap, ci, cosA.ap, ALU.mult, carry + [cosA], [t1])
        self.tt(V, t2.ap, cr, sinA.ap, ALU.mult, carry + [sinA], [t2])
        self.tt(V, g_im.ap, t1.ap, t2.ap, ALU.subtract, [t1, t2], [g_im])
        self.tt(V, t1.ap, g_re.ap, f_re.ap, ALU.mult, [g_re, f_re], [t1])
        self.tt(V, t2.ap, g_im.ap, f_im.ap, ALU.mult, [g_im, f_im], [t2])
        self.tt(V, fin[0].ap, t1.ap, t2.ap, ALU.subtract, [t1, t2], [fin[0]])
        self.tt(V, t1.ap, g_re.ap, f_im.ap, ALU.mult, [g_re, f_im], [t1])
        self.tt(V, t2.ap, g_im.ap, f_re.ap, ALU.mult, [g_im, f_re], [t2])
        self.tt(V, fin[1].ap, t1.ap, t2.ap, ALU.add, [t1, t2], [fin[1]])
        fo = ar.alloc("fo", [2, 128])
        for ri, dst in ((0, self.s5re_p), (1, self.s5im_p)):
            b = self.pbank()
            self.tr(b.ap[0:16, 0:128], fin[ri].ap, ident, [fin[ri], self.consts], [b])
            self.cp("act", fo.ap[0:16, ri, :], b.ap[0:16, 0:128], [b], [fo])
            self.dma(dst, fo.ap[0:16, ri, :], reads=[fo])

    if seg == 0 and not self.debug.get("no_sample_mix"):
        self.sch.barrier()
        ar.release(mloop)
        self.s5_sample(uT[2], BT, CT, f_re, f_im, if_re, if_im, ab_re, ab_im, glu, t1, t2)


Prog.s5_part = _s5_part


def _s5_sample(self, uTs, BT, CT, f_re, f_im, if_re, if_im, ab_re, ab_im, glu, t1, t2):
    nc = self.nc
    ar = self.arena
    ident = self.C("ident")
    V = "dve"
    s5d = CL["s5_d"][0]
    st = [ar.alloc("sst%d" % ri, [16, 16]) for ri in range(2)]
    xs = [ar.alloc("sxs%d" % ri, [16, 16]) for ri in range(2)]
    sin_rot = Rot([ar.alloc("s5in%d" % i, [512]) for i in range(1)])
    for ri, src in ((0, self.s5re0), (1, self.s5im0)):
        b = self.pbank()
        for q in range(4):
            t = sin_rot.next()
            self.dma(t.ap[0:NSS, :], src[:, q * 512:(q + 1) * 512], writes=[t])
            for jl in range(4):
                j = q * 4 + jl
                self.tr(b.ap[:, j * 16:(j + 1) * 16], t.ap[0:NSS, jl * 128:(jl + 1) * 128], ident[0:NSS, 0:NSS],
                        [t, self.consts], [b])
        self.cp("act", st[ri].ap, b.ap[:, 0:256].rearrange("p (a b) -> p a b", a=16), [b], [st[ri]])
    v = [ar.alloc("sv%d" % i, [16, 16]) for i in range(4)]
    bc = lambda t: t.ap.unsqueeze(2).broadcast_to([128, 16, 16])

    def cmul(o_re, o_im, a_re, a_im, b_re, b_im, rd):
        self.tt(V, v[0].ap, a_re.ap, b_re, ALU.mult, [a_re] + rd, [v[0]])
        self.tt(V, v[1].ap, a_im.ap, b_im, ALU.mult, [a_im] + rd, [v[1]])
        self.tt(V, v[2].ap, a_re.ap, b_im, ALU.mult, [a_re] + rd, [v[2]])
        self.tt(V, v[3].ap, a_im.ap, b_re, ALU.mult, [a_im] + rd, [v[3]])
        self.tt(V, o_re.ap, v[0].ap, v[1].ap, ALU.subtract, [v[0], v[1]], [o_re])
        self.tt(V, o_im.ap, v[2].ap, v[3].ap, ALU.add, [v[2], v[3]], [o_im])

    cmul(xs[0], xs[1], st[0], st[1], bc(if_re), bc(if_im), [if_re, if_im])
    braw = [ar.alloc("braw%d" % ri, [16, NSMP]) for ri in range(2)]
    for ft in range(4):
        bre, bim = self.pbank(), self.pbank()
        for jl in range(4):
            j = ft * 4 + jl
            self.mm(bre.ap[:, jl * 64:(jl + 1) * 64], BT[0].ap[:, j, :], uTs.ap[:, ft, 0:NSMP], True, True, [BT[0], uTs], [bre])
            self.mm(bim.ap[:, jl * 64:(jl + 1) * 64], BT[1].ap[:, j, :], uTs.ap[:, ft, 0:NSMP], True, True, [BT[1], uTs], [bim])
        self.cp("act", braw[0].ap[:, ft * 4:(ft + 1) * 4, :], bre.ap[:, 0:256].rearrange("p (a b) -> p a b", a=4), [bre], [braw[0]])
        self.cp("act", braw[1].ap[:, ft * 4:(ft + 1) * 4, :], bim.ap[:, 0:256].rearrange("p (a b) -> p a b", a=4), [bim], [braw[1]])
    ssb = [ar.alloc("ssb%d" % ri, [16, NSMP], BF16) for ri in range(2)]
    nx = [ar.alloc("snx%d" % ri, [16, 16]) for ri in range(2)]
    for tau in range(4):
        cmul(nx[0], nx[1], xs[0], xs[1], bc(ab_re), bc(ab_im), [ab_re, ab_im])
        for ri in range(2):
            bv = braw[ri].ap.rearrange("p a (s t) -> p a s t", t=4)[:, :, :, tau]
            self.tt(V, xs[ri].ap, nx[ri].ap, bv, ALU.add, [nx[ri], braw[ri]], [xs[ri]])
            self.cp("act", ssb[ri].ap.rearrange("p a (s t) -> p a s t", t=4)[:, :, :, tau], xs[ri].ap, [xs[ri]], [ssb[ri]])
    y_s = ar.alloc("y_s5s", [4, NSMP])
    for ft in range(4):
        yb = self.pbank()
        for jl in range(4):
            j = ft * 4 + jl
            self.mm(yb.ap[:, 0:NSMP], CT[0].ap[:, j, :], ssb[0].ap[:, j, :], jl == 0, False, [CT[0], ssb[0]], [yb])
            self.mm(yb.ap[:, 0:NSMP], CT[1].ap[:, j, :], ssb[1].ap[:, j, :], False, jl == 3, [CT[1], ssb[1]], [yb])
        self.stt(y_s.ap[:, ft, :], uTs.ap[:, ft, 0:NSMP], self.consts.ap[:, s5d + ft: s5d + ft + 1], yb.ap[:, 0:NSMP],
                 ALU.mult, ALU.add, [uTs, self.consts, yb], [y_s])
    glu(y_s, 2, NSMP)
    cmul(st[0], st[1], xs[0], xs[1], bc(f_re), bc(f_im), [f_re, f_im])
    so = sin_rot
    for ri, dst in ((0, self.s5re_s), (1, self.s5im_s)):
        for q in range(4):
            b = self.pbank()
            for jl in range(4):
                j = q * 4 + jl
                self.tr(b.ap[0:NSS, jl * 128:(jl + 1) * 128], st[ri].ap[:, j, :], ident, [st[ri], self.consts], [b])
            o = so.next()
            self.cp("act", o.ap[0:NSS, :], b.ap[0:NSS, :], [b], [o])
            self.dma(dst[:, q * 512:(q + 1) * 512], o.ap[0:NSS, :], reads=[o])


Prog.s5_sample = _s5_sample


def _ret_part(self, seg, yTr):
    nc = self.nc
    ar = self.arena
    ident = self.C("ident")
    chunks = self.chunks(seg)
    W = W0 if seg == 0 else SEGT
    V = "dve"
    sinT = ar.alloc("sinT", [W0])
    cosT = ar.alloc("cosT", [W0])
    mt = ar.mark()
    posf = ar.alloc("posf", [W0])
    rr = ar.alloc("rr", [W0])
    tmp = ar.alloc("rtmp", [W0])
    self.dma(posf.ap[:, 0:SEGT], self.cpos_d, writes=[posf])
    if seg > 0:
        self.ts(V, posf.ap[:, 0:SEGT], posf.ap[:, 0:SEGT], float(seg * SEGT), None, ALU.add, None, [posf], [posf])
    if seg == 0:
        posi = ar.alloc("posi", [NSS], I32)
        posff = ar.alloc("posff", [NSS])
        self.dma(posi.ap, self.pos_d, writes=[posi])
        self.cp(V, posff.ap, posi.ap, [posi], [posff])
        pv = posf.ap[:, SEGT:W0].rearrange("p (s t) -> p s t", t=4)
        for tau in range(4):
            self.ts(V, pv[:, :, tau], posff.ap, float(tau), None, ALU.add, None, [posff, posf], [posf])
    invf = self.C("inv_freq")
    self.ts(V, posf.ap[:, 0:W], posf.ap[:, 0:W], invf, None, ALU.mult, None, [posf, self.consts], [posf])
    rr_v = Tile("rr_v", rr.ap[:, 0:W])
    tmp_v = Tile("tmp_v", tmp.ap[:, 0:W])
    self.range_reduce(V, rr_v, posf.ap[:, 0:W], tmp_v, [posf], None)
    sin_v = Tile("sin_v", sinT.ap[:, 0:W])
    cos_v = Tile("cos_v", cosT.ap[:, 0:W])
    self.sincos(rr_v, sin_v, cos_v, tmp_v, sin_scale=self.C("sgn"), extra_reads=[self.consts])
    sinT, cosT = sin_v, cos_v
    self.sch.barrier()
    ar.release(mt)

    Wv = ar.alloc("Wv", [KT, 512], BF16)
    Wg = ar.alloc("Wg", [KT, 512], BF16)
    self.load_w(Wv, self.w_in_ab[:, 1536:2048], KT, 512)
    self.load_w(Wg, self.w_in_ab[:, 2048:2560], KT, 512)
    wqk = Rot([ar.alloc("wqk%d" % i, [KT, 128], BF16) for i in range(4)])
    comp = {}

    def load_sw(dst, c0):
        srcv = self.w_in_ab.rearrange("(k p) n -> p k n", p=128)
        st = self.wstage.next()
        if st.name not in comp:
            comp[st.name] = Tile(st.name + "_c")
        st2 = comp[st.name]
        sv = st.ap[:, 0:KT * 128].rearrange("p (k n) -> p k n", n=128)
        self.dma(sv[:, :, 0:64], srcv[:, :, c0 + 64:c0 + 128], writes=[st])
        self.dma(sv[:, :, 64:128], srcv[:, :, c0:c0 + 64], writes=[st2])
        self.cp("act", dst.ap, sv, [st, st2], [dst])

    ws = [512, 512, NSMP]
    qT = ar.alloc("qT", [4, 512], BF16)
    kT = ar.alloc("kT", [4, 512], BF16)
    qdT = ar.alloc("qdT", [4, 512], BF16)
    qs32 = ar.alloc("qs32", [4, NSMP])
    qd32 = ar.alloc("qd32", [4, NSMP])
    rt = Rot([ar.alloc("rt%d" % i, [512]) for i in range(2)])
    v_toks = Rot([ar.alloc("vtok%d" % i, [512], BF16) for i in range(2)])
    sgs = Rot([ar.alloc("sg%d" % i, [512]) for i in range(2)])
    scms = Rot([ar.alloc("scm%d" % i, [4, 128], BF16) for i in range(2)])
    kds = Rot([ar.alloc("kd%d" % i, [4, 128], BF16) for i in range(2)])
    ytoks = Rot([ar.alloc("ytok%d" % i, [512], BF16) for i in range(2)])
    junk = ar.alloc("junk", [512])
    sss = Rot([ar.alloc("ss%d" % i, [4]) for i in range(2)])
    maskT = self.C("maskT").rearrange("p (h i) -> p h i", h=4)
    maskS = self.C("maskS", 64).rearrange("p (h i) -> p h i", h=4)
    qdec = self.C("qdec").rearrange("p (h i) -> p h i", h=4)
    qdecS = self.C("qdecS").rearrange("p (h i) -> p h i", h=4)
    kdec = self.C("kdec")
    kdecS = self.C("kdecS", 64)
    g128 = self.C("g128")
    pSs = Rot([ar.alloc("rpS%d" % i, [4, 128]) for i in range(1)])
    identb = self.identb
    eps_ap = self.C("eps")

    def epilogue(ob, sg, rows, c, off):
        ss = sss.next()
        for h in range(4):
            self.act(junk.ap[0:rows, h * 128:(h + 1) * 128], ob.ap[0:rows, h * 128:(h + 1) * 128], AF.Square, [ob], [junk, ss],
                     accum_out=ss.ap[0:rows, h:h + 1])
        self.act(ss.ap[0:rows, :], ss.ap[0:rows, :], AF.Ln, [junk, self.consts], [ss], bias=eps_ap[0:rows, :], scale=1.0 / 128)
        self.act(ss.ap[0:rows, :], ss.ap[0:rows, :], AF.Exp, [ss], [ss], scale=-0.5)
        y_tok = ytoks.next()
        for h in range(4):
            self.stt(y_tok.ap[0:rows, h * 128:(h + 1) * 128], ob.ap[0:rows, h * 128:(h + 1) * 128], ss.ap[0:rows, h:h + 1],
                     sg.ap[0:rows, h * 128:(h + 1) * 128], ALU.mult, ALU.mult, [ob, ss, sg], [y_tok])
        yb = self.pbank()
        ybf = yb.ap.bitcast(BF16)
        for h in range(4):
            self.tr(ybf[:, h * 128:h * 128 + rows], y_tok.ap[0:rows, h * 128:(h + 1) * 128], identb.ap[0:rows, 0:rows],
                    [y_tok, identb], [yb])
        self.cp("act", yTr[c].ap[:, :, off:off + rows], ybf[:, 0:512].rearrange("p (h t) -> p h t", h=4)[:, :, 0:rows],
                [yb], [yTr[c]])

    for (c, c0, n) in chunks:
        for which, base, dst in (("q", 512, qT), ("k", 1024, kT)):
            for h in range(4):
                wn, wsw = wqk.next(), wqk.next()
                st = self.wstage.next()
                sv = st.ap[:, 0:KT * 128].rearrange("p (k n) -> p k n", n=128)
                c0w = base + h * 128
                self.dma(sv, self.w_in_ab.rearrange("(k p) n -> p k n", p=128)[:, :, c0w:c0w + 128], writes=[st])
                self.cp("act", wn.ap, sv, [st], [wn])
                self.cp("act", wsw.ap[:, :, 0:64], sv[:, :, 64:128], [st], [wsw])
                self.cp("act", wsw.ap[:, :, 64:128], sv[:, :, 0:64], [st], [wsw])
                pn, psw = self.pbank(), self.pbank()
                for k in range(KT):
                    self.mm(pn.ap[:, 0:n], wn.ap[:, k, :], self.hT[c].ap[:, k, 0:n], k == 0, k == KT - 1, [wn, self.hT[c]], [pn])
                for k in range(KT):
                    self.mm(psw.ap[:, 0:n], wsw.ap[:, k, :], self.hT[c].ap[:, k, 0:n], k == 0, k == KT - 1, [wsw, self.hT[c]], [psw])
                a, b2 = rt.next(), rt.next()
                self.tt(V, a.ap[:, 0:n], pn.ap[:, 0:n], cosT.ap[:, c0:c0 + n], ALU.mult, [pn, cosT], [a])
                self.tt(V, b2.ap[:, 0:n], psw.ap[:, 0:n], sinT.ap[:, c0:c0 + n], ALU.mult, [psw, sinT], [b2])
                if c == 2 and which == "q":
                    self.tt(V, qs32.ap[:, h, :], a.ap[:, 0:n], b2.ap[:, 0:n], ALU.add, [a, b2], [qs32])
                    self.cp("act", dst.ap[:, h, 0:n], qs32.ap[:, h, :], [qs32], [dst])
                else:
                    self.tt(V, dst.ap[:, h, 0:n], a.ap[:, 0:n], b2.ap[:, 0:n], ALU.add, [a, b2], [dst])
        if c < 2:
            for h in range(4):
                self.tt("pool", qdT.ap[:, h, :].rearrange("p (t i) -> p t i", t=4),
                        qT.ap[:, h, :].rearrange("p (t i) -> p t i", t=4),
                        qdec[:, h, :].unsqueeze(1).broadcast_to([128, 4, 128]), ALU.mult, [qT, self.consts], [qdT])
        else:
            self.tt("pool", qd32.ap, qs32.ap, qdecS, ALU.mult, [qs32, self.consts], [qd32])
        ntile = n // 128 if c < 2 else 1

        def pre(t, c=c):
            off = t * 128
            rows = 128 if c < 2 else NSMP
            vb, gb = self.pbank(), self.pbank()
            for k in range(KT):
                self.mm(vb.ap[0:rows, :], self.hT[c].ap[:, k, off:off + rows], Wv.ap[:, k, :], k == 0, k == KT - 1,
                        [self.hT[c], Wv], [vb])
            for k in range(KT):
                self.mm(gb.ap[0:rows, :], self.hT[c].ap[:, k, off:off + rows], Wg.ap[:, k, :], k == 0, k == KT - 1,
                        [self.hT[c], Wg], [gb])
            v_tok, sg = v_toks.next(), sgs.next()
            self.cp("act", v_tok.ap[0:rows, :], vb.ap[0:rows, :], [vb], [v_tok])
            self.act(sg.ap[0:rows, :], gb.ap[0:rows, :], AF.Silu, [gb], [sg])
            sb_ = self.pbank()
            for h in range(4):
                self.mm(sb_.ap[0:rows, h * 128:h * 128 + rows], kT.ap[:, h, off:off + rows], qT.ap[:, h, off:off + rows], True, True,
                        [kT, qT], [sb_])
            scm = scms.next()
            mk = maskT if c < 2 else maskS
            self.tt(V, scm.ap[0:rows, :, 0:rows], sb_.ap[0:rows, :].rearrange("p (h i) -> p h i", h=4)[:, :, 0:rows], mk,
                    ALU.mult, [sb_, self.consts], [scm])
            kb = self.pbank()
            kbf = kb.ap.bitcast(BF16)
            for h in range(4):
                self.tr(kbf[0:rows, h * 128:(h + 1) * 128], kT.ap[:, h, off:off + rows], identb.ap, [kT, identb], [kb])
            kd = kds.next()
            kdc = (kdec if c < 2 else kdecS).unsqueeze(2).broadcast_to([rows, 4, 128])
            self.tt(V, kd.ap[0:rows], kbf[0:rows, 0:512].rearrange("p (h d) -> p h d", h=4), kdc, ALU.mult,
                    [kb, self.consts], [kd])
            return v_tok, sg, scm, kd

        cur = pre(0)
        for t in range(ntile):
            off = t * 128
            rows = 128 if c < 2 else NSMP
            v_tok, sg, scm, kd = cur
            if c < 2:
                pS = pSs.next()
                self.tt("pool", pS.ap, self.Sret.ap, g128.unsqueeze(2).broadcast_to([128, 4, 128]), ALU.mult,
                        [self.Sret, self.consts], [pS])
                ob = self.pbank()
                for h in range(4):
                    self.mm(ob.ap[:, h * 128:(h + 1) * 128], scm.ap[:, h, :], v_tok.ap[:, h * 128:(h + 1) * 128], True, False,
                            [scm, v_tok], [ob])
                    self.mm(ob.ap[:, h * 128:(h + 1) * 128], qdT.ap[:, h, off:off + 128], self.Sretb.ap[:, h, :], False, True,
                            [qdT, self.Sretb], [ob])
                kvb = self.pbank()
                for h in range(4):
                    self.mm(kvb.ap[:, h * 128:(h + 1) * 128], kd.ap[:, h, :], v_tok.ap[:, h * 128:(h + 1) * 128], True, True,
                            [kd, v_tok], [kvb])
                self.tt(V, self.Sretb.ap.rearrange("p a b -> p (a b)"), pS.ap.rearrange("p a b -> p (a b)"), kvb.ap, ALU.add,
                        [pS, kvb], [self.Sretb])
                self.tt(V, self.Sret.ap.rearrange("p a b -> p (a b)"), pS.ap.rearrange("p a b -> p (a b)"), kvb.ap, ALU.add,
                        [pS, kvb], [self.Sret])
                if t + 1 < ntile:
                    cur = pre(t + 1)
                epilogue(ob, sg, 128, c, off)
            else:
                otb = [self.ps[h] for h in range(4)]
                locb = Rot([self.ps[4], self.ps[5], self.ps[6], self.ps[7]])
                for h in range(4):
                    self.mm(otb[h].ap[:, 0:NSMP], v_tok.ap[0:NSMP, h * 128:(h + 1) * 128], scm.ap[0:NSMP, h, 0:NSMP], True, False,
                            [v_tok, scm], [otb[h]])
                s0rot = Rot([ar.alloc("s0r%d" % i, [4, 128]) for i in range(3)])
                sorot = Rot([ar.alloc("sor%d" % i, [4, 128]) for i in range(2)])
                vms = Rot([ar.alloc("vm%d" % i, [512], BF16) for i in range(2)])
                smo = CL["seqmask"][0]
                nxtS0 = s0rot.next()
                self.dma(nxtS0.ap, self.ret0[0].rearrange("h p d -> p h d"), writes=[nxtS0])
                for s in range(NSS):
                    S0s = nxtS0
                    if s + 1 < NSS:
                        nxtS0 = s0rot.next()
                        self.dma(nxtS0.ap, self.ret0[s + 1].rearrange("h p d -> p h d"), writes=[nxtS0])
                    for h in range(4):
                        self.mm(otb[h].ap[:, 4 * s:4 * s + 4], S0s.ap[:, h, :], qd32.ap[:, h, 4 * s:4 * s + 4], False, s == NSS - 1,
                                [S0s, qd32], [otb[h]])
                    vm = vms.next()
                    self.ts(V, vm.ap[0:NSMP, :], v_tok.ap[0:NSMP, :], self.consts.ap[0:NSMP, smo + s:smo + s + 1], None,
                            ALU.mult, None, [v_tok, self.consts], [vm])
                    kvb = locb.next()
                    for h in range(4):
                        self.mm(kvb.ap[:, h * 128:(h + 1) * 128], kd.ap[0:NSMP, h, :], vm.ap[0:NSMP, h * 128:(h + 1) * 128], True, True,
                                [kd, vm], [kvb])
                    So = sorot.next()
                    self.tt("pool", So.ap, S0s.ap, self.C("g4").unsqueeze(2).broadcast_to([128, 4, 128]), ALU.mult,
                            [S0s, self.consts], [So])
                    sov = So.ap.rearrange("p a b -> p (a b)")
                    self.tt(V, sov, sov, kvb.ap, ALU.add, [So, kvb], [So])
                    self.dma(self.ret_s[s].rearrange("h p d -> p h d"), So.ap, reads=[So])
                oT32 = ar.alloc("oT32", [4, NSMP])
                for h in range(4):
                    self.cp("act", oT32.ap[:, h, :], otb[h].ap[:, 0:NSMP], [otb[h]], [oT32])
                ob = locb.next()
                self._pb = 0
                for h in range(4):
                    self.tr(ob.ap[0:NSMP, h * 128:(h + 1) * 128], oT32.ap[:, h, :], ident, [oT32, self.consts], [ob])
                epilogue(ob, sg, NSMP, c, 0)
    if seg == NSEG - 1:
        self.dma(self.ret_p.rearrange("h p d -> p h d"), self.Sret.ap, reads=[self.Sret])


Prog.ret_part = _ret_part


def _mixer_c(self, seg):
    nc = self.nc
    ar = self.arena
    m0 = ar.mark()
    chunks = self.chunks(seg)
    identb = self.identb
    V = "dve"
    ws = [512, 512, NSMP]
    yT = [ar.alloc("yTc%d" % c, [KT, ws[c]], BF16) for c in range(3)]
    m1 = ar.mark()
    self.hT = self.alloc_hT()
    self.rmsnorm(seg, 2, self.hT)
    self.sch.barrier()
    sqt = self.sq.tiles[0]
    rst_ = self.rs.tiles[0]
    xtra = sqt.ap.bitcast(F32).rearrange("p a b -> p (a b)")
    lgo = CL["hg_lg"][0]
    dlg = ar.alloc("dlg", [8])
    oml = ar.alloc("oml", [8])
    self.tt(V, dlg.ap, self.consts.ap[:, lgo:lgo + 8], self.consts.ap[:, lgo + 8:lgo + 16], ALU.subtract, [self.consts], [dlg])
    self.act(oml.ap, dlg.ap, AF.Sigmoid, [dlg], [oml])
    lnoml = ar.alloc("lnoml", [8])
    self.act(lnoml.ap, oml.ap, AF.Ln, [oml], [lnoml])
    one_ap = self.C("one")
    eps_ap = self.C("eps")
    nw_ap = self.C("hg_nw")
    rst = ar.alloc("rst", [512])
    rstS = ar.alloc("rstS", [NSMP])
    self.op(V, partial(nc.vector.memset, rst.ap, 1.0), [], [rst])
    self.op(V, partial(nc.vector.memset, rst.ap.rearrange("p (a b) -> p a b", b=64)[:, :, 0:1], 0.0), [], [rst])
    self.op(V, partial(nc.vector.memset, rstS.ap, 1.0), [], [rstS])
    self.op(V, partial(nc.vector.memset, rstS.ap.rearrange("p (a b) -> p a b", b=4)[:, :, 0:1], 0.0), [], [rstS])
    wts = Rot([ar.alloc("wc%d" % i, [KT, 128], BF16) for i in range(8)])
    qtT = ar.alloc("qtT", [8, 512], BF16)
    ktT = ar.alloc("ktT", [8, 512], BF16)
    kkT = ar.alloc("kkT", [8, 512], BF16)
    vT = ar.alloc("vT", [8, 512], BF16)
    sgT = ar.alloc("sgT", [8, 512], BF16)
    ebls = Rot([ar.alloc("ebl%d" % i, [8, 16]) for i in range(2)])
    qs32 = ar.alloc("qs32c", [8, NSMP])
    blkA = ar.alloc("htaB", [2560])
    setA = [Tile("hta%d" % i, blkA.ap[:, i * 512:(i + 1) * 512]) for i in range(5)]
    setB = [Tile("htb%d" % i, xtra[:, i * 512:(i + 1) * 512]) for i in range(4)] + [Tile("htb4", rst_.ap)]
    tsets = [setA, setB]
    vtoks = Rot([ar.alloc("hvt%d" % i, [1024], BF16) for i in range(2)])
    kktoks = Rot([ar.alloc("hkt%d" % i, [1024], BF16) for i in range(2)])
    scms = Rot([ar.alloc("hsc%d" % i, [8, 64], BF16) for i in range(2)])
    pSs = Rot([ar.alloc("hpS%d" % i, [8, 128]) for i in range(1)])
    sqb = ar.alloc("hsq", [512], BF16)
    rstd = ar.alloc("hrstd", [512])
    otmp = ar.alloc("hotmp", [512])
    tri64 = self.C("triBD")[0:64, 0:64]
    triS = self.C("triS", 64)
    ident = self.C("ident")

    def proj(w, c, n):
        b = self.pbank()
        for k in range(KT):
            self.mm(b.ap[:, 0:n], w.ap[:, k, :], self.hT[c].ap[:, k, 0:n], k == 0, k == KT - 1, [w, self.hT[c]], [b])
        return b

    def load_head(h):
        wq, wf, wv, wg = wts.next(), wts.next(), wts.next(), wts.next()
        self.load_w(wf, self.w_in_c[:, 1024 + h * 128:1024 + (h + 1) * 128], KT, 128, cast_eng="dve")
        self.load_w(wv, self.w_in_c[:, 2048 + h * 128:2048 + (h + 1) * 128], KT, 128, cast_eng="dve")
        self.load_w(wg, self.w_in_c[:, 3072 + h * 128:3072 + (h + 1) * 128], KT, 128, cast_eng="dve")
        self.load_w(wq, self.w_in_c[:, h * 128:(h + 1) * 128], KT, 128, cast_eng="dve")
        return wq, wf, wv, wg

    def epi_rest(o_ap, o_tiles, c, o64):
        sb_ = self.pbank()
        self.mm(sb_.ap, self.onesb.ap, sqb.ap, True, True, [self.onesb, sqb], [sb_])
        self.act(rstd.ap, sb_.ap, AF.Ln, [sb_, self.consts], [rstd], bias=eps_ap, scale=1.0 / 128)
        self.act(rstd.ap, rstd.ap, AF.Exp, [rstd], [rstd], scale=-0.5)
        self.tt(V, otmp.ap, o_ap, rstd.ap, ALU.mult, o_tiles + [rstd], [otmp])
        self.tt("pool", yT[c].ap[:, :, o64:o64 + 64], otmp.ap.rearrange("p (h i) -> p h i", h=8), sgT.ap[:, :, o64:o64 + 64],
                ALU.mult, [otmp, sgT], [yT[c]])

    nxt_w = load_head(0)
    work = [(c, c0, n, h) for (c, c0, n) in chunks for h in range(8)]
    for wi, (c, c0, n, h) in enumerate(work):
        sample = (c == 2)
        blk = 64 if not sample else 4
        nb = n // blk
        if h == 0:
            ebl = ebls.next()
        wq, wf, wv, wg = nxt_w
        if wi + 1 < len(work):
            nxt_w = load_head(work[wi + 1][3])
        kf, lf, bT, enb, kt = tsets[wi % 2]
        pf = proj(wf, c, n)
        self.act(kf.ap[:, 0:n], pf.ap[:, 0:n], AF.Exp, [pf], [kf])
        pv = proj(wv, c, n)
        pg = proj(wg, c, n)
        self.act(kt.ap[:, 0:n], pg.ap[:, 0:n], AF.Exp, [pg], [kt], scale=-1.0)
        self.act(lf.ap[:, 0:n], kf.ap[:, 0:n], AF.Ln, [kf, self.consts], [lf], bias=one_ap)
        self.act(kt.ap[:, 0:n], kt.ap[:, 0:n], AF.Ln, [kt, self.consts], [kt], bias=one_ap)
        self.act(kf.ap[:, 0:n], lf.ap[:, 0:n], AF.Exp, [lf, lnoml], [kf], bias=lnoml.ap[:, h:h + 1], scale=-1.0)
        self.act(kt.ap[:, 0:n], kt.ap[:, 0:n], AF.Exp, [kt], [kt], scale=-1.0)
        self.cp("act", vT.ap[:, h, 0:n], pv.ap[:, 0:n], [pv], [vT])
        self.act(lf.ap[:, 0:n], kf.ap[:, 0:n], AF.Ln, [kf, self.consts], [lf], bias=one_ap, scale=-1.0)
        self.stt(sgT.ap[:, h, 0:n], pg.ap[:, 0:n], nw_ap, kt.ap[:, 0:n], ALU.mult, ALU.mult, [pg, self.consts, kt], [sgT])
        rs_ap = rst.ap[:, 0:n] if not sample else rstS.ap[:, 0:n]
        self.op(V, partial(nc.vector.tensor_tensor_scan, out=bT.ap[:, 0:n], data0=rs_ap, data1=lf.ap[:, 0:n], initial=0.0,
                           op0=ALU.mult, op1=ALU.add), [rst, rstS, lf], [bT])
        pq = proj(wq, c, n)
        self.act(lf.ap[:, 0:n], bT.ap[:, 0:n], AF.Exp, [bT], [lf])
        self.act(enb.ap[:, 0:n], bT.ap[:, 0:n], AF.Exp, [bT], [enb], scale=-1.0)
        eb = lf
        ebv = eb.ap[:, 0:n].rearrange("p (a b) -> p a b", b=blk)[:, :, blk - 1]
        self.cp("pool", ebl.ap[:, h, 0:nb], ebv, [eb], [ebl])
        if sample:
            self.tt(V, qs32.ap[:, h, :], pq.ap[:, 0:n], eb.ap[:, 0:n], ALU.mult, [pq, eb], [qs32])
            self.cp("act", qtT.ap[:, h, 0:n], qs32.ap[:, h, :], [qs32], [qtT])
        else:
            self.tt(V, qtT.ap[:, h, 0:n], pq.ap[:, 0:n], eb.ap[:, 0:n], ALU.mult, [pq, eb], [qtT])
        self.tt(V, kt.ap[:, 0:n], kf.ap[:, 0:n], enb.ap[:, 0:n], ALU.mult, [kf, enb], [kt])
        self.cp("act", ktT.ap[:, h, 0:n], kt.ap[:, 0:n], [kt], [ktT])
        self.tt(V, kkT.ap[:, h, 0:n].rearrange("p (a b) -> p a b", b=blk), kt.ap[:, 0:n].rearrange("p (a b) -> p a b", b=blk),
                ebl.ap[:, h, 0:nb].unsqueeze(2).broadcast_to([128, nb, blk]), ALU.mult, [kt, ebl], [kkT])
        if h < 7:
            continue
        nt = n // 64 if not sample else 1

        def pre(t):
            o64 = t * 64
            vb, kb = self.pbank(), self.pbank()
            vbf, kbf = vb.ap.bitcast(BF16), kb.ap.bitcast(BF16)
            for hh in range(8):
                self.tr(vbf[0:64, hh * 128:(hh + 1) * 128], vT.ap[:, hh, o64:o64 + 64], identb.ap, [vT, identb], [vb])
            for hh in range(8):
                self.tr(kbf[0:64, hh * 128:(hh + 1) * 128], kkT.ap[:, hh, o64:o64 + 64], identb.ap, [kkT, identb], [kb])
            v_tok, kk_tok = vtoks.next(), kktoks.next()
            self.cp("act", v_tok.ap[0:64, :], vbf[0:64, :], [vb], [v_tok])
            self.cp(V, kk_tok.ap[0:64, :], kbf[0:64, :], [kb], [kk_tok])
            sb_ = self.pbank()
            for hh in range(8):
                self.mm(sb_.ap[0:64, hh * 64:(hh + 1) * 64], ktT.ap[:, hh, o64:o64 + 64], qtT.ap[:, hh, o64:o64 + 64], True, True,
                        [ktT, qtT], [sb_])
            scm = scms.next()
            mk = (triS if sample else tri64).unsqueeze(1).broadcast_to([64, 8, 64])
            self.tt(V, scm.ap[0:64], sb_.ap[0:64, :].rearrange("p (h i) -> p h i", h=8), mk, ALU.mult, [sb_, self.consts], [scm])
            return v_tok, kk_tok, scm

        if not sample:
            cur = pre(0)
            for t in range(nt):
                o64 = t * 64
                v_tok, kk_tok, scm = cur
                pS = pSs.next()
                self.tt("pool", pS.ap, self.Shg.ap, ebl.ap[:, :, t:t + 1].broadcast_to([128, 8, 128]), ALU.mult,
                        [self.Shg, ebl], [pS])
                ob = self.pbank()
                for hh in range(8):
                    self.mm(ob.ap[:, hh * 64:(hh + 1) * 64], v_tok.ap[0:64, hh * 128:(hh + 1) * 128], scm.ap[0:64, hh, :], True, False,
                            [v_tok, scm], [ob])
                    self.mm(ob.ap[:, hh * 64:(hh + 1) * 64], self.Shgb.ap[:, hh, :], qtT.ap[:, hh, o64:o64 + 64], False, True,
                            [self.Shgb, qtT], [ob])
                self.act(sqb.ap, ob.ap, AF.Square, [ob], [sqb])
                for half in range(2):
                    kvb = self.pbank()
                    for q in range(4):
                        hh = half * 4 + q
                        self.mm(kvb.ap[:, q * 128:(q + 1) * 128], kk_tok.ap[0:64, hh * 128:(hh + 1) * 128],
                                v_tok.ap[0:64, hh * 128:(hh + 1) * 128], True, True, [kk_tok, v_tok], [kvb])
                    psv = pS.ap[:, half * 4:(half + 1) * 4, :].rearrange("p a b -> p (a b)")
                    self.tt(V, self.Shgb.ap[:, half * 4:(half + 1) * 4, :].rearrange("p a b -> p (a b)"), psv, kvb.ap, ALU.add,
                            [pS, kvb], [self.Shgb])
                    self.tt(V, self.Shg.ap[:, half * 4:(half + 1) * 4, :].rearrange("p a b -> p (a b)"), psv, kvb.ap, ALU.add,
                            [pS, kvb], [self.Shg])
                if t + 1 < nt:
                    cur = pre(t + 1)
                epi_rest(ob.ap, [ob], c, o64)
        else:
            self.sch.barrier()
            big = Tile("hbig", None)
            v_tok, kk_tok, scm = pre(0)
            ob = self.ps[0]
            ib = self.ps[1]
            locb = Rot([self.ps[2], self.ps[3], self.ps[4], self.ps[5], self.ps[6], self.ps[7]])
            for hh in range(8):
                self.mm(ob.ap[:, hh * 64:(hh + 1) * 64], v_tok.ap[0:64, hh * 128:(hh + 1) * 128], scm.ap[0:64, hh, :], True, True,
                        [v_tok, scm], [ob])
            hfree = self.hT[0].ap.bitcast(F32).rearrange("p a b -> p (a b)")
            s0rot = Rot([Tile("hs0r0", blkA.ap[:, 1024:2048].rearrange("p (a b) -> p a b", a=8)),
                         Tile("hs0r1", hfree[:, 0:1024].rearrange("p (a b) -> p a b", a=8))])
            sorot = Rot([Tile("hsor0", xtra[:, 0:1024].rearrange("p (a b) -> p a b", a=8)),
                         Tile("hsor1", hfree[:, 1024:2048].rearrange("p (a b) -> p a b", a=8))])
            vms = Rot([Tile("hvm0", xtra[:, 1024:1536].bitcast(BF16))])
            smo = CL["seqmask"][0]
            nxtS0 = s0rot.next()
            self.dma(nxtS0.ap, self.hg0[0].rearrange("h p d -> p h d"), writes=[nxtS0])
            for s_ in range(NSS):
                S0s = nxtS0
                if s_ + 1 < NSS:
                    nxtS0 = s0rot.next()
                    self.dma(nxtS0.ap, self.hg0[s_ + 1].rearrange("h p d -> p h d"), writes=[nxtS0])
                for hh in range(8):
                    self.mm(ib.ap[:, hh * 64 + 4 * s_:hh * 64 + 4 * s_ + 4], S0s.ap[:, hh, :], qs32.ap[:, hh, 4 * s_:4 * s_ + 4], True, True,
                            [S0s, qs32], [ib])
                vm = vms.next()
                self.ts(V, vm.ap[0:64, :], v_tok.ap[0:64, :], self.consts.ap[0:64, smo + s_:smo + s_ + 1], None, ALU.mult, None,
                        [v_tok, self.consts], [vm])
                So = sorot.next()
                self.tt("pool", So.ap, S0s.ap, ebl.ap[:, :, s_:s_ + 1].broadcast_to([128, 8, 128]), ALU.mult, [S0s, ebl], [So])
                for half in range(2):
                    kvb = locb.next()
                    for q in range(4):
                        hh = half * 4 + q
                        self.mm(kvb.ap[:, q * 128:(q + 1) * 128], kk_tok.ap[0:64, hh * 128:(hh + 1) * 128],
                                vm.ap[0:64, hh * 128:(hh + 1) * 128], True, True, [kk_tok, vm], [kvb])
                    sov = So.ap[:, half * 4:(half + 1) * 4, :].rearrange("p a b -> p (a b)")
                    self.tt(V, sov, sov, kvb.ap, ALU.add, [So, kvb], [So])
                self.dma(self.hg_s[s_].rearrange("h p d -> p h d"), So.ap, reads=[So])
            oi = setA[0]
            self.cp("act", oi.ap, ib.ap, [ib], [oi])
            osum = setA[1]
            self.tt(V, osum.ap, ob.ap, oi.ap, ALU.add, [ob, oi], [osum])
            self._pb = 0
            self.act(sqb.ap, osum.ap, AF.Square, [osum], [sqb])
            epi_rest(osum.ap, [osum], c, 0)
    if seg == NSEG - 1:
        self.dma(self.hg_p.rearrange("h p d -> p h d"), self.Shg.ap, reads=[self.Shg])
    self.sch.barrier()
    ar.release(m1)
    wout = ar.alloc("woutc", [KT, D], BF16)
    self.load_w(wout, self.w_out_c, KT, D)
    for (c, c0, n) in chunks:
        for mo in range(KT):
            b = self.pbank()
            for k in range(KT):
                self.mm(b.ap[:, 0:n], wout.ap[:, k, mo * 128:(mo + 1) * 128], yT[c].ap[:, k, 0:n], k == 0, k == KT - 1,
                        [wout, yT[c]], [b])
            xv = self.xcols(c, mo, mo + 1)[:, 0, :]
            self.tt("dve", xv, xv, b.ap[:, 0:n], ALU.add, [self.xT[c], b], [self.xT[c]])
    self.sch.barrier()
    ar.release(m0)


Prog.mixer_c = _mixer_c
```

```python
import numpy as np
import concourse.bass as bass
import concourse.mybir as mybir
from concourse.bass_utils import run_bass_kernel_spmd

F32 = mybir.dt.float32
BF16 = mybir.dt.bfloat16
I32 = mybir.dt.int32
AF = mybir.ActivationFunctionType
ALU = mybir.AluOpType

NCORES = 8
D = 1024
KT = 8
SEQ = 2048
NSEG = 2
SEGT = SEQ // NSEG
NSMP = 64
NSS = 16
W0 = SEGT + NSMP
DFF = 2816
FT = DFF // 128
EPS = 1e-6

ENGS = ["pe", "act", "dve", "pool", "sp"]
NDSEM = 24


class Tile:
    __slots__ = ("name", "ap", "w", "r", "persist")

    def __init__(self, name, ap=None):
        self.name = name
        self.ap = ap
        self.w = None
        self.r = {}
        self.persist = False


class Sched:
    def __init__(self, nc):
        self.nc = nc
        self.ops = {e: [] for e in ENGS}
        self.serial = {e: 0 for e in ENGS}
        self.nvc = len(ENGS) + NDSEM
        self.vc = {e: [0] * self.nvc for e in ENGS}
        self.snap = {e: [None] for e in ENGS}
        self.pe_inc = set()
        self.dsem_val = [0] * NDSEM
        self.dsem_next = 0
        self.eidx = {e: i for i, e in enumerate(ENGS)}
        self.out_dma = {}
        self.sp_barrier = None

    def _need(self, eng, dep, waits, raw=True):
        vc = self.vc[eng]
        if dep[0] == "e":
            _, e2, s2 = dep
            if e2 == eng and not raw and eng == "pe":
                return
            i2 = self.eidx[e2]
            if vc[i2] >= s2:
                return
            waits.append(dep)
            if e2 == "pe":
                self.pe_inc.add(s2)
            sn = self.snap[e2][s2]
            for i in range(self.nvc):
                if sn[i] > vc[i]:
                    vc[i] = sn[i]
            if vc[i2] < s2:
                vc[i2] = s2
        else:
            _, k, v = dep
            i2 = len(ENGS) + k
            if vc[i2] >= v:
                return
            waits.append(dep)
            vc[i2] = v

    def op(self, eng, fn, reads=(), writes=()):
        waits = []
        for t in reads:
            if t.w is not None:
                self._need(eng, t.w, waits)
        for t in writes:
            if t.w is not None:
                self._need(eng, t.w, waits, raw=False)
            for d in list(t.r.values()):
                self._need(eng, d, waits, raw=False)
        self.serial[eng] += 1
        s = self.serial[eng]
        tok = ("e", eng, s)
        for t in writes:
            t.w = tok
            t.r = {}
        for t in reads:
            t.r[("e", eng)] = tok
        sn = list(self.vc[eng])
        sn[self.eidx[eng]] = s
        self.snap[eng].append(tuple(sn))
        self.ops[eng].append((fn, waits, s, None))
        return tok

    def dma(self, fn, reads=(), writes=()):
        eng = "sp"
        waits = []
        if self.sp_barrier is not None and any(not t.persist for t in writes):
            for dep in self.sp_barrier:
                self._need(eng, dep, waits)
            self.sp_barrier = None
        for t in reads:
            if t.w is not None:
                self._need(eng, t.w, waits)
        for t in writes:
            if t.w is not None:
                self._need(eng, t.w, waits)
            for d in list(t.r.values()):
                self._need(eng, d, waits)
        k = self.dsem_next
        self.dsem_next = (k + 1) % NDSEM
        if self.dsem_val[k] > 0:
            self._need(eng, ("d", k, self.dsem_val[k]), waits)
        self.dsem_val[k] += 16
        tok = ("d", k, self.dsem_val[k])
        if reads:
            self.out_dma[k] = self.dsem_val[k]
        self.serial[eng] += 1
        s = self.serial[eng]
        for t in writes:
            t.w = tok
            t.r = {}
        for t in reads:
            t.r[("d", k)] = tok
        self.snap[eng].append(tuple(self.vc[eng]))
        self.ops[eng].append((fn, waits, s, k))
        return tok

    def barrier(self, full=False):
        cur = {e: self.serial[e] for e in ENGS if e != "sp"}
        for e in ENGS:
            if e == "pe":
                continue
            if e == "sp" and not full:
                deps = [("e", e2, s2) for e2, s2 in cur.items() if s2 > 0] + [("d", k, v) for k, v in self.out_dma.items()]
                self.sp_barrier = deps if self.sp_barrier is None else self.sp_barrier + deps
                continue
            waits = []
            for e2, s2 in cur.items():
                if s2 > 0 and e2 != e:
                    self._need(e, ("e", e2, s2), waits)
            for k, v in self.out_dma.items():
                self._need(e, ("d", k, v), waits)
            if waits:
                self.ops[e].append((None, waits, None, None))

    def finish(self):
        waits = []
        for e in ENGS:
            if e != "sp" and self.serial[e] > 0:
                self._need("sp", ("e", e, self.serial[e]), waits)
        for k in range(NDSEM):
            if self.dsem_val[k] > 0:
                self._need("sp", ("d", k, self.dsem_val[k]), waits)
        self.ops["sp"].append((None, waits, None, None))

    def emit(self, sems, dsems, block):
        nc = self.nc
        pe_sorted = sorted(self.pe_inc)
        pe_rank = {s: i + 1 for i, s in enumerate(pe_sorted)}
        handles = {"pe": nc.tensor, "act": nc.scalar, "dve": nc.vector, "pool": nc.gpsimd, "sp": nc.sync}

        def run(eng):
            h = handles[eng]
            for fn, waits, s, dk in self.ops[eng]:
                for d in waits:
                    if d[0] == "e":
                        v = pe_rank[d[2]] if d[1] == "pe" else d[2]
                        h.wait_ge(sems[d[1]], v)
                    else:
                        h.wait_ge(dsems[d[1]], d[2])
                if fn is None:
                    continue
                ins = fn()
                if dk is not None:
                    ins.then_inc(dsems[dk], 16)
                elif eng == "pe":
                    if s in pe_rank:
                        ins.then_inc(sems[eng], 1)
                elif eng != "sp":
                    ins.then_inc(sems[eng], 1)

        block.tensor(lambda e: run("pe"))
        block.scalar(lambda e: run("act"))
        block.vector(lambda e: run("dve"))
        block.gpsimd(lambda e: run("pool"))
        block.sync(lambda e: run("sp"))


def _const_layout():
    off = {}
    cur = 0

    def add(name, n):
        nonlocal cur
        off[name] = (cur, n)
        cur += n

    add("ident", 128)
    add("eps", 1)
    add("nw", 5 * 8)
    add("convw", 2 * 3 * FT)
    add("convb", 2 * FT)
    add("glu_b", 4)
    add("s5_d", 4)
    add("hg_lg", 2 * 8)
    add("inv_freq", 1)
    add("sgn", 1)
    add("one", 1)
    add("hg_nw", 1)
    add("maskT", 4 * 128)
    add("qdec", 4 * 128)
    add("kdec", 4)
    add("g128", 4)
    add("g4", 4)
    add("maskS", 4 * 64)
    add("qdecS", 4 * 64)
    add("kdecS", 4)
    add("seqmask", 16)
    add("triBD", 128)
    add("supBD", 128)
    add("triS", 64)
    add("supS", 64)
    return off, cur


CL, NCONST = _const_layout()
RET_GAMMA = [1.0 - 2.0 ** (-5.0 - h) for h in range(4)]


class Arena:
    def __init__(self, ap_words, nwords):
        self.ap = ap_words
        self.n = nwords
        self.cur = 0

    def mark(self):
        return self.cur

    def release(self, m):
        self.cur = m

    def alloc(self, name, free_shape, dtype=F32):
        n = int(np.prod(free_shape))
        words = n if dtype in (F32, I32) else (n + 1) // 2
        words = (words + 7) // 8 * 8
        assert self.cur + words <= self.n, f"arena overflow at {name}: need {words} have {self.n - self.cur}"
        a = self.ap[:, self.cur:self.cur + words]
        self.cur += words
        if dtype == BF16:
            a = a.bitcast(BF16)[:, 0:n]
        elif dtype == I32:
            a = a.bitcast(I32)[:, 0:n]
        else:
            a = a[:, 0:n]
        if len(free_shape) == 2:
            a = a.rearrange("p (a b) -> p a b", a=free_shape[0])
        elif len(free_shape) == 3:
            a = a.rearrange("p (a b c) -> p a b c", a=free_shape[0], b=free_shape[1])
        return Tile(name, a)


class Rot:
    def __init__(self, tiles):
        self.tiles = tiles
        self.i = 0

    def next(self):
        t = self.tiles[self.i]
        self.i = (self.i + 1) % len(self.tiles)
        return t


from functools import partial
from contextlib import ExitStack

ARENA_WORDS = 39600


class Prog:
    def __init__(self, debug=None):
        self.debug = debug or {}
        self.nc = bass.Bass("TRN2", target_bir_lowering=False)
        self.sch = Sched(self.nc)
        self.h = {"pe": self.nc.tensor, "act": self.nc.scalar, "dve": self.nc.vector, "pool": self.nc.gpsimd}
        self.inputs = {}
        self.outputs = {}

    def din(self, name, shape, dt=F32):
        ap = self.nc.dram_tensor(name, list(shape), dt, kind="ExternalInput").ap()
        self.inputs[name] = ap
        return ap

    def dout(self, name, shape):
        ap = self.nc.dram_tensor(name, list(shape), F32, kind="ExternalOutput").ap()
        self.outputs[name] = ap
        return ap

    def op(self, eng, fn, reads=(), writes=()):
        return self.sch.op(eng, fn, reads, writes)

    def tt(self, eng, out, in0, in1, op, reads, writes):
        self.op(eng, partial(self.h[eng].tensor_tensor, out=out, in0=in0, in1=in1, op=op), reads, writes)

    def stt(self, out, in0, scalar, in1, op0, op1, reads, writes):
        self.op("dve", partial(self.nc.vector.scalar_tensor_tensor, out=out, in0=in0, scalar=scalar, in1=in1,
                               op0=op0, op1=op1), reads, writes)

    def ts(self, eng, out, in0, s1, s2, op0, op1, reads, writes):
        if s2 is None and eng == "pool" and op0 == ALU.mult:
            s2, op1 = 0.0, ALU.add
        if s2 is None:
            self.op(eng, partial(self.h[eng].tensor_scalar, out=out, in0=in0, scalar1=s1, scalar2=None, op0=op0),
                    reads, writes)
        else:
            self.op(eng, partial(self.h[eng].tensor_scalar, out=out, in0=in0, scalar1=s1, scalar2=s2, op0=op0,
                                 op1=op1), reads, writes)

    def cp(self, eng, out, in_, reads, writes):
        if eng == "act":
            self.op(eng, partial(self.nc.scalar.copy, out=out, in_=in_), reads, writes)
        else:
            self.op(eng, partial(self.h[eng].tensor_copy, out=out, in_=in_), reads, writes)

    def act(self, out, in_, func, reads, writes, bias=None, scale=None, accum_out=None):
        kw = dict(out=out, in_=in_, func=func)
        if bias is not None:
            kw["bias"] = bias
        if scale is not None:
            kw["scale"] = scale
        if accum_out is not None:
            kw["accum_out"] = accum_out
        self.op("act", partial(self.nc.scalar.activation, **kw), reads, writes)

    def mm(self, out, lhsT, rhs, start, stop, reads, writes):
        self.op("pe", partial(self.nc.tensor.matmul, out, lhsT=lhsT, rhs=rhs, start=start, stop=stop), reads, writes)

    def tr(self, out, in_, ident, reads, writes):
        self.op("pe", partial(self.nc.tensor.transpose, out=out, in_=in_, identity=ident), reads, writes)

    def dma(self, out, in_, reads=(), writes=(), **kw):
        self.sch.dma(partial(self.nc.sync.dma_start, out=out, in_=in_, **kw), reads, writes)

    def load_w(self, dst, src, K, n, cast_eng="act"):
        srcv = src.rearrange("(k p) n -> p k n", p=128)
        kk = max(1, 1024 // n)
        for k0 in range(0, K, kk):
            k1 = min(K, k0 + kk)
            st = self.wstage.next()
            sv = st.ap[:, 0:(k1 - k0) * n].rearrange("p (k n) -> p k n", n=n)
            self.dma(sv, srcv[:, k0:k1, :], reads=[], writes=[st])
            self.cp(cast_eng, dst.ap[:, k0:k1, :], sv, reads=[st], writes=[dst])

    def build(self):
        nc = self.nc
        with ExitStack() as es:
            self.es = es
            self._declare_dram()
            self._alloc(es)
            self._body()
            self.sch.finish()
            sems = {e: es.enter_context(nc.semaphore("sem_" + e)) for e in ENGS}
            dsems = [es.enter_context(nc.semaphore("dsem%d" % k)) for k in range(NDSEM)]
            block = es.enter_context(nc.Block())
            self.sch.emit(sems, dsems, block)
        return nc

    def _declare_dram(self):
        d = self.din
        self.xp = d("xp", [SEQ, D])
        self.xs = d("xs", [NSMP, D])
        self.consts_d = d("consts", [128, NCONST])
        self.wg_d = d("wg", [2, D, DFF])
        self.wu_d = d("wu", [2, D, DFF])
        self.wd_d = d("wd", [2, DFF, D])
        self.conv0_d = d("conv0", [2, 32, DFF])
        self.w_in_ab = d("w_in_ab", [D, 2560])
        self.glu_w = d("glu_w", [512, 512])
        self.w_out_ab = d("w_out_ab", [D, D])
        self.s5_sp = d("s5_sp", [128, 48])
        self.s5_BT = d("s5_BT", [2, 128, 2048])
        self.s5_CT = d("s5_CT", [2, 128, 2048])
        self.s5re0 = d("s5re0", [NSS, 2048])
        self.s5im0 = d("s5im0", [NSS, 2048])
        self.ret0 = d("ret0", [NSS, 4, 128, 128])
        self.pos_d = d("pos", [128, NSS], I32)
        self.cpos_d = d("cpos", [128, SEGT])
        self.w_in_c = d("w_in_c", [D, 4096])
        self.w_out_c = d("w_out_c", [D, D])
        self.hg0 = d("hg0", [NSS, 8, 128, 128])
        o = self.dout
        self.y_p = o("y_p", [SEQ, D])
        self.y_s = o("y_s", [NSMP, D])
        self.conv_p = o("conv_p", [2, 2, DFF])
        self.conv_s = o("conv_s", [2, 32, DFF])
        self.s5re_p = o("s5re_p", [16, 128])
        self.s5im_p = o("s5im_p", [16, 128])
        self.ret_p = o("ret_p", [4, 128, 128])
        self.s5re_s = o("s5re_s", [NSS, 2048])
        self.s5im_s = o("s5im_s", [NSS, 2048])
        self.ret_s = o("ret_s", [NSS, 4, 128, 128])
        self.hg_p = o("hg_p", [8, 128, 128])
        self.hg_s = o("hg_s", [NSS, 8, 128, 128])

    def _alloc(self, es):
        nc = self.nc
        sb = lambda name, shape, dt=F32: es.enter_context(nc.sbuf_tensor(name, shape, dt))
        self.consts = Tile("consts", sb("consts_sb", [128, NCONST])[:])
        self.xT = [Tile("xT%d" % c, None) for c in range(3)]
        xT_full = sb("xT", [128, KT, W0])
        self.xT_ap = xT_full
        self.identb = Tile("identb", sb("identb", [128, 128], BF16)[:])
        self.onesb = Tile("onesb", sb("onesb", [128, 128], BF16)[:])
        self.tails = Tile("tails", sb("tails", [128, 2, 2, FT])[:])
        self.Sret = Tile("Sret", sb("Sret", [128, 4, 128])[:])
        self.Sretb = Tile("Sretb", sb("Sretb", [128, 4, 128], BF16)[:])
        self.s5carry = Tile("s5carry", sb("s5carry", [128, 2, 16])[:])
        self.Shg = Tile("Shg", sb("Shg", [128, 8, 128])[:])
        self.Shgb = Tile("Shgb", sb("Shgb", [128, 8, 128], BF16)[:])
        arena_t = sb("arena", [128, ARENA_WORDS])
        self.arena = Arena(arena_t, ARENA_WORDS)
        self.ps = [Tile("ps%d" % i, es.enter_context(nc.psum_tensor("ps%d" % i, [128, 512], F32))[:]) for i in range(8)]

    def _sq_halves(self, s):
        key = id(s)
        if not hasattr(self, "_sqh"):
            self._sqh = {}
        if key not in self._sqh:
            self._sqh[key] = (Tile(s.name + "_A", s.ap), Tile(s.name + "_B", s.ap))
        return self._sqh[key]

    def alloc_hT(self):
        return [self.arena.alloc("hT%d" % c, [KT, 512 if c < 2 else NSMP], BF16) for c in range(3)]

    def C(self, name, rows=128):
        o, n = CL[name]
        return self.consts.ap[0:rows, o:o + n]

    def chunks(self, seg):
        ch = [(0, 0, 512), (1, 512, 512)]
        if seg == 0:
            ch.append((2, 1024, NSMP))
        return ch

    def _body(self):
        nc = self.nc
        ar = self.arena
        self.dma(self.consts.ap, self.consts_d, writes=[self.consts])
        self.cp("dve", self.identb.ap, self.C("ident"), [self.consts], [self.identb])
        self.op("dve", partial(nc.vector.memset, self.onesb.ap, 1.0), [], [self.onesb])
        self.op("dve", partial(nc.vector.memset, self.tails.ap, 0.0), [], [self.tails])
        self.op("dve", partial(nc.vector.memset, self.Sret.ap, 0.0), [], [self.Sret])
        self.op("dve", partial(nc.vector.memset, self.Sretb.ap, 0.0), [], [self.Sretb])
        self.op("dve", partial(nc.vector.memset, self.s5carry.ap, 0.0), [], [self.s5carry])
        self.op("dve", partial(nc.vector.memset, self.Shg.ap, 0.0), [], [self.Shg])
        self.op("dve", partial(nc.vector.memset, self.Shgb.ap, 0.0), [], [self.Shgb])
        self._pb = 0
        for seg in range(NSEG):
            self.seg = seg
            m0 = ar.mark()
            self.wstage = Rot([ar.alloc("wst%d" % i, [1024]) for i in range(4)])
            for t_ in self.wstage.tiles:
                t_.persist = True
            self.sq = Rot([ar.alloc("sq%d" % i, [KT, 512], BF16) for i in range(1)])
            self.rs = Rot([ar.alloc("rs%d" % i, [512]) for i in range(1)])
            self.load_x(seg)
            for layer in ([0] if self.debug.get("only_ab") else [1] if self.debug.get("only_c") else [0, 1]):
                if not self.debug.get("skip_mixer"):
                    if layer == 0:
                        self.mixer_ab(seg)
                    else:
                        self.mixer_c(seg)
                if not self.debug.get("skip_ffn"):
                    self.ffn(seg, layer)
            self.final(seg)
            self.sch.barrier(full=True)
            ar.release(m0)

    def xcols(self, c, k0=0, k1=KT):
        c0, n = [(0, 512), (512, 512), (1024, NSMP)][c]
        return self.xT_ap[:, k0:k1, c0:c0 + n]

    def load_x(self, seg):
        ar = self.arena
        m = ar.mark()
        stg = Rot([ar.alloc("xst%d" % i, [D]) for i in range(2)])
        ident = self.C("ident")
        tiles = [(self.xp[seg * SEGT + t * 128: seg * SEGT + (t + 1) * 128, :], 128, t // 4, (t % 4) * 128) for t in range(8)]
        if seg == 0:
            tiles.append((self.xs[:, :], NSMP, 2, 0))
        banks = Rot([self.ps[6], self.ps[7]])
        for (src, rows, c, off) in tiles:
            st = stg.next()
            self.dma(st.ap[0:rows, :], src, writes=[st])
            for half in range(2):
                b = banks.next()
                for q in range(4):
                    k = half * 4 + q
                    self.tr(b.ap[:, q * 128:q * 128 + rows], st.ap[0:rows, k * 128:(k + 1) * 128], ident[0:rows, 0:rows],
                            [st, self.consts], [b])
                c0 = [0, 512, 1024][c] + off
                dst = self.xT_ap[:, half * 4:half * 4 + 4, c0:c0 + rows]
                srcv = b.ap.rearrange("p (q t) -> p q t", q=4)[:, :, 0:rows]
                self.cp("act", dst, srcv, [b], [self.xT[c]])
        self.sch.barrier()
        ar.release(m)

    def rmsnorm(self, seg, idx, out_tiles):
        sq, rs = self.sq, self.rs
        nwo = CL["nw"][0]
        eps_ap = self.C("eps")
        for (c, c0, n) in self.chunks(seg):
            s = sq.next()
            if not hasattr(s, "_halves"):
                pass
            sA, sB = self._sq_halves(s)
            for k in range(KT):
                xk = self.xcols(c, k, k + 1)[:, 0, :]
                if k < 5:
                    self.act(s.ap[:, k, 0:n], xk, AF.Square, [self.xT[c]], [sA])
                else:
                    self.tt("pool", s.ap[:, k, 0:n], xk, xk, ALU.mult, [self.xT[c]], [sB])
            b = self.ps[5]
            for k in range(KT):
                self.mm(b.ap[:, 0:n], self.onesb.ap, s.ap[:, k, 0:n], k == 0, k == KT - 1,
                        [self.onesb, sA if k < 5 else sB], [b])
            r = rs.next()
            self.act(r.ap[:, 0:n], b.ap[:, 0:n], AF.Ln, [b], [r], bias=eps_ap, scale=1.0 / D)
            self.act(r.ap[:, 0:n], r.ap[:, 0:n], AF.Exp, [r], [r], scale=-0.5)
            for k in range(KT):
                self.stt(out_tiles[c].ap[:, k, 0:n], self.xcols(c, k, k + 1)[:, 0, :],
                         self.consts.ap[:, nwo + idx * 8 + k: nwo + idx * 8 + k + 1], r.ap[:, 0:n],
                         ALU.mult, ALU.mult, [self.xT[c], self.consts, r], [out_tiles[c]])

    def ffn(self, seg, layer):
        nc = self.nc
        ar = self.arena
        m0 = ar.mark()
        self.hT = self.alloc_hT()
        groups = [list(range(0, 6)), list(range(6, 12)), list(range(12, 17)), list(range(17, 22))]
        wgt = Rot([ar.alloc("wg%d" % i, [KT, 128], BF16) for i in range(4)])
        wut = Rot([ar.alloc("wu%d" % i, [KT, 128], BF16) for i in range(4)])
        wdt = Rot([ar.alloc("wd%d" % i, [6, D], BF16) for i in range(2)])
        actTs = [[ar.alloc("act%d_%d" % (i, c), [6, 512 if c < 2 else NSMP], BF16) for c in range(3)] for i in range(2)]
        exth = Rot([ar.alloc("exth%d" % i, [2]) for i in range(24)])
        cbuf = Rot([ar.alloc("cb%d" % i, [512]) for i in range(3)])
        sbuf = Rot([ar.alloc("sb%d" % i, [512]) for i in range(2)])
        class _G:
            def next(_s):
                return self.pbank()
        psA = _G()
        psB = _G()
        ident = self.C("ident")
        cwo = CL["convw"][0]
        cbo = CL["convb"][0]
        cw = lambda j, f: self.consts.ap[:, cwo + (layer * 3 + j) * FT + f: cwo + (layer * 3 + j) * FT + f + 1]
        cbias = lambda f: self.consts.ap[:, cbo + layer * FT + f: cbo + layer * FT + f + 1]
        chunks = self.chunks(seg)
        if seg == 0:
            cs = ar.alloc("cs", [DFF])
            bufT = ar.alloc("bufT", [FT, 32])
            ext_sh = ar.alloc("ext_sh", [FT, 16, 2])
            ext_sb = ar.alloc("ext_sb", [FT, 16, 4])
            ext_s2 = ar.alloc("ext_s2", [FT, 32])
            self.dma(cs.ap[0:32, :], self.conv0_d[layer], writes=[cs])
            for half in range(2):
                b = self.ps[6 + half]
                f0, f1 = (0, 16) if half == 0 else (16, FT)
                for f in range(f0, f1):
                    q = f - f0
                    self.tr(b.ap[:, q * 32:(q + 1) * 32], cs.ap[0:32, f * 128:(f + 1) * 128], ident[0:32, 0:32],
                            [cs, self.consts], [b])
                self.cp("act", bufT.ap[:, f0:f1, :], b.ap[:, 0:(f1 - f0) * 32].rearrange("p (f t) -> p f t", t=32),
                        [b], [bufT])
            self.cp("pool", ext_sh.ap, bufT.ap.rearrange("p f (s j) -> p f s j", j=2), [bufT], [ext_sh])

        def load_gu(f):
            g, u = wgt.next(), wut.next()
            self.load_w(g, self.wg_d[layer][:, f * 128:(f + 1) * 128], KT, 128)
            self.load_w(u, self.wu_d[layer][:, f * 128:(f + 1) * 128], KT, 128)
            return g, u

        def load_d(grp):
            w = wdt.next()
            self.load_w(w, self.wd_d[layer][grp[0] * 128:(grp[-1] + 1) * 128, :], len(grp), D)
            return w

        allf = [f for g in groups for f in g]
        pend = {allf[0]: load_gu(allf[0]), allf[1]: load_gu(allf[1])}
        rs_saved = self.rs
        self.rs = Rot([rs_saved.tiles[0], ar.alloc("rs_x", [512])])
        self.rmsnorm(seg, 1 + 2 * layer, self.hT)
        self.rs = rs_saved

        def phase_b(grp, wd, actT):
            for (c, c0, n) in chunks:
                for mo in range(KT):
                    b = psB.next()
                    for fl in range(len(grp)):
                        self.mm(b.ap[:, 0:n], wd.ap[:, fl, mo * 128:(mo + 1) * 128], actT[c].ap[:, fl, 0:n],
                                fl == 0, fl == len(grp) - 1, [wd, actT[c]], [b])
                    xv = self.xcols(c, mo, mo + 1)[:, 0, :]
                    self.tt("dve", xv, xv, b.ap[:, 0:n], ALU.add, [self.xT[c], b], [self.xT[c]])

        pend_b = None
        pend_tail = None
        for gi, grp in enumerate(groups):
            wd = load_d(grp)
            actT = actTs[gi % 2]
            prev_ext = None
            for fl, f in enumerate(grp):
                g, u = pend.pop(f)
                nxt = allf.index(f) + 2
                if nxt < len(allf):
                    pend[allf[nxt]] = load_gu(allf[nxt])
                for (c, c0, n) in chunks:
                    gb, ub = psA.next(), psA.next()
                    for k in range(KT):
                        self.mm(gb.ap[:, 0:n], g.ap[:, k, :], self.hT[c].ap[:, k, 0:n], k == 0, k == KT - 1,
                                [g, self.hT[c]], [gb])
                    for k in range(KT):
                        self.mm(ub.ap[:, 0:n], u.ap[:, k, :], self.hT[c].ap[:, k, 0:n], k == 0, k == KT - 1,
                                [u, self.hT[c]], [ub])
                    cb, sb_ = cbuf.next(), sbuf.next()
                    if c < 2:
                        eh, eb = exth.next(), exth.next()
                        if c == 0:
                            self.cp("pool", eh.ap, self.tails.ap[:, layer, :, f], [self.tails], [eh])
                        else:
                            self.cp("pool", eh.ap, prev_ext.ap, [prev_ext], [eh])
                        self.cp("act", eb.ap, gb.ap[:, n - 2:n], [gb], [eb])
                        self.act(cb.ap[:, 0:n], gb.ap[:, 0:n], AF.Identity, [gb, self.consts], [cb],
                                 bias=cbias(f), scale=cw(2, f))
                        self.stt(cb.ap[:, 1:n], gb.ap[:, 0:n - 1], cw(1, f), cb.ap[:, 1:n], ALU.mult, ALU.add,
                                 [gb, cb, self.consts], [cb])
                        self.stt(cb.ap[:, 2:n], gb.ap[:, 0:n - 2], cw(0, f), cb.ap[:, 2:n], ALU.mult, ALU.add,
                                 [gb, cb, self.consts], [cb])
                        hc, hc2 = exth.next(), exth.next()
                        self.ts("pool", hc.ap, eh.ap, cw(0, f), None, ALU.mult, None, [eh, self.consts], [hc])
                        self.ts("pool", hc2.ap[:, 0:1], eh.ap[:, 1:2], cw(1, f), None, ALU.mult, None, [eh, self.consts], [hc2])
                        self.tt("pool", hc.ap[:, 0:1], hc.ap[:, 0:1], hc2.ap[:, 0:1], ALU.add, [hc, hc2], [hc])
                        self.tt("dve", cb.ap[:, 0:2], cb.ap[:, 0:2], hc.ap, ALU.add, [cb, hc], [cb])
                        if c == 1:
                            self.cp("pool", self.tails.ap[:, layer, :, f], eb.ap, [eb], [self.tails])
                        prev_ext = eb
                    else:
                        gv = gb.ap[:, 0:NSMP].rearrange("p (s t) -> p s t", t=4)
                        cv = cb.ap[:, 0:NSMP].rearrange("p (s t) -> p s t", t=4)
                        esb = ext_sb.ap[:, f]
                        esh = ext_sh.ap[:, f]
                        self.cp("act", esb, gv, [gb], [ext_sb])
                        self.cp("pool", ext_s2.ap[:, f].rearrange("p (s j) -> p s j", j=2), esb[:, :, 2:4], [ext_sb], [ext_s2])
                        self.act(cb.ap[:, 0:NSMP], gb.ap[:, 0:NSMP], AF.Identity, [gb, self.consts], [cb],
                                 bias=cbias(f), scale=cw(2, f))
                        self.stt(cv[:, :, 1:4], esb[:, :, 0:3], cw(1, f), cv[:, :, 1:4], ALU.mult, ALU.add,
                                 [ext_sb, cb, self.consts], [cb])
                        self.stt(cv[:, :, 2:4], esb[:, :, 0:2], cw(0, f), cv[:, :, 2:4], ALU.mult, ALU.add,
                                 [ext_sb, cb, self.consts], [cb])
                        self.stt(cv[:, :, 0:1], esh[:, :, 1:2], cw(1, f), cv[:, :, 0:1], ALU.mult, ALU.add,
                                 [ext_sh, cb, self.consts], [cb])
                        self.stt(cv[:, :, 0:2], esh[:, :, 0:2], cw(0, f), cv[:, :, 0:2], ALU.mult, ALU.add,
                                 [ext_sh, cb, self.consts], [cb])
                    def tail(cb=cb, sb_=sb_, ub=ub, actTc=actT[c], fl=fl, n=n):
                        self.act(sb_.ap[:, 0:n], cb.ap[:, 0:n], AF.Silu, [cb], [sb_])
                        self.tt("dve", actTc.ap[:, fl, 0:n], sb_.ap[:, 0:n], ub.ap[:, 0:n], ALU.mult, [sb_, ub], [actTc])
                    if pend_tail is not None:
                        pend_tail()
                    pend_tail = tail
            if pend_tail is not None:
                pend_tail()
                pend_tail = None
            if pend_b is not None:
                phase_b(*pend_b)
            pend_b = (grp, wd, actT)
        phase_b(*pend_b)
        if seg == 0:
            cso = cs
            for q0 in range(0, FT, 4):
                b = psB.next()
                fs = list(range(q0, min(FT, q0 + 4)))
                for qi, f in enumerate(fs):
                    self.mm(b.ap[0:32, qi * 128:(qi + 1) * 128], ext_s2.ap[:, f, :], ident, True, True,
                            [ext_s2, self.consts], [b])
                self.cp("act", cso.ap[0:32, q0 * 128:(q0 + len(fs)) * 128], b.ap[0:32, 0:len(fs) * 128], [b], [cso])
            self.dma(self.conv_s[layer], cso.ap[0:32, :], reads=[cso])
        if seg == NSEG - 1:
            b = psB.next()
            tl = ar.alloc("tl", [128])
            self.mm(b.ap[0:2 * FT, 0:128], self.tails.ap[:, layer].rearrange("p j f -> p (j f)"), ident, True, True,
                    [self.tails, self.consts], [b])
            self.cp("act", tl.ap[0:2 * FT, :], b.ap[0:2 * FT, 0:128], [b], [tl])
            for j in range(2):
                self.dma(self.conv_p[layer, j].rearrange("(f p) -> f p", p=128), tl.ap[j * FT:(j + 1) * FT, :], reads=[tl])
        self.sch.barrier()
        ar.release(m0)

    def final(self, seg):
        ar = self.arena
        m0 = ar.mark()
        hF = [ar.alloc("hF%d" % c, [KT, 512 if c < 2 else NSMP]) for c in range(3)]
        rs_saved = self.rs
        self.rs = Rot([rs_saved.tiles[0], ar.alloc("rs_x", [512])])
        self.rmsnorm(seg, 4, hF)
        self.rs = rs_saved
        osts = []
        for i in range(2):
            t_ = ar.alloc("ost%d" % i, [D])
            osts.append((t_, Tile("ostb%d" % i, t_.ap)))
        ost = Rot(osts)
        ident = self.C("ident")
        banks = Rot([self.ps[0], self.ps[1], self.ps[2], self.ps[3]])
        tiles = [(t // 4, (t % 4) * 128, 128, self.y_p[seg * SEGT + t * 128: seg * SEGT + (t + 1) * 128, :]) for t in range(8)]
        if seg == 0:
            tiles.append((2, 0, NSMP, self.y_s[:, :]))
        for (c, off, rows, dst) in tiles:
            oa, ob = ost.next()
            for half in range(2):
                b = banks.next()
                for q in range(4):
                    k = half * 4 + q
                    self.tr(b.ap[0:rows, q * 128:(q + 1) * 128], hF[c].ap[:, k, off:off + rows], ident, [hF[c], self.consts], [b])
                self.cp("act" if half == 0 else "dve", oa.ap[0:rows, half * 512:(half + 1) * 512], b.ap[0:rows, :], [b],
                        [oa] if half == 0 else [ob])
            self.dma(dst, oa.ap[0:rows, :], reads=[oa, ob])
        ar.release(m0)


def _fm(v):
    v = np.asarray(v, np.float32)
    return np.ascontiguousarray(v.reshape(-1, 128).T)


def _build_consts(inp):
    c = np.zeros((128, NCONST), np.float32)

    def put(name, arr, rows=128):
        o, n = CL[name]
        arr = np.asarray(arr, np.float32).reshape(rows, n)
        c[0:rows, o:o + n] = arr

    put("ident", np.eye(128, dtype=np.float32))
    put("eps", np.full((128, 1), EPS, np.float32))
    nw = np.stack([_fm(inp["norm_mix"][0]), _fm(inp["norm_ffn"][0]), _fm(inp["norm_mix"][1]), _fm(inp["norm_ffn"][1]),
                   _fm(inp["norm_final"])], axis=1)
    put("nw", nw)
    cw = np.asarray(inp["ffn_conv_w"], np.float32).reshape(2, 3, FT, 128).transpose(3, 0, 1, 2)
    put("convw", cw)
    cb = np.asarray(inp["ffn_conv_b"], np.float32).reshape(2, FT, 128).transpose(2, 0, 1)
    put("convb", cb)
    put("glu_b", _fm(inp["s5_glu_b"][0]))
    put("s5_d", _fm(np.asarray(inp["s5_d"][0]).reshape(-1)))
    lg = np.stack([_fm(inp["hg_lb_logits"][0]), _fm(inp["hg_lb_logits"][1])], axis=1)
    put("hg_lg", lg)
    half = 64
    inv = (1.0 / (10000.0 ** np.linspace(0.0, 1.0, half, dtype=np.float32))).astype(np.float32)
    p = np.arange(128)
    put("inv_freq", inv[p % 64].reshape(128, 1))
    put("sgn", np.where(p < 64, -1.0, 1.0).reshape(128, 1))
    put("one", np.ones((128, 1), np.float32))
    put("hg_nw", np.asarray(inp["hg_norm_w"][0], np.float32).reshape(128, 1))
    g = np.array(RET_GAMMA, np.float64)
    scale = 128.0 ** -0.5
    j = np.arange(128)[:, None]
    i = np.arange(128)[None, :]
    mt = np.zeros((128, 4, 128))
    for h in range(4):
        mt[:, h, :] = np.where(i >= j, g[h] ** np.maximum(i - j, 0), 0.0) * scale
    put("maskT", mt)
    qd = np.zeros((128, 4, 128))
    kd = np.zeros((128, 4))
    for h in range(4):
        qd[:, h, :] = (g[h] ** (np.arange(128) + 1))[None, :]
        kd[:, h] = g[h] ** (127 - np.arange(128)) * scale
    put("qdec", qd)
    put("kdec", kd)
    put("g128", np.broadcast_to((g ** 128)[None, :], (128, 4)))
    put("g4", np.broadcast_to((g ** 4)[None, :], (128, 4)))
    tok = np.arange(64)
    sq_, tau = tok // 4, tok % 4
    ms = np.zeros((128, 4, 64))
    for h in range(4):
        same = (sq_[:, None] == sq_[None, :]) & (tau[None, :] >= tau[:, None])
        ms[0:64, h, :] = np.where(same, g[h] ** np.maximum(tau[None, :] - tau[:, None], 0), 0.0) * scale
    put("maskS", ms)
    qds = np.zeros((128, 4, 64))
    kds = np.zeros((128, 4))
    for h in range(4):
        qds[:, h, :] = (g[h] ** (tau + 1))[None, :]
        kds[0:64, h] = g[h] ** (3 - tau) * scale
    put("qdecS", qds)
    put("kdecS", kds)
    sm = np.zeros((128, 16))
    sm[tok, sq_] = 1.0
    put("seqmask", sm)
    jj = np.arange(128)[:, None]
    ii = np.arange(128)[None, :]
    samec = (jj // 64) == (ii // 64)
    put("triBD", (samec & (jj <= ii)).astype(np.float32))
    put("supBD", (samec & (jj > ii)).astype(np.float32))
    ts_ = np.zeros((128, 64))
    us_ = np.zeros((128, 64))
    sames = sq_[:, None] == sq_[None, :]
    ts_[0:64] = (sames & (tok[:, None] <= tok[None, :]))
    us_[0:64] = (sames & (tok[:, None] > tok[None, :]))
    put("triS", ts_)
    put("supS", us_)
    return c


def _s5_layouts(inp):
    lam_re = np.asarray(inp["s5_lam_re"][0], np.float32)
    lam_im = np.asarray(inp["s5_lam_im"][0], np.float32)
    ldt = np.asarray(inp["s5_log_dt"][0], np.float32)

    def sp(a):
        return np.ascontiguousarray(a.reshape(16, 2, 64).transpose(1, 2, 0).reshape(128, 16))

    s5_sp = np.concatenate([sp(lam_re), sp(lam_im), sp(np.repeat(ldt[:, None], 64, axis=1))], axis=1)
    BT = np.zeros((2, 128, 16, 128), np.float32)
    CT = np.zeros((2, 128, 16, 128), np.float32)
    for ri, (bsrc, csrc) in enumerate(((inp["s5_b_re"][0], inp["s5_c_re"][0]), (inp["s5_b_im"][0], inp["s5_c_im"][0]))):
        bsrc = np.asarray(bsrc, np.float32)
        csrc = np.asarray(csrc, np.float32)
        for gidx in range(32):
            jx, g2, gl = gidx // 2, gidx % 2, gidx % 8
            BT[ri, gl * 16:(gl + 1) * 16, jx, g2 * 64:(g2 + 1) * 64] = bsrc[gidx].T
            CT[ri, g2 * 64:(g2 + 1) * 64, jx, gl * 16:(gl + 1) * 16] = csrc[gidx].T
    return s5_sp, BT.reshape(2, 128, 2048), CT.reshape(2, 128, 2048)


_PROG_CACHE = {}


def _get_prog(debug=None):
    key = tuple(sorted((debug or {}).items()))
    if key not in _PROG_CACHE:
        p = Prog(debug)
        p.build()
        _PROG_CACHE[key] = p
    return _PROG_CACHE[key]


def kernel(_debug=None, _cores=NCORES, **inp):
    inp = {k: np.asarray(v) for k, v in inp.items()}
    prog = _get_prog(_debug)
    consts = _build_consts(inp)
    s5_sp, s5_BT, s5_CT = _s5_layouts(inp)
    cpos = np.ascontiguousarray(np.broadcast_to(np.arange(SEGT, dtype=np.float32)[None, :], (128, SEGT)))
    in_maps = []
    for b in range(_cores):
        m = {
            "xp": np.ascontiguousarray(inp["x_prompt"][b]),
            "xs": np.ascontiguousarray(inp["x_sample"][NSS * b:NSS * (b + 1)].reshape(NSMP, D)),
            "consts": consts,
            "wg": inp["ffn_w_gate"], "wu": inp["ffn_w_up"], "wd": inp["ffn_w_down"],
            "conv0": np.ascontiguousarray(inp["state_ffn_conv"][:, NSS * b:NSS * (b + 1)].reshape(2, 32, DFF)),
            "w_in_ab": inp["w_in_ab"][0], "glu_w": inp["s5_glu_w"][0], "w_out_ab": inp["w_out_ab"][0],
            "s5_sp": s5_sp, "s5_BT": s5_BT, "s5_CT": s5_CT,
            "s5re0": np.ascontiguousarray(inp["state_s5_re"][0, NSS * b:NSS * (b + 1)].reshape(NSS, 2048)),
            "s5im0": np.ascontiguousarray(inp["state_s5_im"][0, NSS * b:NSS * (b + 1)].reshape(NSS, 2048)),
            "ret0": np.ascontiguousarray(inp["state_ret"][0, NSS * b:NSS * (b + 1)]),
            "pos": np.ascontiguousarray(np.broadcast_to(inp["pos_sample"][NSS * b:NSS * (b + 1)].astype(np.int32)[None, :], (128, NSS))),
            "cpos": cpos,
            "w_in_c": inp["w_in_c"][0], "w_out_c": inp["w_out_c"][0],
            "hg0": np.ascontiguousarray(inp["state_hgrn"][0, NSS * b:NSS * (b + 1)]),
        }
        in_maps.append({k: v for k, v in m.items() if k in prog.inputs})
    res = run_bass_kernel_spmd(prog.nc, in_maps, core_ids=list(range(_cores)))
    R = res.results
    B = _cores
    y_p = np.stack([R[b]["y_p"] for b in range(B)])
    y_s = np.concatenate([R[b]["y_s"].reshape(NSS, 4, D) for b in range(B)])
    conv_p = np.stack([R[b]["conv_p"] for b in range(B)], axis=1)
    conv_s = np.concatenate([R[b]["conv_s"].reshape(2, NSS, 2, DFF) for b in range(B)], axis=1)
    out = {"y_p": y_p, "y_s": y_s, "conv_p": conv_p, "conv_s": conv_s}
    if "s5re_p" in R[0]:
        out["s5re_p"] = np.stack([R[b]["s5re_p"].reshape(32, 64) for b in range(B)])[None]
        out["s5im_p"] = np.stack([R[b]["s5im_p"].reshape(32, 64) for b in range(B)])[None]
        out["ret_p"] = np.stack([R[b]["ret_p"] for b in range(B)])[None]
        out["s5re_s"] = np.concatenate([R[b]["s5re_s"].reshape(NSS, 32, 64) for b in range(B)])[None]
        out["s5im_s"] = np.concatenate([R[b]["s5im_s"].reshape(NSS, 32, 64) for b in range(B)])[None]
        out["ret_s"] = np.concatenate([R[b]["ret_s"] for b in range(B)])[None]
        out["hg_p"] = np.stack([R[b]["hg_p"] for b in range(B)])[None]
        out["hg_s"] = np.concatenate([R[b]["hg_s"] for b in range(B)])[None]
    if _debug:
        return out
    f = lambda a: np.ascontiguousarray(a, dtype=np.float32)
    return (f(out["y_p"]), f(out["y_s"]), f(out["s5re_p"]), f(out["s5im_p"]), f(out["ret_p"]), f(out["hg_p"]), f(out["conv_p"]),
            f(out["s5re_s"]), f(out["s5im_s"]), f(out["ret_s"]), f(out["hg_s"]), f(out["conv_s"]))


MAGIC = 12582912.0
TWO_PI_INV = float(1.0 / (2.0 * np.pi))
C1 = 6.28125
C2 = 0.0019353071795864769
PI = float(np.pi)


def _pbank(self):
    b = self.ps[self._pb]
    self._pb = (self._pb + 1) % 8
    return b


def _range_reduce(self, eng, out, ang, tmp, reads, tiles_w):
    o, t = out, tmp
    self.ts(eng, t.ap, ang, TWO_PI_INV, MAGIC, ALU.mult, ALU.add, reads, [t])
    self.ts(eng, t.ap, t.ap, -MAGIC, None, ALU.add, None, [t], [t])
    self.stt(o.ap, t.ap, -C1, ang, ALU.mult, ALU.add, [t] + list(reads), [o])
    self.stt(o.ap, t.ap, -C2, o.ap, ALU.mult, ALU.add, [t, o], [o])
    self.ts("dve", o.ap, o.ap, -PI, PI, ALU.max, ALU.min, [o], [o])


def _sincos(self, r, sin_out, cos_out, tmp, sin_scale=None, extra_reads=()):
    if sin_scale is None:
        self.act(sin_out.ap, r.ap, AF.Sin, [r], [sin_out])
    else:
        self.act(sin_out.ap, r.ap, AF.Sin, [r] + list(extra_reads), [sin_out], scale=sin_scale)
    self.ts("dve", tmp.ap, r.ap, -1.0, None, ALU.mult, None, [r], [tmp])
    self.tt("dve", tmp.ap, tmp.ap, r.ap, ALU.max, [tmp, r], [tmp])
    self.ts("dve", tmp.ap, tmp.ap, -1.0, PI / 2, ALU.mult, ALU.add, [tmp], [tmp])
    self.act(cos_out.ap, tmp.ap, AF.Sin, [tmp], [cos_out])


Prog.pbank = _pbank
Prog.range_reduce = _range_reduce
Prog.sincos = _sincos


def _mixer_ab(self, seg):
    ar = self.arena
    m0 = ar.mark()
    chunks = self.chunks(seg)
    ws = [512, 512, NSMP]
    yTbuf = [ar.alloc("yT%d" % c, [KT, ws[c]], BF16) for c in range(3)]
    yTs = [Tile("yTs%d" % c, yTbuf[c].ap[:, 0:4, :]) for c in range(3)]
    yTr = [Tile("yTr%d" % c, yTbuf[c].ap[:, 4:8, :]) for c in range(3)]
    uT = [ar.alloc("uT%d" % c, [4, ws[c]], BF16) for c in range(3)]
    m1 = ar.mark()
    self.hT = self.alloc_hT()
    rs_saved = self.rs
    self.rs = Rot([rs_saved.tiles[0], ar.alloc("rs_x", [512])])
    self.rmsnorm(seg, 0, self.hT)
    self.rs = rs_saved
    wt = Rot([ar.alloc("wu5_%d" % i, [KT, 128], BF16) for i in range(2)])
    for ft in range(4):
        w = wt.next()
        self.load_w(w, self.w_in_ab[:, ft * 128:(ft + 1) * 128], KT, 128)
        for (c, c0, n) in chunks:
            b = self.pbank()
            for k in range(KT):
                self.mm(b.ap[:, 0:n], w.ap[:, k, :], self.hT[c].ap[:, k, 0:n], k == 0, k == KT - 1, [w, self.hT[c]], [b])
            self.cp("act", uT[c].ap[:, ft, 0:n], b.ap[:, 0:n], [b], [uT[c]])
    if self.debug.get("no_ret"):
        for (c, c0, n) in chunks:
            self.op("dve", partial(self.nc.vector.memset, yTr[c].ap, 0.0), [], [yTr[c]])
    else:
        self.ret_part(seg, yTr)
    self.sch.barrier()
    ar.release(m1)
    if self.debug.get("no_s5"):
        for (c, c0, n) in chunks:
            self.op("dve", partial(self.nc.vector.memset, yTs[c].ap, 0.0), [], [yTs[c]])
    else:
        self.s5_part(seg, yTs, uT)
    self.sch.barrier()
    ar.release(m1)
    wout = ar.alloc("wout", [KT, D], BF16)
    self.load_w(wout, self.w_out_ab, KT, D)
    for (c, c0, n) in chunks:
        for mo in range(KT):
            b = self.pbank()
            for k in range(KT):
                src = yTs[c] if k < 4 else yTr[c]
                self.mm(b.ap[:, 0:n], wout.ap[:, k, mo * 128:(mo + 1) * 128], yTbuf[c].ap[:, k, 0:n], k == 0, k == KT - 1,
                        [wout, src], [b])
            xv = self.xcols(c, mo, mo + 1)[:, 0, :]
            self.tt("dve", xv, xv, b.ap[:, 0:n], ALU.add, [self.xT[c], b], [self.xT[c]])
    self.sch.barrier()
    ar.release(m0)


Prog.mixer_ab = _mixer_ab


def _s5_part(self, seg, yTs, uT):
    nc = self.nc
    ar = self.arena
    ident = self.C("ident")
    V = "dve"

    def T16(name):
        return ar.alloc(name, [16])

    sp = ar.alloc("s5sp", [48])
    self.dma(sp.ap, self.s5_sp, writes=[sp])
    lre, lim, ldt = sp.ap[:, 0:16], sp.ap[:, 16:32], sp.ap[:, 32:48]
    dt, mag, ang, r, tmp, sinA, cosA = [T16(n) for n in ["dt", "mag", "ang", "r", "tmp", "sinA", "cosA"]]
    self.act(dt.ap, ldt, AF.Exp, [sp], [dt])
    self.tt(V, tmp.ap, lre, dt.ap, ALU.mult, [sp, dt], [tmp])
    self.act(mag.ap, tmp.ap, AF.Exp, [tmp], [mag])
    self.tt(V, ang.ap, lim, dt.ap, ALU.mult, [sp, dt], [ang])
    tmp2 = T16("tmp2")
    self.range_reduce(V, r, ang.ap, tmp2, [ang], None)
    tmp3 = T16("tmp3")
    self.sincos(r, sinA, cosA, tmp3)
    names = ["ab_re", "ab_im", "nr", "t1", "t2", "den", "rden", "f_re", "f_im", "if_re", "if_im"]
    ab_re, ab_im, nr, t1, t2, den, rden, f_re, f_im, if_re, if_im = [T16(n) for n in names]
    self.tt(V, ab_re.ap, mag.ap, cosA.ap, ALU.mult, [mag, cosA], [ab_re])
    self.tt(V, ab_im.ap, mag.ap, sinA.ap, ALU.mult, [mag, sinA], [ab_im])
    self.ts(V, nr.ap, ab_re.ap, -1.0, None, ALU.add, None, [ab_re], [nr])
    self.tt(V, t1.ap, lre, lre, ALU.mult, [sp], [t1])
    self.tt(V, t2.ap, lim, lim, ALU.mult, [sp], [t2])
    self.tt(V, den.ap, t1.ap, t2.ap, ALU.add, [t1, t2], [den])
    self.op(V, partial(nc.vector.reciprocal, out=rden.ap, in_=den.ap), [den], [rden])
    self.tt(V, t1.ap, nr.ap, lre, ALU.mult, [nr, sp], [t1])
    self.tt(V, t2.ap, ab_im.ap, lim, ALU.mult, [ab_im, sp], [t2])
    self.tt(V, t1.ap, t1.ap, t2.ap, ALU.add, [t1, t2], [t1])
    self.tt(V, f_re.ap, t1.ap, rden.ap, ALU.mult, [t1, rden], [f_re])
    self.tt(V, t1.ap, ab_im.ap, lre, ALU.mult, [ab_im, sp], [t1])
    self.tt(V, t2.ap, nr.ap, lim, ALU.mult, [nr, sp], [t2])
    self.tt(V, t1.ap, t1.ap, t2.ap, ALU.subtract, [t1, t2], [t1])
    self.tt(V, f_im.ap, t1.ap, rden.ap, ALU.mult, [t1, rden], [f_im])
    self.tt(V, t1.ap, f_re.ap, f_re.ap, ALU.mult, [f_re], [t1])
    self.tt(V, t2.ap, f_im.ap, f_im.ap, ALU.mult, [f_im], [t2])
    self.tt(V, den.ap, t1.ap, t2.ap, ALU.add, [t1, t2], [den])
    self.op(V, partial(nc.vector.reciprocal, out=rden.ap, in_=den.ap), [den], [rden])
    self.tt(V, if_re.ap, f_re.ap, rden.ap, ALU.mult, [f_re, rden], [if_re])
    self.tt(V, t1.ap, f_im.ap, rden.ap, ALU.mult, [f_im, rden], [t1])
    self.ts(V, if_im.ap, t1.ap, -1.0, None, ALU.mult, None, [t1], [if_im])

    R = ar.alloc("R", [4096])
    u = [Tile("u%d" % i, R.ap[:, i * 1024:(i + 1) * 1024].rearrange("p (a b) -> p a b", a=16)) for i in range(4)]
    u2 = [Tile("w%d" % i, R.ap[:, i * 2048:(i + 1) * 2048].rearrange("p (a b) -> p a b", a=16)) for i in range(2)]
    ysb_t = [Tile("ysb%d" % i, R.ap[:, i * 2048:(i + 1) * 2048].rearrange("p (a b) -> p a b", a=4)) for i in range(2)]

    cE = ar.alloc("cE", [16, 128])
    sE = ar.alloc("sE", [16, 128])
    self.op(V, partial(nc.vector.memset, cE.ap[:, :, 0:1], 1.0), [], [cE])
    self.op(V, partial(nc.vector.memset, sE.ap[:, :, 0:1], 0.0), [], [sE])
    self.cp(V, cE.ap[:, :, 1:2], cosA.ap.unsqueeze(2), [cosA], [cE])
    self.cp(V, sE.ap[:, :, 1:2], sinA.ap.unsqueeze(2), [sinA], [sE])
    p_re, p_im = cosA, sinA
    pw = [(T16("pwr%d" % i), T16("pwi%d" % i)) for i in range(7)]
    n = 2
    for i in range(7):
        q_re, q_im = pw[i]
        self.tt(V, t1.ap, p_re.ap, p_re.ap, ALU.mult, [p_re], [t1])
        self.tt(V, t2.ap, p_im.ap, p_im.ap, ALU.mult, [p_im], [t2])
        self.tt(V, q_re.ap, t1.ap, t2.ap, ALU.subtract, [t1, t2], [q_re])
        self.tt(V, t1.ap, p_re.ap, p_im.ap, ALU.mult, [p_re, p_im], [t1])
        self.ts(V, q_im.ap, t1.ap, 2.0, None, ALU.mult, None, [t1], [q_im])
        if n <= 64:
            qr = q_re.ap.unsqueeze(2).broadcast_to([128, 16, n])
            qi = q_im.ap.unsqueeze(2).broadcast_to([128, 16, n])
            sc_, ss_ = cE.ap[:, :, 0:n], sE.ap[:, :, 0:n]
            self.tt(V, u[0].ap[:, :, 0:n], sc_, qr, ALU.mult, [cE, q_re], [u[0]])
            self.tt(V, u[1].ap[:, :, 0:n], ss_, qi, ALU.mult, [sE, q_im], [u[1]])
            self.tt(V, u[2].ap[:, :, 0:n], sc_, qi, ALU.mult, [cE, q_im], [u[2]])
            self.tt(V, u[3].ap[:, :, 0:n], ss_, qr, ALU.mult, [sE, q_re], [u[3]])
            self.tt(V, cE.ap[:, :, n:2 * n], u[0].ap[:, :, 0:n], u[1].ap[:, :, 0:n], ALU.subtract, [u[0], u[1]], [cE])
            self.tt(V, sE.ap[:, :, n:2 * n], u[2].ap[:, :, 0:n], u[3].ap[:, :, 0:n], ALU.add, [u[2], u[3]], [sE])
        p_re, p_im = q_re, q_im
        n *= 2
    c128, s128 = T16("c128r"), T16("s128r")
    self.tt(V, c128.ap, p_re.ap, mag.ap, ALU.mult, [p_re, mag], [c128])
    self.tt(V, s128.ap, p_im.ap, mag.ap, ALU.mult, [p_im, mag], [s128])
    irho = T16("irho")
    self.act(irho.ap, tmp.ap, AF.Exp, [tmp], [irho], scale=-1.0)
    rho_t = ar.alloc("rho_t", [16, 128])
    self.cp(V, rho_t.ap, mag.ap.unsqueeze(2).broadcast_to([128, 16, 128]), [mag], [rho_t])
    self.op(V, partial(nc.vector.memset, rho_t.ap[:, :, 0:1], 0.0), [], [rho_t])

    BT = [ar.alloc("BT%d" % i, [16, 128], BF16) for i in range(2)]
    CT = [ar.alloc("CT%d" % i, [16, 128], BF16) for i in range(2)]
    for i in range(2):
        for hf in range(2):
            st = self.wstage.next()
            self.dma(st.ap, self.s5_BT[i][:, hf * 1024:(hf + 1) * 1024], writes=[st])
            self.cp("act", BT[i].ap.rearrange("p a b -> p (a b)")[:, hf * 1024:(hf + 1) * 1024], st.ap, [st], [BT[i]])
    cst = [self.wstage.next() for _ in range(4)]
    for i in range(2):
        for hf in range(2):
            self.dma(cst[i * 2 + hf].ap, self.s5_CT[i][:, hf * 1024:(hf + 1) * 1024], writes=[cst[i * 2 + hf]])
    for hf in range(2):
        js = slice(hf * 8, (hf + 1) * 8)
        cre = cst[hf].ap.rearrange("p (a b) -> p a b", a=8)
        cim = cst[2 + hf].ap.rearrange("p (a b) -> p a b", a=8)
        fr = f_re.ap[:, js].unsqueeze(2).broadcast_to([128, 8, 128])
        fi = f_im.ap[:, js].unsqueeze(2).broadcast_to([128, 8, 128])
        w0, w1 = u2[0].ap[:, 0:8, :], u2[1].ap[:, 0:8, :]
        self.tt(V, w0, cre, fr, ALU.mult, [cst[hf], f_re] + u, [u2[0]])
        self.tt(V, w1, cim, fi, ALU.mult, [cst[2 + hf], f_im] + u, [u2[1]])
        self.tt(V, CT[0].ap[:, js, :], w0, w1, ALU.subtract, [u2[0], u2[1]], [CT[0]])
        self.tt(V, w0, cre, fi, ALU.mult, [cst[hf], f_im], [u2[0]])
        self.tt(V, w1, cim, fr, ALU.mult, [cst[2 + hf], f_re], [u2[1]])
        self.tt(V, CT[1].ap[:, js, :], w0, w1, ALU.add, [u2[0], u2[1]], [CT[1]])
    self.ts(V, CT[1].ap, CT[1].ap, -1.0, None, ALU.mult, None, [CT[1]], [CT[1]])
    CTn0 = ar.alloc("CTn0", [16, 128], BF16)
    self.ts(V, CTn0.ap, CT[0].ap, -1.0, None, ALU.mult, None, [CT[0]], [CTn0])
    gluw = ar.alloc("gluw", [4, 512], BF16)
    self.load_w(gluw, self.glu_w, 4, 512)

    tA = Rot([ar.alloc("tA%d" % i, [512]) for i in range(2)])
    tB = Rot([ar.alloc("tB%d" % i, [512]) for i in range(1)])
    ygbs = Rot([ar.alloc("ygb%d" % i, [4, 512], BF16) for i in range(1)])
    glub = CL["glu_b"][0]
    s5d = CL["s5_d"][0]

    def glu(y_t, c, n, uTc=None):
        ygb = ygbs.next()
        for ft in range(4):
            a, bq = tA.next(), tB.next()
            yv = y_t.ap[:, ft, 0:n]
            if uTc is not None:
                self.stt(yv, uTc.ap[:, ft, 0:n], self.consts.ap[:, s5d + ft: s5d + ft + 1], yv, ALU.mult, ALU.add,
                         [uTc, self.consts, y_t], [y_t])
            self.act(a.ap[:, 0:n], yv, AF.Square, [y_t], [a])
            self.ts("pool", bq.ap[:, 0:n], a.ap[:, 0:n], 0.044715, 1.0, ALU.mult, ALU.add, [a], [bq])
            self.tt("pool", bq.ap[:, 0:n], bq.ap[:, 0:n], yv, ALU.mult, [bq, y_t], [bq])
            self.act(a.ap[:, 0:n], bq.ap[:, 0:n], AF.Sigmoid, [bq], [a], scale=1.5957691216057308)
            self.tt("dve", ygb.ap[:, ft, 0:n], a.ap[:, 0:n], yv, ALU.mult, [a, y_t], [ygb])
        for mo in range(4):
            b = self.pbank()
            for k in range(4):
                self.mm(b.ap[:, 0:n], gluw.ap[:, k, mo * 128:(mo + 1) * 128], ygb.ap[:, k, 0:n], k == 0, k == 3,
                        [gluw, ygb], [b])
            a = tA.next()
            self.act(a.ap[:, 0:n], b.ap[:, 0:n], AF.Sigmoid, [b, self.consts], [a],
                     bias=self.consts.ap[:, glub + mo: glub + mo + 1])
            self.tt("pool", yTs[c].ap[:, mo, 0:n], ygb.ap[:, mo, 0:n], a.ap[:, 0:n], ALU.mult, [ygb, a], [yTs[c]])

    ysb = Rot(ysb_t)
    mloop = ar.mark()
    tq = [ar.alloc("tq%d" % i, [512]) for i in range(2)]
    tq = tq + tq
    bt = [ar.alloc("bt%d" % i, [512]) for i in range(2)]
    sts = Rot([(ar.alloc("str%d" % i, [512]), ar.alloc("sti%d" % i, [512])) for i in range(2)])
    sbs = Rot([tuple(ar.alloc("spr%d_%d" % (i, q), [512], BF16) for q in range(4)) for i in range(2)])
    pc = [ar.alloc("pc%d" % i, [4]) for i in range(4)]
    pcv = [ar.alloc("pcv%d" % i, [4]) for i in range(2)]

    carry = [Tile("s5c%d" % ft, self.s5carry.ap[:, :, ft * 4:(ft + 1) * 4]) for ft in range(4)]
    flat = lambda t, ft: t.ap[:, ft * 4:(ft + 1) * 4, :].rearrange("p a b -> p (a b)")
    y_t = None
    pending = None
    for cc in range(SEGT // 128):
        c, off = cc // 4, (cc % 4) * 128
        if cc % 4 == 0:
            y_t = ysb.next()
        for ft in range(4):
            bre, bim = self.pbank(), self.pbank()
            for jl in range(4):
                j = ft * 4 + jl
                self.mm(bre.ap[:, jl * 128:(jl + 1) * 128], BT[0].ap[:, j, :], uT[c].ap[:, ft, off:off + 128], True, True,
                        [BT[0], uT[c]], [bre])
                self.mm(bim.ap[:, jl * 128:(jl + 1) * 128], BT[1].ap[:, j, :], uT[c].ap[:, ft, off:off + 128], True, True,
                        [BT[1], uT[c]], [bim])
            ce, se = flat(cE, ft), flat(sE, ft)
            self.tt(V, bt[0].ap, bre.ap, ce, ALU.mult, [bre, cE], [bt[0]])
            self.tt(V, bt[1].ap, bim.ap, ce, ALU.mult, [bim, cE], [bt[1]])
            self.tt(V, tq[0].ap, bim.ap, se, ALU.mult, [bim, sE], [tq[0]])
            self.tt(V, tq[1].ap, bre.ap, se, ALU.mult, [bre, sE], [tq[1]])
            self.tt(V, bt[0].ap, bt[0].ap, tq[0].ap, ALU.add, [bt[0], tq[0]], [bt[0]])
            self.tt(V, bt[1].ap, bt[1].ap, tq[1].ap, ALU.subtract, [bt[1], tq[1]], [bt[1]])
            st_r, st_i = sts.next()
            for ri in range(2):
                b0 = bt[ri].ap.rearrange("p (a b) -> p a b", a=4)[:, :, 0]
                self.tt(V, b0, b0, carry[ft].ap[:, ri, :], ALU.add, [bt[ri], carry[ft]], [bt[ri]])
            for ri, stt_ in ((0, st_r), (1, st_i)):
                self.op(V, partial(nc.vector.tensor_tensor_scan, out=stt_.ap, data0=flat(rho_t, ft), data1=bt[ri].ap,
                                   initial=0.0, op0=ALU.mult, op1=ALU.add), [rho_t, bt[ri]], [stt_])
            sr = st_r.ap.rearrange("p (a b) -> p a b", a=4)[:, :, 127]
            si = st_i.ap.rearrange("p (a b) -> p a b", a=4)[:, :, 127]
            cc_, ss_ = c128.ap[:, ft * 4:(ft + 1) * 4], s128.ap[:, ft * 4:(ft + 1) * 4]
            P = "pool"
            self.tt(P, pc[0].ap, sr, cc_, ALU.mult, [st_r, c128], [pc[0]])
            self.tt(P, pc[1].ap, si, ss_, ALU.mult, [st_i, s128], [pc[1]])
            self.tt(P, pc[2].ap, sr, ss_, ALU.mult, [st_r, s128], [pc[2]])
            self.tt(P, pc[3].ap, si, cc_, ALU.mult, [st_i, c128], [pc[3]])
            self.tt(P, carry[ft].ap[:, 0, :], pc[0].ap, pc[1].ap, ALU.subtract, [pc[0], pc[1]], [carry[ft]])
            self.tt(P, carry[ft].ap[:, 1, :], pc[2].ap, pc[3].ap, ALU.add, [pc[2], pc[3]], [carry[ft]])
            p1, p2, p3, p4 = sbs.next()
            self.tt(V, p1.ap, st_r.ap, ce, ALU.mult, [st_r, cE], [p1])
            self.tt(V, p2.ap, st_i.ap, se, ALU.mult, [st_i, sE], [p2])
            self.tt(P, p3.ap, st_r.ap, se, ALU.mult, [st_r, sE], [p3])
            self.tt(P, p4.ap, st_i.ap, ce, ALU.mult, [st_i, cE], [p4])
            def fin(ft=ft, ps_=(p1, p2, p3, p4), y_t=y_t, off=off, c=c, last=(cc % 4 == 3 and ft == 3)):
                yb = self.pbank()
                ws_ = (CT[0], CTn0, CT[1], CT[1])
                for jl in range(4):
                    j = ft * 4 + jl
                    for q in range(4):
                        self.mm(yb.ap[:, 0:128], ws_[q].ap[:, j, :], ps_[q].ap[:, jl * 128:(jl + 1) * 128],
                                jl == 0 and q == 0, jl == 3 and q == 3, [ws_[q], ps_[q]], [yb])
                self.cp("act", y_t.ap[:, ft, off:off + 128], yb.ap[:, 0:128], [yb] + u + u2, [y_t])
                if last:
                    glu(y_t, c, 512, uT[c])
            if pending is not None:
                pending()
            pending = fin
    if pending is not None:
        pending()

    if seg == NSEG - 1:
        fin = [T16("fin_re"), T16("fin_im")]
        g_re, g_im = T16("g_re"), T16("g_im")
        crt, cit = T16("crt"), T16("cit")
        self.tt(V, crt.ap, self.s5carry.ap[:, 0, :], irho.ap, ALU.mult, carry + [irho], [crt])
        self.tt(V, cit.ap, self.s5carry.ap[:, 1, :], irho.ap, ALU.mult, carry + [irho], [cit])
        cr = crt.ap
        ci = cit.ap
        carry = carry + [crt, cit]
        self.tt(V, t1.ap, cr, cosA.ap, ALU.mult, carry + [cosA], [t1])
        self.tt(V, t2.ap, ci, sinA.ap, ALU.mult, carry + [sinA], [t2])
        self.tt(V, g_re.ap, t1.ap, t2.ap, ALU.add, [t1, t2], [g_re])
        self.tt(V, t1.ap, ci, cosA.ap, ALU.mult, carry + [cosA], [t1])
        self.tt(V, t2.ap, cr, sinA.ap, ALU.mult, carry + [sinA], [t2])
        self.tt(V, g_im.ap, t1.ap, t2.ap, ALU.subtract, [t1, t2], [g_im])
        self.tt(V, t1.ap, g_re.ap, f_re.ap, ALU.mult, [g_re, f_re], [t1])
        self.tt(V, t2.ap, g_im.ap, f_im.ap, ALU.mult, [g_im, f_im], [t2])
        self.tt(V, fin[0].ap, t1.ap, t2.ap, ALU.subtract, [t1, t2], [fin[0]])
        self.tt(V, t1.ap, g_re.ap, f_im.ap, ALU.mult, [g_re, f_im], [t1])
        self.tt(V, t2.ap, g_im.ap, f_re.ap, ALU.mult, [g_im, f_re], [t2])
        self.tt(V, fin[1].ap, t1.ap, t2.ap, ALU.add, [t1, t2], [fin[1]])
        fo = ar.alloc("fo", [2, 128])
        for ri, dst in ((0, self.s5re_p), (1, self.s5im_p)):
            b = self.pbank()
            self.tr(b.ap[0:16, 0:128], fin[ri].ap, ident, [fin[ri], self.consts], [b])
            self.cp("act", fo.ap[0:16, ri, :], b.ap[0:16, 0:128], [b], [fo])
            self.dma(dst, fo.ap[0:16, ri, :], reads=[fo])

    if seg == 0 and not self.debug.get("no_sample_mix"):
        self.sch.barrier()
        ar.release(mloop)
        self.s5_sample(uT[2], BT, CT, f_re, f_im, if_re, if_im, ab_re, ab_im, glu, t1, t2)


Prog.s5_part = _s5_part


def _s5_sample(self, uTs, BT, CT, f_re, f_im, if_re, if_im, ab_re, ab_im, glu, t1, t2):
    nc = self.nc
    ar = self.arena
    ident = self.C("ident")
    V = "dve"
    s5d = CL["s5_d"][0]
    st = [ar.alloc("sst%d" % ri, [16, 16]) for ri in range(2)]
    xs = [ar.alloc("sxs%d" % ri, [16, 16]) for ri in range(2)]
    sin_rot = Rot([ar.alloc("s5in%d" % i, [512]) for i in range(1)])
    for ri, src in ((0, self.s5re0), (1, self.s5im0)):
        b = self.pbank()
        for q in range(4):
            t = sin_rot.next()
            self.dma(t.ap[0:NSS, :], src[:, q * 512:(q + 1) * 512], writes=[t])
            for jl in range(4):
                j = q * 4 + jl
                self.tr(b.ap[:, j * 16:(j + 1) * 16], t.ap[0:NSS, jl * 128:(jl + 1) * 128], ident[0:NSS, 0:NSS],
                        [t, self.consts], [b])
        self.cp("act", st[ri].ap, b.ap[:, 0:256].rearrange("p (a b) -> p a b", a=16), [b], [st[ri]])
    v = [ar.alloc("sv%d" % i, [16, 16]) for i in range(4)]
    bc = lambda t: t.ap.unsqueeze(2).broadcast_to([128, 16, 16])

    def cmul(o_re, o_im, a_re, a_im, b_re, b_im, rd):
        self.tt(V, v[0].ap, a_re.ap, b_re, ALU.mult, [a_re] + rd, [v[0]])
        self.tt(V, v[1].ap, a_im.ap, b_im, ALU.mult, [a_im] + rd, [v[1]])
        self.tt(V, v[2].ap, a_re.ap, b_im, ALU.mult, [a_re] + rd, [v[2]])
        self.tt(V, v[3].ap, a_im.ap, b_re, ALU.mult, [a_im] + rd, [v[3]])
        self.tt(V, o_re.ap, v[0].ap, v[1].ap, ALU.subtract, [v[0], v[1]], [o_re])
        self.tt(V, o_im.ap, v[2].ap, v[3].ap, ALU.add, [v[2], v[3]], [o_im])

    cmul(xs[0], xs[1], st[0], st[1], bc(if_re), bc(if_im), [if_re, if_im])
    braw = [ar.alloc("braw%d" % ri, [16, NSMP]) for ri in range(2)]
    for ft in range(4):
        bre, bim = self.pbank(), self.pbank()
        for jl in range(4):
            j = ft * 4 + jl
            self.mm(bre.ap[:, jl * 64:(jl + 1) * 64], BT[0].ap[:, j, :], uTs.ap[:, ft, 0:NSMP], True, True, [BT[0], uTs], [bre])
            self.mm(bim.ap[:, jl * 64:(jl + 1) * 64], BT[1].ap[:, j, :], uTs.ap[:, ft, 0:NSMP], True, True, [BT[1], uTs], [bim])
        self.cp("act", braw[0].ap[:, ft * 4:(ft + 1) * 4, :], bre.ap[:, 0:256].rearrange("p (a b) -> p a b", a=4), [bre], [braw[0]])
        self.cp("act", braw[1].ap[:, ft * 4:(ft + 1) * 4, :], bim.ap[:, 0:256].rearrange("p (a b) -> p a b", a=4), [bim], [braw[1]])
    ssb = [ar.alloc("ssb%d" % ri, [16, NSMP], BF16) for ri in range(2)]
    nx = [ar.alloc("snx%d" % ri, [16, 16]) for ri in range(2)]
    for tau in range(4):
        cmul(nx[0], nx[1], xs[0], xs[1], bc(ab_re), bc(ab_im), [ab_re, ab_im])
        for ri in range(2):
            bv = braw[ri].ap.rearrange("p a (s t) -> p a s t", t=4)[:, :, :, tau]
            self.tt(V, xs[ri].ap, nx[ri].ap, bv, ALU.add, [nx[ri], braw[ri]], [xs[ri]])
            self.cp("act", ssb[ri].ap.rearrange("p a (s t) -> p a s t", t=4)[:, :, :, tau], xs[ri].ap, [xs[ri]], [ssb[ri]])
    y_s = ar.alloc("y_s5s", [4, NSMP])
    for ft in range(4):
        yb = self.pbank()
        for jl in range(4):
            j = ft * 4 + jl
            self.mm(yb.ap[:, 0:NSMP], CT[0].ap[:, j, :], ssb[0].ap[:, j, :], jl == 0, False, [CT[0], ssb[0]], [yb])
            self.mm(yb.ap[:, 0:NSMP], CT[1].ap[:, j, :], ssb[1].ap[:, j, :], False, jl == 3, [CT[1], ssb[1]], [yb])
        self.stt(y_s.ap[:, ft, :], uTs.ap[:, ft, 0:NSMP], self.consts.ap[:, s5d + ft: s5d + ft + 1], yb.ap[:, 0:NSMP],
                 ALU.mult, ALU.add, [uTs, self.consts, yb], [y_s])
    glu(y_s, 2, NSMP)
    cmul(st[0], st[1], xs[0], xs[1], bc(f_re), bc(f_im), [f_re, f_im])
    so = sin_rot
    for ri, dst in ((0, self.s5re_s), (1, self.s5im_s)):
        for q in range(4):
            b = self.pbank()
            for jl in range(4):
                j = q * 4 + jl
                self.tr(b.ap[0:NSS, jl * 128:(jl + 1) * 128], st[ri].ap[:, j, :], ident, [st[ri], self.consts], [b])
            o = so.next()
            self.cp("act", o.ap[0:NSS, :], b.ap[0:NSS, :], [b], [o])
            self.dma(dst[:, q * 512:(q + 1) * 512], o.ap[0:NSS, :], reads=[o])


Prog.s5_sample = _s5_sample


def _ret_part(self, seg, yTr):
    nc = self.nc
    ar = self.arena
    ident = self.C("ident")
    chunks = self.chunks(seg)
    W = W0 if seg == 0 else SEGT
    V = "dve"
    sinT = ar.alloc("sinT", [W0])
    cosT = ar.alloc("cosT", [W0])
    mt = ar.mark()
    posf = ar.alloc("posf", [W0])
    rr = ar.alloc("rr", [W0])
    tmp = ar.alloc("rtmp", [W0])
    self.dma(posf.ap[:, 0:SEGT], self.cpos_d, writes=[posf])
    if seg > 0:
        self.ts(V, posf.ap[:, 0:SEGT], posf.ap[:, 0:SEGT], float(seg * SEGT), None, ALU.add, None, [posf], [posf])
    if seg == 0:
        posi = ar.alloc("posi", [NSS], I32)
        posff = ar.alloc("posff", [NSS])
        self.dma(posi.ap, self.pos_d, writes=[posi])
        self.cp(V, posff.ap, posi.ap, [posi], [posff])
        pv = posf.ap[:, SEGT:W0].rearrange("p (s t) -> p s t", t=4)
        for tau in range(4):
            self.ts(V, pv[:, :, tau], posff.ap, float(tau), None, ALU.add, None, [posff, posf], [posf])
    invf = self.C("inv_freq")
    self.ts(V, posf.ap[:, 0:W], posf.ap[:, 0:W], invf, None, ALU.mult, None, [posf, self.consts], [posf])
    rr_v = Tile("rr_v", rr.ap[:, 0:W])
    tmp_v = Tile("tmp_v", tmp.ap[:, 0:W])
    self.range_reduce(V, rr_v, posf.ap[:, 0:W], tmp_v, [posf], None)
    sin_v = Tile("sin_v", sinT.ap[:, 0:W])
    cos_v = Tile("cos_v", cosT.ap[:, 0:W])
    self.sincos(rr_v, sin_v, cos_v, tmp_v, sin_scale=self.C("sgn"), extra_reads=[self.consts])
    sinT, cosT = sin_v, cos_v
    self.sch.barrier()
    ar.release(mt)

    Wv = ar.alloc("Wv", [KT, 512], BF16)
    Wg = ar.alloc("Wg", [KT, 512], BF16)
    self.load_w(Wv, self.w_in_ab[:, 1536:2048], KT, 512)
    self.load_w(Wg, self.w_in_ab[:, 2048:2560], KT, 512)
    wqk = Rot([ar.alloc("wqk%d" % i, [KT, 128], BF16) for i in range(4)])
    comp = {}

    def load_sw(dst, c0):
        srcv = self.w_in_ab.rearrange("(k p) n -> p k n", p=128)
        st = self.wstage.next()
        if st.name not in comp:
            comp[st.name] = Tile(st.name + "_c")
        st2 = comp[st.name]
        sv = st.ap[:, 0:KT * 128].rearrange("p (k n) -> p k n", n=128)
        self.dma(sv[:, :, 0:64], srcv[:, :, c0 + 64:c0 + 128], writes=[st])
        self.dma(sv[:, :, 64:128], srcv[:, :, c0:c0 + 64], writes=[st2])
        self.cp("act", dst.ap, sv, [st, st2], [dst])

    ws = [512, 512, NSMP]
    qT = ar.alloc("qT", [4, 512], BF16)
    kT = ar.alloc("kT", [4, 512], BF16)
    qdT = ar.alloc("qdT", [4, 512], BF16)
    qs32 = ar.alloc("qs32", [4, NSMP])
    qd32 = ar.alloc("qd32", [4, NSMP])
    rt = Rot([ar.alloc("rt%d" % i, [512]) for i in range(2)])
    v_toks = Rot([ar.alloc("vtok%d" % i, [512], BF16) for i in range(2)])
    sgs = Rot([ar.alloc("sg%d" % i, [512]) for i in range(2)])
    scms = Rot([ar.alloc("scm%d" % i, [4, 128], BF16) for i in range(2)])
    kds = Rot([ar.alloc("kd%d" % i, [4, 128], BF16) for i in range(2)])
    ytoks = Rot([ar.alloc("ytok%d" % i, [512], BF16) for i in range(2)])
    junk = ar.alloc("junk", [512])
    sss = Rot([ar.alloc("ss%d" % i, [4]) for i in range(2)])
    maskT = self.C("maskT").rearrange("p (h i) -> p h i", h=4)
    maskS = self.C("maskS", 64).rearrange("p (h i) -> p h i", h=4)
    qdec = self.C("qdec").rearrange("p (h i) -> p h i", h=4)
    qdecS = self.C("qdecS").rearrange("p (h i) -> p h i", h=4)
    kdec = self.C("kdec")
    kdecS = self.C("kdecS", 64)
    g128 = self.C("g128")
    pSs = Rot([ar.alloc("rpS%d" % i, [4, 128]) for i in range(1)])
    identb = self.identb
    eps_ap = self.C("eps")

    def epilogue(ob, sg, rows, c, off):
        ss = sss.next()
        for h in range(4):
            self.act(junk.ap[0:rows, h * 128:(h + 1) * 128], ob.ap[0:rows, h * 128:(h + 1) * 128], AF.Square, [ob], [junk, ss],
                     accum_out=ss.ap[0:rows, h:h + 1])
        self.act(ss.ap[0:rows, :], ss.ap[0:rows, :], AF.Ln, [junk, self.consts], [ss], bias=eps_ap[0:rows, :], scale=1.0 / 128)
        self.act(ss.ap[0:rows, :], ss.ap[0:rows, :], AF.Exp, [ss], [ss], scale=-0.5)
        y_tok = ytoks.next()
        for h in range(4):
            self.stt(y_tok.ap[0:rows, h * 128:(h + 1) * 128], ob.ap[0:rows, h * 128:(h + 1) * 128], ss.ap[0:rows, h:h + 1],
                     sg.ap[0:rows, h * 128:(h + 1) * 128], ALU.mult, ALU.mult, [ob, ss, sg], [y_tok])
        yb = self.pbank()
        ybf = yb.ap.bitcast(BF16)
        for h in range(4):
            self.tr(ybf[:, h * 128:h * 128 + rows], y_tok.ap[0:rows, h * 128:(h + 1) * 128], identb.ap[0:rows, 0:rows],
                    [y_tok, identb], [yb])
        self.cp("act", yTr[c].ap[:, :, off:off + rows], ybf[:, 0:512].rearrange("p (h t) -> p h t", h=4)[:, :, 0:rows],
                [yb], [yTr[c]])

    for (c, c0, n) in chunks:
        for which, base, dst in (("q", 512, qT), ("k", 1024, kT)):
            for h in range(4):
                wn, wsw = wqk.next(), wqk.next()
                st = self.wstage.next()
                sv = st.ap[:, 0:KT * 128].rearrange("p (k n) -> p k n", n=128)
                c0w = base + h * 128
                self.dma(sv, self.w_in_ab.rearrange("(k p) n -> p k n", p=128)[:, :, c0w:c0w + 128], writes=[st])
                self.cp("act", wn.ap, sv, [st], [wn])
                self.cp("act", wsw.ap[:, :, 0:64], sv[:, :, 64:128], [st], [wsw])
                self.cp("act", wsw.ap[:, :, 64:128], sv[:, :, 0:64], [st], [wsw])
                pn, psw = self.pbank(), self.pbank()
                for k in range(KT):
                    self.mm(pn.ap[:, 0:n], wn.ap[:, k, :], self.hT[c].ap[:, k, 0:n], k == 0, k == KT - 1, [wn, self.hT[c]], [pn])
                for k in range(KT):
                    self.mm(psw.ap[:, 0:n], wsw.ap[:, k, :], self.hT[c].ap[:, k, 0:n], k == 0, k == KT - 1, [wsw, self.hT[c]], [psw])
                a, b2 = rt.next(), rt.next()
                self.tt(V, a.ap[:, 0:n], pn.ap[:, 0:n], cosT.ap[:, c0:c0 + n], ALU.mult, [pn, cosT], [a])
                self.tt(V, b2.ap[:, 0:n], psw.ap[:, 0:n], sinT.ap[:, c0:c0 + n], ALU.mult, [psw, sinT], [b2])
                if c == 2 and which == "q":
                    self.tt(V, qs32.ap[:, h, :], a.ap[:, 0:n], b2.ap[:, 0:n], ALU.add, [a, b2], [qs32])
                    self.cp("act", dst.ap[:, h, 0:n], qs32.ap[:, h, :], [qs32], [dst])
                else:
                    self.tt(V, dst.ap[:, h, 0:n], a.ap[:, 0:n], b2.ap[:, 0:n], ALU.add, [a, b2], [dst])
        if c < 2:
            for h in range(4):
                self.tt("pool", qdT.ap[:, h, :].rearrange("p (t i) -> p t i", t=4),
                        qT.ap[:, h, :].rearrange("p (t i) -> p t i", t=4),
                        qdec[:, h, :].unsqueeze(1).broadcast_to([128, 4, 128]), ALU.mult, [qT, self.consts], [qdT])
        else:
            self.tt("pool", qd32.ap, qs32.ap, qdecS, ALU.mult, [qs32, self.consts], [qd32])
        ntile = n // 128 if c < 2 else 1

        def pre(t, c=c):
            off = t * 128
            rows = 128 if c < 2 else NSMP
            vb, gb = self.pbank(), self.pbank()
            for k in range(KT):
                self.mm(vb.ap[0:rows, :], self.hT[c].ap[:, k, off:off + rows], Wv.ap[:, k, :], k == 0, k == KT - 1,
                        [self.hT[c], Wv], [vb])
            for k in range(KT):
                self.mm(gb.ap[0:rows, :], self.hT[c].ap[:, k, off:off + rows], Wg.ap[:, k, :], k == 0, k == KT - 1,
                        [self.hT[c], Wg], [gb])
            v_tok, sg = v_toks.next(), sgs.next()
            self.cp("act", v_tok.ap[0:rows, :], vb.ap[0:rows, :], [vb], [v_tok])
            self.act(sg.ap[0:rows, :], gb.ap[0:rows, :], AF.Silu, [gb], [sg])
            sb_ = self.pbank()
            for h in range(4):
                self.mm(sb_.ap[0:rows, h * 128:h * 128 + rows], kT.ap[:, h, off:off + rows], qT.ap[:, h, off:off + rows], True, True,
                        [kT, qT], [sb_])
            scm = scms.next()
            mk = maskT if c < 2 else maskS
            self.tt(V, scm.ap[0:rows, :, 0:rows], sb_.ap[0:rows, :].rearrange("p (h i) -> p h i", h=4)[:, :, 0:rows], mk,
                    ALU.mult, [sb_, self.consts], [scm])
            kb = self.pbank()
            kbf = kb.ap.bitcast(BF16)
            for h in range(4):
                self.tr(kbf[0:rows, h * 128:(h + 1) * 128], kT.ap[:, h, off:off + rows], identb.ap, [kT, identb], [kb])
            kd = kds.next()
            kdc = (kdec if c < 2 else kdecS).unsqueeze(2).broadcast_to([rows, 4, 128])
            self.tt(V, kd.ap[0:rows], kbf[0:rows, 0:512].rearrange("p (h d) -> p h d", h=4), kdc, ALU.mult,
                    [kb, self.consts], [kd])
            return v_tok, sg, scm, kd

        cur = pre(0)
        for t in range(ntile):
            off = t * 128
            rows = 128 if c < 2 else NSMP
            v_tok, sg, scm, kd = cur
            if c < 2:
                pS = pSs.next()
                self.tt("pool", pS.ap, self.Sret.ap, g128.unsqueeze(2).broadcast_to([128, 4, 128]), ALU.mult,
                        [self.Sret, self.consts], [pS])
                ob = self.pbank()
                for h in range(4):
                    self.mm(ob.ap[:, h * 128:(h + 1) * 128], scm.ap[:, h, :], v_tok.ap[:, h * 128:(h + 1) * 128], True, False,
                            [scm, v_tok], [ob])
                    self.mm(ob.ap[:, h * 128:(h + 1) * 128], qdT.ap[:, h, off:off + 128], self.Sretb.ap[:, h, :], False, True,
                            [qdT, self.Sretb], [ob])
                kvb = self.pbank()
                for h in range(4):
                    self.mm(kvb.ap[:, h * 128:(h + 1) * 128], kd.ap[:, h, :], v_tok.ap[:, h * 128:(h + 1) * 128], True, True,
                            [kd, v_tok], [kvb])
                self.tt(V, self.Sretb.ap.rearrange("p a b -> p (a b)"), pS.ap.rearrange("p a b -> p (a b)"), kvb.ap, ALU.add,
                        [pS, kvb], [self.Sretb])
                self.tt(V, self.Sret.ap.rearrange("p a b -> p (a b)"), pS.ap.rearrange("p a b -> p (a b)"), kvb.ap, ALU.add,
                        [pS, kvb], [self.Sret])
                if t + 1 < ntile:
                    cur = pre(t + 1)
                epilogue(ob, sg, 128, c, off)
            else:
                otb = [self.ps[h] for h in range(4)]
                locb = Rot([self.ps[4], self.ps[5], self.ps[6], self.ps[7]])
                for h in range(4):
                    self.mm(otb[h].ap[:, 0:NSMP], v_tok.ap[0:NSMP, h * 128:(h + 1) * 128], scm.ap[0:NSMP, h, 0:NSMP], True, False,
                            [v_tok, scm], [otb[h]])
                s0rot = Rot([ar.alloc("s0r%d" % i, [4, 128]) for i in range(3)])
                sorot = Rot([ar.alloc("sor%d" % i, [4, 128]) for i in range(2)])
                vms = Rot([ar.alloc("vm%d" % i, [512], BF16) for i in range(2)])
                smo = CL["seqmask"][0]
                nxtS0 = s0rot.next()
                self.dma(nxtS0.ap, self.ret0[0].rearrange("h p d -> p h d"), writes=[nxtS0])
                for s in range(NSS):
                    S0s = nxtS0
                    if s + 1 < NSS:
                        nxtS0 = s0rot.next()
                        self.dma(nxtS0.ap, self.ret0[s + 1].rearrange("h p d -> p h d"), writes=[nxtS0])
                    for h in range(4):
                        self.mm(otb[h].ap[:, 4 * s:4 * s + 4], S0s.ap[:, h, :], qd32.ap[:, h, 4 * s:4 * s + 4], False, s == NSS - 1,
                                [S0s, qd32], [otb[h]])
                    vm = vms.next()
                    self.ts(V, vm.ap[0:NSMP, :], v_tok.ap[0:NSMP, :], self.consts.ap[0:NSMP, smo + s:smo + s + 1], None,
                            ALU.mult, None, [v_tok, self.consts], [vm])
                    kvb = locb.next()
                    for h in range(4):
                        self.mm(kvb.ap[:, h * 128:(h + 1) * 128], kd.ap[0:NSMP, h, :], vm.ap[0:NSMP, h * 128:(h + 1) * 128], True, True,
                                [kd, vm], [kvb])
                    So = sorot.next()
                    self.tt("pool", So.ap, S0s.ap, self.C("g4").unsqueeze(2).broadcast_to([128, 4, 128]), ALU.mult,
                            [S0s, self.consts], [So])
                    sov = So.ap.rearrange("p a b -> p (a b)")
                    self.tt(V, sov, sov, kvb.ap, ALU.add, [So, kvb], [So])
                    self.dma(self.ret_s[s].rearrange("h p d -> p h d"), So.ap, reads=[So])
                oT32 = ar.alloc("oT32", [4, NSMP])
                for h in range(4):
                    self.cp("act", oT32.ap[:, h, :], otb[h].ap[:, 0:NSMP], [otb[h]], [oT32])
                ob = locb.next()
                self._pb = 0
                for h in range(4):
                    self.tr(ob.ap[0:NSMP, h * 128:(h + 1) * 128], oT32.ap[:, h, :], ident, [oT32, self.consts], [ob])
                epilogue(ob, sg, NSMP, c, 0)
    if seg == NSEG - 1:
        self.dma(self.ret_p.rearrange("h p d -> p h d"), self.Sret.ap, reads=[self.Sret])


Prog.ret_part = _ret_part


def _mixer_c(self, seg):
    nc = self.nc
    ar = self.arena
    m0 = ar.mark()
    chunks = self.chunks(seg)
    identb = self.identb
    V = "dve"
    ws = [512, 512, NSMP]
    yT = [ar.alloc("yTc%d" % c, [KT, ws[c]], BF16) for c in range(3)]
    m1 = ar.mark()
    self.hT = self.alloc_hT()
    self.rmsnorm(seg, 2, self.hT)
    self.sch.barrier()
    sqt = self.sq.tiles[0]
    rst_ = self.rs.tiles[0]
    xtra = sqt.ap.bitcast(F32).rearrange("p a b -> p (a b)")
    lgo = CL["hg_lg"][0]
    dlg = ar.alloc("dlg", [8])
    oml = ar.alloc("oml", [8])
    self.tt(V, dlg.ap, self.consts.ap[:, lgo:lgo + 8], self.consts.ap[:, lgo + 8:lgo + 16], ALU.subtract, [self.consts], [dlg])
    self.act(oml.ap, dlg.ap, AF.Sigmoid, [dlg], [oml])
    lnoml = ar.alloc("lnoml", [8])
    self.act(lnoml.ap, oml.ap, AF.Ln, [oml], [lnoml])
    one_ap = self.C("one")
    eps_ap = self.C("eps")
    nw_ap = self.C("hg_nw")
    rst = ar.alloc("rst", [512])
    rstS = ar.alloc("rstS", [NSMP])
    self.op(V, partial(nc.vector.memset, rst.ap, 1.0), [], [rst])
    self.op(V, partial(nc.vector.memset, rst.ap.rearrange("p (a b) -> p a b", b=64)[:, :, 0:1], 0.0), [], [rst])
    self.op(V, partial(nc.vector.memset, rstS.ap, 1.0), [], [rstS])
    self.op(V, partial(nc.vector.memset, rstS.ap.rearrange("p (a b) -> p a b", b=4)[:, :, 0:1], 0.0), [], [rstS])
    wts = Rot([ar.alloc("wc%d" % i, [KT, 128], BF16) for i in range(8)])
    qtT = ar.alloc("qtT", [8, 512], BF16)
    ktT = ar.alloc("ktT", [8, 512], BF16)
    kkT = ar.alloc("kkT", [8, 512], BF16)
    vT = ar.alloc("vT", [8, 512], BF16)
    sgT = ar.alloc("sgT", [8, 512], BF16)
    ebls = Rot([ar.alloc("ebl%d" % i, [8, 16]) for i in range(2)])
    qs32 = ar.alloc("qs32c", [8, NSMP])
    blkA = ar.alloc("htaB", [2560])
    setA = [Tile("hta%d" % i, blkA.ap[:, i * 512:(i + 1) * 512]) for i in range(5)]
    setB = [Tile("htb%d" % i, xtra[:, i * 512:(i + 1) * 512]) for i in range(4)] + [Tile("htb4", rst_.ap)]
    tsets = [setA, setB]
    vtoks = Rot([ar.alloc("hvt%d" % i, [1024], BF16) for i in range(2)])
    kktoks = Rot([ar.alloc("hkt%d" % i, [1024], BF16) for i in range(2)])
    scms = Rot([ar.alloc("hsc%d" % i, [8, 64], BF16) for i in range(2)])
    pSs = Rot([ar.alloc("hpS%d" % i, [8, 128]) for i in range(1)])
    sqb = ar.alloc("hsq", [512], BF16)
    rstd = ar.alloc("hrstd", [512])
    otmp = ar.alloc("hotmp", [512])
    tri64 = self.C("triBD")[0:64, 0:64]
    triS = self.C("triS", 64)
    ident = self.C("ident")

    def proj(w, c, n):
        b = self.pbank()
        for k in range(KT):
            self.mm(b.ap[:, 0:n], w.ap[:, k, :], self.hT[c].ap[:, k, 0:n], k == 0, k == KT - 1, [w, self.hT[c]], [b])
        return b

    def load_head(h):
        wq, wf, wv, wg = wts.next(), wts.next(), wts.next(), wts.next()
        self.load_w(wf, self.w_in_c[:, 1024 + h * 128:1024 + (h + 1) * 128], KT, 128, cast_eng="dve")
        self.load_w(wv, self.w_in_c[:, 2048 + h * 128:2048 + (h + 1) * 128], KT, 128, cast_eng="dve")
        self.load_w(wg, self.w_in_c[:, 3072 + h * 128:3072 + (h + 1) * 128], KT, 128, cast_eng="dve")
        self.load_w(wq, self.w_in_c[:, h * 128:(h + 1) * 128], KT, 128, cast_eng="dve")
        return wq, wf, wv, wg

    def epi_rest(o_ap, o_tiles, c, o64):
        sb_ = self.pbank()
        self.mm(sb_.ap, self.onesb.ap, sqb.ap, True, True, [self.onesb, sqb], [sb_])
        self.act(rstd.ap, sb_.ap, AF.Ln, [sb_, self.consts], [rstd], bias=eps_ap, scale=1.0 / 128)
        self.act(rstd.ap, rstd.ap, AF.Exp, [rstd], [rstd], scale=-0.5)
        self.tt(V, otmp.ap, o_ap, rstd.ap, ALU.mult, o_tiles + [rstd], [otmp])
        self.tt("pool", yT[c].ap[:, :, o64:o64 + 64], otmp.ap.rearrange("p (h i) -> p h i", h=8), sgT.ap[:, :, o64:o64 + 64],
                ALU.mult, [otmp, sgT], [yT[c]])

    nxt_w = load_head(0)
    work = [(c, c0, n, h) for (c, c0, n) in chunks for h in range(8)]
    for wi, (c, c0, n, h) in enumerate(work):
        sample = (c == 2)
        blk = 64 if not sample else 4
        nb = n // blk
        if h == 0:
            ebl = ebls.next()
        wq, wf, wv, wg = nxt_w
        if wi + 1 < len(work):
            nxt_w = load_head(work[wi + 1][3])
        kf, lf, bT, enb, kt = tsets[wi % 2]
        pf = proj(wf, c, n)
        self.act(kf.ap[:, 0:n], pf.ap[:, 0:n], AF.Exp, [pf], [kf])
        pv = proj(wv, c, n)
        pg = proj(wg, c, n)
        self.act(kt.ap[:, 0:n], pg.ap[:, 0:n], AF.Exp, [pg], [kt], scale=-1.0)
        self.act(lf.ap[:, 0:n], kf.ap[:, 0:n], AF.Ln, [kf, self.consts], [lf], bias=one_ap)
        self.act(kt.ap[:, 0:n], kt.ap[:, 0:n], AF.Ln, [kt, self.consts], [kt], bias=one_ap)
        self.act(kf.ap[:, 0:n], lf.ap[:, 0:n], AF.Exp, [lf, lnoml], [kf], bias=lnoml.ap[:, h:h + 1], scale=-1.0)
        self.act(kt.ap[:, 0:n], kt.ap[:, 0:n], AF.Exp, [kt], [kt], scale=-1.0)
        self.cp("act", vT.ap[:, h, 0:n], pv.ap[:, 0:n], [pv], [vT])
        self.act(lf.ap[:, 0:n], kf.ap[:, 0:n], AF.Ln, [kf, self.consts], [lf], bias=one_ap, scale=-1.0)
        self.stt(sgT.ap[:, h, 0:n], pg.ap[:, 0:n], nw_ap, kt.ap[:, 0:n], ALU.mult, ALU.mult, [pg, self.consts, kt], [sgT])
        rs_ap = rst.ap[:, 0:n] if not sample else rstS.ap[:, 0:n]
        self.op(V, partial(nc.vector.tensor_tensor_scan, out=bT.ap[:, 0:n], data0=rs_ap, data1=lf.ap[:, 0:n], initial=0.0,
                           op0=ALU.mult, op1=ALU.add), [rst, rstS, lf], [bT])
        pq = proj(wq, c, n)
        self.act(lf.ap[:, 0:n], bT.ap[:, 0:n], AF.Exp, [bT], [lf])
        self.act(enb.ap[:, 0:n], bT.ap[:, 0:n], AF.Exp, [bT], [enb], scale=-1.0)
        eb = lf
        ebv = eb.ap[:, 0:n].rearrange("p (a b) -> p a b", b=blk)[:, :, blk - 1]
        self.cp("pool", ebl.ap[:, h, 0:nb], ebv, [eb], [ebl])
        if sample:
            self.tt(V, qs32.ap[:, h, :], pq.ap[:, 0:n], eb.ap[:, 0:n], ALU.mult, [pq, eb], [qs32])
            self.cp("act", qtT.ap[:, h, 0:n], qs32.ap[:, h, :], [qs32], [qtT])
        else:
            self.tt(V, qtT.ap[:, h, 0:n], pq.ap[:, 0:n], eb.ap[:, 0:n], ALU.mult, [pq, eb], [qtT])
        self.tt(V, kt.ap[:, 0:n], kf.ap[:, 0:n], enb.ap[:, 0:n], ALU.mult, [kf, enb], [kt])
        self.cp("act", ktT.ap[:, h, 0:n], kt.ap[:, 0:n], [kt], [ktT])
        self.tt(V, kkT.ap[:, h, 0:n].rearrange("p (a b) -> p a b", b=blk), kt.ap[:, 0:n].rearrange("p (a b) -> p a b", b=blk),
                ebl.ap[:, h, 0:nb].unsqueeze(2).broadcast_to([128, nb, blk]), ALU.mult, [kt, ebl], [kkT])
        if h < 7:
            continue
        nt = n // 64 if not sample else 1

        def pre(t):
            o64 = t * 64
            vb, kb = self.pbank(), self.pbank()
            vbf, kbf = vb.ap.bitcast(BF16), kb.ap.bitcast(BF16)
            for hh in range(8):
                self.tr(vbf[0:64, hh * 128:(hh + 1) * 128], vT.ap[:, hh, o64:o64 + 64], identb.ap, [vT, identb], [vb])
            for hh in range(8):
                self.tr(kbf[0:64, hh * 128:(hh + 1) * 128], kkT.ap[:, hh, o64:o64 + 64], identb.ap, [kkT, identb], [kb])
            v_tok, kk_tok = vtoks.next(), kktoks.next()
            self.cp("act", v_tok.ap[0:64, :], vbf[0:64, :], [vb], [v_tok])
            self.cp(V, kk_tok.ap[0:64, :], kbf[0:64, :], [kb], [kk_tok])
            sb_ = self.pbank()
            for hh in range(8):
                self.mm(sb_.ap[0:64, hh * 64:(hh + 1) * 64], ktT.ap[:, hh, o64:o64 + 64], qtT.ap[:, hh, o64:o64 + 64], True, True,
                        [ktT, qtT], [sb_])
            scm = scms.next()
            mk = (triS if sample else tri64).unsqueeze(1).broadcast_to([64, 8, 64])
            self.tt(V, scm.ap[0:64], sb_.ap[0:64, :].rearrange("p (h i) -> p h i", h=8), mk, ALU.mult, [sb_, self.consts], [scm])
            return v_tok, kk_tok, scm

        if not sample:
            cur = pre(0)
            for t in range(nt):
                o64 = t * 64
                v_tok, kk_tok, scm = cur
                pS = pSs.next()
                self.tt("pool", pS.ap, self.Shg.ap, ebl.ap[:, :, t:t + 1].broadcast_to([128, 8, 128]), ALU.mult,
                        [self.Shg, ebl], [pS])
                ob = self.pbank()
                for hh in range(8):
                    self.mm(ob.ap[:, hh * 64:(hh + 1) * 64], v_tok.ap[0:64, hh * 128:(hh + 1) * 128], scm.ap[0:64, hh, :], True, False,
                            [v_tok, scm], [ob])
                    self.mm(ob.ap[:, hh * 64:(hh + 1) * 64], self.Shgb.ap[:, hh, :], qtT.ap[:, hh, o64:o64 + 64], False, True,
                            [self.Shgb, qtT], [ob])
                self.act(sqb.ap, ob.ap, AF.Square, [ob], [sqb])
                for half in range(2):
                    kvb = self.pbank()
                    for q in range(4):
                        hh = half * 4 + q
                        self.mm(kvb.ap[:, q * 128:(q + 1) * 128], kk_tok.ap[0:64, hh * 128:(hh + 1) * 128],
                                v_tok.ap[0:64, hh * 128:(hh + 1) * 128], True, True, [kk_tok, v_tok], [kvb])
                    psv = pS.ap[:, half * 4:(half + 1) * 4, :].rearrange("p a b -> p (a b)")
                    self.tt(V, self.Shgb.ap[:, half * 4:(half + 1) * 4, :].rearrange("p a b -> p (a b)"), psv, kvb.ap, ALU.add,
                            [pS, kvb], [self.Shgb])
                    self.tt(V, self.Shg.ap[:, half * 4:(half + 1) * 4, :].rearrange("p a b -> p (a b)"), psv, kvb.ap, ALU.add,
                            [pS, kvb], [self.Shg])
                if t + 1 < nt:
                    cur = pre(t + 1)
                epi_rest(ob.ap, [ob], c, o64)
        else:
            self.sch.barrier()
            big = Tile("hbig", None)
            v_tok, kk_tok, scm = pre(0)
            ob = self.ps[0]
            ib = self.ps[1]
            locb = Rot([self.ps[2], self.ps[3], self.ps[4], self.ps[5], self.ps[6], self.ps[7]])
            for hh in range(8):
                self.mm(ob.ap[:, hh * 64:(hh + 1) * 64], v_tok.ap[0:64, hh * 128:(hh + 1) * 128], scm.ap[0:64, hh, :], True, True,
                        [v_tok, scm], [ob])
            hfree = self.hT[0].ap.bitcast(F32).rearrange("p a b -> p (a b)")
            s0rot = Rot([Tile("hs0r0", blkA.ap[:, 1024:2048].rearrange("p (a b) -> p a b", a=8)),
                         Tile("hs0r1", hfree[:, 0:1024].rearrange("p (a b) -> p a b", a=8))])
            sorot = Rot([Tile("hsor0", xtra[:, 0:1024].rearrange("p (a b) -> p a b", a=8)),
                         Tile("hsor1", hfree[:, 1024:2048].rearrange("p (a b) -> p a b", a=8))])
            vms = Rot([Tile("hvm0", xtra[:, 1024:1536].bitcast(BF16))])
            smo = CL["seqmask"][0]
            nxtS0 = s0rot.next()
            self.dma(nxtS0.ap, self.hg0[0].rearrange("h p d -> p h d"), writes=[nxtS0])
            for s_ in range(NSS):
                S0s = nxtS0
                if s_ + 1 < NSS:
                    nxtS0 = s0rot.next()
                    self.dma(nxtS0.ap, self.hg0[s_ + 1].rearrange("h p d -> p h d"), writes=[nxtS0])
                for hh in range(8):
                    self.mm(ib.ap[:, hh * 64 + 4 * s_:hh * 64 + 4 * s_ + 4], S0s.ap[:, hh, :], qs32.ap[:, hh, 4 * s_:4 * s_ + 4], True, True,
                            [S0s, qs32], [ib])
                vm = vms.next()
                self.ts(V, vm.ap[0:64, :], v_tok.ap[0:64, :], self.consts.ap[0:64, smo + s_:smo + s_ + 1], None, ALU.mult, None,
                        [v_tok, self.consts], [vm])
                So = sorot.next()
                self.tt("pool", So.ap, S0s.ap, ebl.ap[:, :, s_:s_ + 1].broadcast_to([128, 8, 128]), ALU.mult, [S0s, ebl], [So])
                for half in range(2):
                    kvb = locb.next()
                    for q in range(4):
                        hh = half * 4 + q
                        self.mm(kvb.ap[:, q * 128:(q + 1) * 128], kk_tok.ap[0:64, hh * 128:(hh + 1) * 128],
                                vm.ap[0:64, hh * 128:(hh + 1) * 128], True, True, [kk_tok, vm], [kvb])
                    sov = So.ap[:, half * 4:(half + 1) * 4, :].rearrange("p a b -> p (a b)")
                    self.tt(V, sov, sov, kvb.ap, ALU.add, [So, kvb], [So])
                self.dma(self.hg_s[s_].rearrange("h p d -> p h d"), So.ap, reads=[So])
            oi = setA[0]
            self.cp("act", oi.ap, ib.ap, [ib], [oi])
            osum = setA[1]
            self.tt(V, osum.ap, ob.ap, oi.ap, ALU.add, [ob, oi], [osum])
            self._pb = 0
            self.act(sqb.ap, osum.ap, AF.Square, [osum], [sqb])
            epi_rest(osum.ap, [osum], c, 0)
    if seg == NSEG - 1:
        self.dma(self.hg_p.rearrange("h p d -> p h d"), self.Shg.ap, reads=[self.Shg])
    self.sch.barrier()
    ar.release(m1)
    wout = ar.alloc("woutc", [KT, D], BF16)
    self.load_w(wout, self.w_out_c, KT, D)
    for (c, c0, n) in chunks:
        for mo in range(KT):
            b = self.pbank()
            for k in range(KT):
                self.mm(b.ap[:, 0:n], wout.ap[:, k, mo * 128:(mo + 1) * 128], yT[c].ap[:, k, 0:n], k == 0, k == KT - 1,
                        [wout, yT[c]], [b])
            xv = self.xcols(c, mo, mo + 1)[:, 0, :]
            self.tt("dve", xv, xv, b.ap[:, 0:n], ALU.add, [self.xT[c], b], [self.xT[c]])
    self.sch.barrier()
    ar.release(m0)


Prog.mixer_c = _mixer_c
```
